# Optimizing a Trainium2 kernel written in Bass

```python
import math
import jax, jax.numpy as jnp
from jax import lax
import numpy as np

D_MODEL = 1024
BATCH = 4
SEQ = 8192
DEPTH = 4

N_MIXERS = 3
D_FF = 2816
N_SUB = 3
EPS = 1e-6
A_HEADS = 8
A_DK = D_MODEL // A_HEADS
A_DV = D_MODEL // A_HEADS
A_CHUNK = 64
B_HEADS = 8
B_DH = D_MODEL // (2 * B_HEADS)
B_ROT = B_DH // 4
ROPE_THETA = 500000.0
Q_BLOCK = 128
C_WIDTH = 3
N_A = (DEPTH + 2) // N_MIXERS
N_B = (DEPTH + 1) // N_MIXERS
N_C = DEPTH // N_MIXERS

kernel_name = "hybrid_hgrn2_diffattn_shortconv_macaron"


def rmsnorm(x, g):
    xf = x.astype(jnp.float32)
    y = xf * lax.rsqrt(jnp.mean(xf * xf, axis=-1, keepdims=True) + EPS)
    return (y * g.astype(jnp.float32)).astype(x.dtype)


def swiglu(h, wi, wo):
    gt, up = jnp.split(h @ wi, 2, axis=-1)
    return (jax.nn.silu(gt) * up) @ wo


def partial_rope(x, cos, sin):
    xr, xp = x[..., :B_ROT], x[..., B_ROT:]
    x1 = xr[..., :B_ROT // 2].astype(jnp.float32)
    x2 = xr[..., B_ROT // 2:].astype(jnp.float32)
    rot = jnp.concatenate([x1 * cos - x2 * sin, x2 * cos + x1 * sin], axis=-1)
    return jnp.concatenate([rot.astype(x.dtype), xp], axis=-1)


def hgrn2_mixer(h, w_in, w_out, lb, onorm_g):
    Bn, S, _ = h.shape
    nc = S // A_CHUNK
    q, fz, i, g = jnp.split(h @ w_in, 4, axis=-1)
    q = jax.nn.silu(q).astype(jnp.float32)
    f = lb.astype(jnp.float32) + (1.0 - lb.astype(jnp.float32)) * jax.nn.sigmoid(fz.astype(jnp.float32))
    logf = jnp.log(f)
    k = 1.0 - f

    def heads(t, dh):
        t = t.astype(jnp.float32).reshape(Bn, nc, A_CHUNK, A_HEADS, dh)
        return t.transpose(1, 0, 3, 2, 4)

    causal = jnp.tril(jnp.ones((A_CHUNK, A_CHUNK), dtype=bool))

    def step(state, inp):
        q_c, k_c, v_c, lf_c = inp
        b = jnp.cumsum(lf_c, axis=2)
        inter = jnp.einsum('bhtk,bhkv->bhtv', q_c * jnp.exp(b), state)
        diff = b[:, :, :, None, :] - b[:, :, None, :, :]
        decay = jnp.exp(jnp.where(causal[:, :, None], diff, -jnp.inf))
        att = jnp.einsum('bhtk,bhsk,bhtsk->bhts', q_c, k_c, decay)
        intra = jnp.einsum('bhts,bhsv->bhtv', att, v_c)
        b_last = b[:, :, -1]
        new_state = jnp.exp(b_last)[..., None] * state + jnp.einsum(
            'bhsk,bhsv->bhkv', k_c * jnp.exp(b_last[:, :, None] - b), v_c)
        return new_state, inter + intra

    s0 = jnp.zeros((Bn, A_HEADS, A_DK, A_DV), jnp.float32)
    _, o = lax.scan(step, s0, (heads(q, A_DK), heads(k, A_DK), heads(i, A_DV), heads(logf, A_DK)))
    o = o.transpose(1, 0, 3, 2, 4).reshape(Bn, S, A_HEADS, A_DV)
    o = rmsnorm(o, onorm_g) * jax.nn.silu(g.astype(jnp.float32).reshape(Bn, S, A_HEADS, A_DV))
    return o.reshape(Bn, S, D_MODEL).astype(h.dtype) @ w_out


def diff_attention_mixer(h, w_in, w_out, qk_g, lam_p, subln_g, cos, sin, lambda_init):
    Bn, S, _ = h.shape
    q, k, v = jnp.split(h @ w_in, 3, axis=-1)
    q = q.reshape(Bn, S, B_HEADS, 2, B_DH)
    k = k.reshape(Bn, S, B_HEADS, 2, B_DH)
    v = v.reshape(Bn, S, B_HEADS, 2 * B_DH).transpose(0, 2, 1, 3)
    q = partial_rope(rmsnorm(q, qk_g[0]), cos, sin).transpose(0, 2, 3, 1, 4)
    k = partial_rope(rmsnorm(k, qk_g[1]), cos, sin).transpose(0, 2, 3, 1, 4)
    lp = lam_p.astype(jnp.float32)
    lam = jnp.exp(jnp.sum(lp[0] * lp[1])) - jnp.exp(jnp.sum(lp[2] * lp[3])) + lambda_init
    scale = B_DH ** -0.5
    outs = []
    for blk in range(S // Q_BLOCK):
        s0 = blk * Q_BLOCK
        s1 = s0 + Q_BLOCK
        sc = jnp.einsum('bhcqd,bhckd->bhcqk', q[:, :, :, s0:s1], k[:, :, :, :s1]).astype(jnp.float32) * scale
        mask = (s0 + jnp.arange(Q_BLOCK))[:, None] >= jnp.arange(s1)[None, :]
        p = jax.nn.softmax(jnp.where(mask, sc, -jnp.inf), axis=-1)
        a = p[:, :, 0] - lam * p[:, :, 1]
        outs.append(jnp.einsum('bhqk,bhkv->bhqv', a.astype(v.dtype), v[:, :, :s1]))
    o = jnp.concatenate(outs, axis=2)
    o = rmsnorm(o, subln_g) * (1.0 - lambda_init)
    return o.transpose(0, 2, 1, 3).reshape(Bn, S, D_MODEL) @ w_out


def short_conv_mixer(h, w_in, conv_w, w_out):
    bg, cg, u = jnp.split(h @ w_in, 3, axis=-1)
    u = cg * u
    up = jnp.pad(u, ((0, 0), (C_WIDTH - 1, 0), (0, 0)))
    y = conv_w[0] * up[:, :-2] + conv_w[1] * up[:, 1:-1] + conv_w[2] * up[:, 2:]
    return (bg * y) @ w_out


def setup_inputs(seed: int = 0) -> dict:
    key = jax.random.key(seed)
    ks = jax.random.split(key, 24)
    D, F = D_MODEL, D_FF
    sD, sF = D ** -0.5, F ** -0.5
    n = jax.random.normal
    return {
        "x": n(ks[0], (BATCH, SEQ, D), jnp.float32),
        "c": n(ks[1], (BATCH, D), jnp.float32),
        "positions": (jnp.arange(SEQ, dtype=jnp.int32)[None, :]
                      + jax.random.randint(ks[2], (BATCH, 1), 0, 4096, dtype=jnp.int32)),
        "ada_w": n(ks[3], (DEPTH, D, 3 * N_SUB * D), jnp.float32) * (0.2 * sD),
        "ada_b": n(ks[4], (DEPTH, 3 * N_SUB * D), jnp.float32) * 0.02,
        "norm_g": 1.0 + 0.02 * n(ks[5], (DEPTH, N_SUB, D), jnp.float32),
        "ffn_wi": n(ks[6], (DEPTH, 2, D, 2 * F), jnp.float32) * sD,
        "ffn_wo": n(ks[7], (DEPTH, 2, F, D), jnp.float32) * sF,
        "a_w_in": n(ks[8], (N_A, D, 4 * D), jnp.float32) * sD,
        "a_w_out": n(ks[9], (N_A, D, D), jnp.float32) * sD,
        "a_lb": 0.5 * n(ks[10], (N_A, D), jnp.float32),
        "a_onorm": 1.0 + 0.02 * n(ks[11], (N_A, A_DV), jnp.float32),
        "b_w_in": n(ks[12], (N_B, D, 3 * D), jnp.float32) * sD,
        "b_w_out": n(ks[13], (N_B, D, D), jnp.float32) * sD,
        "b_qk_g": 1.0 + 0.02 * n(ks[14], (N_B, 2, B_DH), jnp.float32),
        "b_lam": 0.1 * n(ks[15], (N_B, 4, B_DH), jnp.float32),
        "b_subln": 1.0 + 0.02 * n(ks[16], (N_B, 2 * B_DH), jnp.float32),
        "c_w_in": n(ks[17], (N_C, D, 3 * D), jnp.float32) * sD,
        "c_conv": n(ks[18], (N_C, C_WIDTH, D), jnp.float32) * (C_WIDTH ** -0.5),
        "c_w_out": n(ks[19], (N_C, D, D), jnp.float32) * sD,
    }


def reference(x, c, positions, ada_w, ada_b, norm_g, ffn_wi, ffn_wo,
              a_w_in, a_w_out, a_lb, a_onorm,
              b_w_in, b_w_out, b_qk_g, b_lam, b_subln,
              c_w_in, c_conv, c_w_out):
    Bn, S, D = x.shape
    inv_freq = ROPE_THETA ** (-jnp.arange(0, B_ROT, 2, dtype=jnp.float32) / B_ROT)
    ang = positions.astype(jnp.float32)[..., None] * inv_freq
    cos = jnp.cos(ang)[:, :, None, None, :]
    sin = jnp.sin(ang)[:, :, None, None, :]
    lb_sm = jax.nn.softmax(a_lb.astype(jnp.float32), axis=0)
    lb_all = jnp.cumsum(lb_sm, axis=0) - lb_sm[0]
    c_act = jax.nn.silu(c)

    for l in range(DEPTH):
        mod = (c_act @ ada_w[l] + ada_b[l]).reshape(Bn, N_SUB, 3, 1, D)

        def pre(h, j):
            return rmsnorm(h, norm_g[l, j]) * (1.0 + mod[:, j, 1]) + mod[:, j, 0]

        x = x + 0.5 * (1.0 + mod[:, 0, 2]) * swiglu(pre(x, 0), ffn_wi[l, 0], ffn_wo[l, 0])
        h = pre(x, 1)
        kind, idx = l % N_MIXERS, l // N_MIXERS
        if kind == 0:
            y = hgrn2_mixer(h, a_w_in[idx], a_w_out[idx], lb_all[idx], a_onorm[idx])
        elif kind == 1:
            lambda_init = 0.8 - 0.6 * math.exp(-0.3 * l)
            y = diff_attention_mixer(h, b_w_in[idx], b_w_out[idx], b_qk_g[idx], b_lam[idx],
                                     b_subln[idx], cos, sin, lambda_init)
        else:
            y = short_conv_mixer(h, c_w_in[idx], c_conv[idx], c_w_out[idx])
        x = x + (1.0 + mod[:, 1, 2]) * y
        x = x + 0.5 * (1.0 + mod[:, 2, 2]) * swiglu(pre(x, 2), ffn_wi[l, 1], ffn_wo[l, 1])
    return x
```

```python
import contextlib
import math
import numpy as np
import ml_dtypes
import concourse.bass as bass
import concourse.mybir as mybir
from concourse.bass_utils import run_bass_kernel_spmd

F32 = mybir.dt.float32
BF16 = mybir.dt.bfloat16
I32 = mybir.dt.int32
AF = mybir.ActivationFunctionType
ALU = mybir.AluOpType

D = 1024
FF = 2816
NF = 22
DEPTH = 4
EPS = 1e-6
TWO_PI = 2.0 * math.pi


class Res:
    __slots__ = ('w', 'r', 'name')

    def __init__(self, name=''):
        self.w = None
        self.r = {}
        self.name = name


class KB:
    NDMA = {'sp': 12, 'pool': 12}

    def __init__(self, nc):
        self.nc = nc
        self.es = contextlib.ExitStack()
        self.engs = {'pe': nc.tensor, 'act': nc.scalar, 'dve': nc.vector, 'pool': nc.gpsimd, 'sp': nc.sync}
        self.sems = {}
        self.cnt = {}
        self.cur = {}
        self.gen = 0
        self.retired = set()
        self._fresh()
        self.dsem = {}
        self.drr = {}
        for q, n in self.NDMA.items():
            self.dsem[q] = []
            self.drr[q] = 0
            for i in range(n):
                nm = 'd_%s%d' % (q, i)
                self.sems[nm] = self.es.enter_context(nc.semaphore(nm))
                self.cnt[nm] = 0
                self.dsem[q].append(nm)
        self.waited = {e: {} for e in self.engs}
        self.n_instr = 0
        self.n_wait = 0
        self.uid = 0

    def _fresh(self):
        for e in ['pe', 'act', 'dve', 'pool']:
            if e in self.cur:
                self.retired.add(self.cur[e])
            nm = '%s@%d' % (e, self.gen)
            self.sems[nm] = self.es.enter_context(self.nc.semaphore('s_%s_%d' % (e, self.gen)))
            self.cnt[nm] = 0
            self.cur[e] = nm
        self.gen += 1

    def sb(self, name, shape, dt):
        return self.es.enter_context(self.nc.sbuf_tensor(name, list(shape), dt))

    def ps(self, name, shape, dt=F32):
        return self.es.enter_context(self.nc.psum_tensor(name, list(shape), dt))

    def _wait(self, eng, dep):
        s, v = dep
        if s in self.retired:
            return
        if eng == 'pe' and s == self.cur['pe']:
            return
        if self.waited[eng].get(s, 0) >= v:
            return
        self.engs[eng].wait_ge(self.sems[s], v)
        self.waited[eng][s] = v
        self.n_wait += 1

    def _deps(self, eng, reads, writes):
        deps = {}
        for r in reads:
            if r.w is not None:
                s, v = r.w
                deps[s] = max(deps.get(s, 0), v)
        for w in writes:
            if w.w is not None:
                s, v = w.w
                deps[s] = max(deps.get(s, 0), v)
            for s, v in w.r.items():
                deps[s] = max(deps.get(s, 0), v)
        for s, v in deps.items():
            self._wait(eng, (s, v))

    def _mark(self, tick, reads, writes):
        s, v = tick
        for r in reads:
            r.r[s] = max(r.r.get(s, 0), v)
        for w in writes:
            w.w = tick
            w.r = {}

    def op(self, eng, fn, reads=(), writes=()):
        self._deps(eng, reads, writes)
        ins = fn(self.engs[eng])
        nm = self.cur[eng]
        ins.then_inc(self.sems[nm], 1)
        self.cnt[nm] += 1
        self.n_instr += 1
        self._mark((nm, self.cnt[nm]), reads, writes)

    def mm(self, out, pairs, reads=(), writes=(), start=True, stop=True):
        self._deps('pe', reads, writes)
        n = len(pairs)
        ins = None
        for i, (l, r) in enumerate(pairs):
            ins = self.nc.tensor.matmul(out, l, r, start=(start and i == 0), stop=(stop and i == n - 1))
            self.n_instr += 1
        nm = self.cur['pe']
        ins.then_inc(self.sems[nm], 1)
        self.cnt[nm] += 1
        self._mark((nm, self.cnt[nm]), reads, writes)

    def dma(self, q, out, in_, reads=(), writes=(), **kw):
        sl = self.dsem[q]
        nm = sl[self.drr[q] % len(sl)]
        self.drr[q] += 1
        if self.cnt[nm] > 0:
            self._wait(q, (nm, self.cnt[nm]))
        self._deps(q, reads, writes)
        self.engs[q].dma_start(out=out, in_=in_, **kw).then_inc(self.sems[nm], 16)
        self.cnt[nm] += 16
        self.n_instr += 1
        self._mark((nm, self.cnt[nm]), reads, writes)

    def barrier(self, fresh=True):
        for eng in ['pe', 'act', 'dve', 'pool', 'sp']:
            for nm in self.sems:
                if self.cnt[nm] > 0 and nm not in self.retired:
                    if eng == 'pe' and nm == self.cur['pe']:
                        continue
                    self._wait(eng, (nm, self.cnt[nm]))
        if fresh and self.gen < 16:
            self._fresh()

    def finish(self, eng='sp'):
        for nm in self.sems:
            if self.cnt[nm] > 0 and nm not in self.retired:
                self._wait(eng, (nm, self.cnt[nm]))

    def close(self):
        self.es.close()


class Rot:
    def __init__(self, items):
        self.items = items
        self.i = 0

    def next(self):
        it = self.items[self.i % len(self.items)]
        self.i += 1
        return it


class WStream:
    def __init__(self, k, bufs, kw=None):
        self.k = k
        self.bufs = bufs
        self.q = []
        self.issued = 0
        self.used = 0
        self.kw = kw or {}
        self.srcs = []

    def add(self, src):
        self.srcs.append(src)

    def pump(self):
        while self.issued < len(self.srcs) and self.issued - self.used < len(self.bufs):
            t, r = self.bufs[self.issued % len(self.bufs)]
            src = self.srcs[self.issued]
            self.k.dma('pool', t[:], src, writes=[r], **self.kw)
            self.issued += 1

    def get(self):
        self.pump()
        assert self.used < self.issued
        it = self.bufs[self.used % len(self.bufs)]
        self.used += 1
        return it


MAGIC = 12582912.0
C1 = 6.28125
C2 = TWO_PI - 6.28125
PI_LO = 3.1415925


def build_program(S, stages):
    nc = bass.Bass("TRN2", target_bir_lowering=False)
    k = KB(nc)
    TT = min(1024, S)
    NSUB = TT // 512
    NTILE = S // TT
    NB512 = S // 512

    def din(name, shape, dt=F32):
        return nc.dram_tensor(name, list(shape), dt, kind="ExternalInput").ap()

    xT_in = din("xT", [8, 128, S])
    c_in = din("c_l", [128, 8])
    adaw = din("ada_w", [DEPTH, D, 9 * D])
    adab = din("ada_b_l", [128, DEPTH, 72])
    normg = din("norm_g_l", [128, DEPTH, 3, 8])
    ffw = din("ffw", [DEPTH, 2, NF, 128, 3072])
    woutm = din("wout_l", [DEPTH, 128, 8, D])
    a_win = din("a_win_l", [2, 4, 2, 128, 8, 512])
    a_lbf = din("a_lb_fm", [128, 2, 8])
    a_lbr = din("a_lb_row", [2, D])
    a_on = din("a_onorm_l", [128, 2])
    b_win = din("b_win_l", [3, 2, 128, 8, 512])
    b_qkg = din("b_qkg_l", [128, 2])
    b_lam = din("b_lam", [1, 256])
    b_sub = din("b_subln_l", [128, 1])
    c_win = din("c_win_l", [3, 2, 128, 8, 512])
    c_cv = din("c_conv_l", [128, 8, 3])
    pos_in = din("pos", [1, S], I32)
    cst = din("consts", [128, 1024])
    xT_out = nc.dram_tensor("xT_out", [8, 128, S], F32, kind="ExternalOutput").ap()
    xs = xT_out
    hs = nc.dram_tensor("hs", [8, 128, S], BF16).ap()
    os_ = nc.dram_tensor("os", [8, 128, S], BF16).ap()
    R_xs = [Res() for _ in range(NTILE)]
    R_hs = [Res() for _ in range(NB512)]
    R_os = [[Res() for _ in range(NB512)] for _ in range(2)]
    rp = nc.dram_tensor("rp", [128, NB512, 2, 512], F32).ap()
    R_rp = [Res() for _ in range(NB512)]

    cf = k.sb('cf', [128, 1024], F32); R_cf = Res()
    k.dma('sp', cf[:], cst[:, :], writes=[R_cf])
    cb = k.sb('cb', [128, 1024], BF16); R_cb = Res()
    k.op('dve', lambda e: e.tensor_copy(cb[:], cf[:]), reads=[R_cf], writes=[R_cb])
    ones_bf = cb[:, 512:640]
    bones_bf = cb[:, 640:768]
    modp = k.sb('modp', [128, DEPTH, 3, 3, 8], F32); R_mod = Res()
    epsb = k.sb('epsb', [128, 1], F32)
    k.op('dve', lambda e: e.memset(epsb[:], EPS), writes=[R_cf])

    PS = [k.ps('ps%d' % i, [128, 512]) for i in range(8)]
    RPS = [Res() for _ in range(8)]
    psA = Rot([(PS[i], RPS[i]) for i in range(4)])
    psB = Rot([(PS[i], RPS[i]) for i in range(4, 6)])
    psC = Rot([(PS[i], RPS[i]) for i in range(6, 8)])
    TMP = [(k.sb('tmp%d' % i, [128, 512], F32), Res()) for i in range(6)]
    tmp = Rot(TMP)
    rsp = Rot([(k.sb('rsp%d' % i, [128, 512], F32), Res()) for i in range(2)])

    def rstd_from_ss(ss_ps, R_ss, n, width=512):
        t, r = rsp.next()
        k.op('act', lambda e: e.activation(t[:, :width], ss_ps, AF.Ln, bias=epsb[:, 0:1], scale=1.0 / n), reads=[R_ss, R_cf], writes=[r])
        k.op('act', lambda e: e.activation(t[:, :width], t[:, :width], AF.Exp, scale=-0.5), reads=[r], writes=[r])
        return t, r

    def sig_to(dst, src_ap, R_src, R_dst, extra_reads=()):
        k.op('act', lambda e: e.activation(dst, src_ap, AF.Exp, scale=-1.0), reads=[R_src] + list(extra_reads), writes=[R_dst])
        k.op('act', lambda e: e.activation(dst, dst, AF.Identity, bias=cf[:, 512:513], scale=1.0), reads=[R_dst, R_cf], writes=[R_dst])
        k.op('dve', lambda e: e.reciprocal(dst, dst), reads=[R_dst], writes=[R_dst])

    def stage_alloc():
        es = contextlib.ExitStack()

        def sb(name, shape, dt):
            k.uid += 1
            return es.enter_context(nc.sbuf_tensor('%s_%d' % (name, k.uid), list(shape), dt))
        return es, sb

    def prologue():
        es, sb = stage_alloc()
        with es:
            ct = sb('ct', [128, 8], F32); R_ct = Res()
            k.dma('sp', ct[:], c_in[:, :], writes=[R_ct])
            cs_ = sb('cs_', [128, 8], F32)
            sig_to(cs_[:], ct[:], R_ct, R_ct)
            k.op('dve', lambda e: e.tensor_tensor(ct[:], ct[:], cs_[:], ALU.mult), reads=[R_ct], writes=[R_ct])
            ab = sb('ab', [128, DEPTH, 72], F32); R_ab = Res()
            k.dma('sp', ab[:], adab[:, :, :], writes=[R_ab])
            ng = sb('ng', [128, DEPTH, 3, 8], F32); R_ng = Res()
            k.dma('sp', ng[:], normg[:, :, :, :], writes=[R_ng])
            CW = 1152
            awb = Rot([(sb('awb%d' % i, [128, 8, CW], F32), Res()) for i in range(2)])
            ncc = CW // 128
            for l in range(DEPTH):
                for g in range(9 * D // CW):
                    t, r = awb.next()
                    src = adaw[l, :, g * CW:(g + 1) * CW].rearrange("(k p) c -> p k c", p=128)
                    k.dma('sp', t[:], src, writes=[r])
                    pt, pr = psC.next()
                    for cc in range(ncc):
                        k.mm(pt[:, cc:cc + 1], [(t[:, kc, cc * 128:(cc + 1) * 128], ct[:, kc:kc + 1]) for kc in range(8)],
                             reads=[r, R_ct], writes=[pr])
                    k.op('dve', lambda e: e.tensor_tensor(ab[:, l, g * ncc:(g + 1) * ncc], pt[:, 0:ncc], ab[:, l, g * ncc:(g + 1) * ncc], ALU.add),
                         reads=[pr, R_ab], writes=[R_ab])
            for l in range(DEPTH):
                for j in range(3):
                    base = j * 24
                    k.op('dve', lambda e: e.tensor_copy(modp[:, l, j, 0, :], ab[:, l, base:base + 8]), reads=[R_ab], writes=[R_mod])
                    k.op('dve', lambda e: e.scalar_tensor_tensor(modp[:, l, j, 1, :], ab[:, l, base + 8:base + 16], 1.0, ng[:, l, j, :], ALU.add, ALU.mult),
                         reads=[R_ab, R_ng], writes=[R_mod])
                    cj = 1.0 if j == 1 else 0.5
                    k.op('dve', lambda e: e.tensor_scalar(modp[:, l, j, 2, :], ab[:, l, base + 16:base + 24], 1.0, cj, ALU.add, ALU.mult),
                         reads=[R_ab], writes=[R_mod])
            k.barrier()

    def tok_stage(src, l_out, ffns, prenorm_l, dst):
        es, sb = stage_alloc()
        with es:
            xt = sb('xt', [128, 8, TT], F32); R_xt = [[Res() for _ in range(NSUB)] for _ in range(8)]
            hb = sb('hb', [128, 8, TT], BF16); R_hb = [Res() for _ in range(NSUB)]
            act = sb('actT', [128, NF, TT], BF16); R_act = [[Res() for _ in range(NSUB)] for _ in range(NF)]
            wo_sb = sb('wo_sb', [128, NF, D], BF16); R_wo = [Res() for _ in range(NF)]
            wi_bufs = [(sb('wi%d' % i, [128, 2048], BF16), Res()) for i in range(6)]
            wm = sb('wm', [128, 8, D], BF16); R_wm = Res()
            allx = [R_xt[dc][s] for dc in range(8) for s in range(NSUB)]

            def norm_to_hb(l, j, sub):
                sl = slice(sub * 512, (sub + 1) * 512)
                for dc in range(8):
                    k.op('act', lambda e: e.activation(act[:, dc, sl], xt[:, dc, sl], AF.Square), reads=[R_xt[dc][sub]], writes=[R_act[dc][sub]])
                pt, pr = psC.next()
                k.mm(pt[:], [(ones_bf, act[:, dc, sl]) for dc in range(8)], reads=[R_cb] + [R_act[dc][sub] for dc in range(8)], writes=[pr])
                rt, rr = rstd_from_ss(pt[:], pr, float(D))
                for dc in range(8):
                    t, r = tmp.next()
                    k.op('dve', lambda e: e.scalar_tensor_tensor(t[:], xt[:, dc, sl], modp[:, l, j, 1, dc:dc + 1], rt[:], ALU.mult, ALU.mult),
                         reads=[R_xt[dc][sub], R_mod, rr], writes=[r])
                    k.op('act', lambda e: e.activation(hb[:, dc, sl], t[:], AF.Identity, bias=modp[:, l, j, 0, dc:dc + 1], scale=1.0),
                         reads=[r, R_mod], writes=[R_hb[sub]])

            def resid_update(l, j, dc, sub, pt, pr):
                sl = slice(sub * 512, (sub + 1) * 512)
                k.op('dve', lambda e: e.scalar_tensor_tensor(xt[:, dc, sl], pt[:], modp[:, l, j, 2, dc:dc + 1], xt[:, dc, sl], ALU.mult, ALU.add),
                     reads=[pr, R_mod, R_xt[dc][sub]], writes=[R_xt[dc][sub]])

            ws = WStream(k, wi_bufs, kw=dict(max_dma_last_dim=8192))
            for ti in range(NTILE):
                for (l, j) in ffns:
                    for f in range(NF):
                        ws.add(ffw[l, j // 2, f, :, 0:2048])
            for ti in range(NTILE):
                t0 = ti * TT
                k.dma('sp', xt[:], src[:, :, t0:t0 + TT].rearrange("k p t -> p k t"), reads=[R_xs[ti]], writes=allx)
                if l_out is not None:
                    if ti == 0:
                        k.dma('pool', wm[:], woutm[l_out], writes=[R_wm], max_dma_last_dim=8192)
                    k.dma('sp', hb[:], os_[:, :, t0:t0 + TT].rearrange("k p t -> p k t"),
                          reads=[R_os[g][t0 // 512 + s] for s in range(NSUB) for g in range(2)], writes=R_hb)
                    for dc in range(8):
                        for sub in range(NSUB):
                            sl = slice(sub * 512, (sub + 1) * 512)
                            pt, pr = psB.next()
                            k.mm(pt[:], [(wm[:, kc, dc * 128:(dc + 1) * 128], hb[:, kc, sl]) for kc in range(8)], reads=[R_wm, R_hb[sub]], writes=[pr])
                            resid_update(l_out, 1, dc, sub, pt, pr)
                for (l, j) in ffns:
                    for sub in range(NSUB):
                        norm_to_hb(l, j, sub)
                    for f in range(NF):
                        wt, wr = ws.get()
                        k.dma('pool', wo_sb[:, f, :], ffw[l, j // 2, f, :, 2048:3072], writes=[R_wo[f]], max_dma_last_dim=8192)
                        for sub in range(NSUB):
                            sl = slice(sub * 512, (sub + 1) * 512)
                            pg, prg = psA.next()
                            pu, pru = psA.next()
                            k.mm(pg[:], [(wt[:, kc * 128:(kc + 1) * 128], hb[:, kc, sl]) for kc in range(8)], reads=[wr, R_hb[sub]], writes=[prg])
                            k.mm(pu[:], [(wt[:, 1024 + kc * 128:1024 + (kc + 1) * 128], hb[:, kc, sl]) for kc in range(8)], reads=[wr, R_hb[sub]], writes=[pru])
                            t, r = tmp.next()
                            sig_to(t[:], pg[:], prg, r)
                            k.op('dve', lambda e: e.tensor_tensor(t[:], t[:], pg[:], ALU.mult), reads=[r, prg], writes=[r])
                            k.op('dve', lambda e: e.tensor_tensor(act[:, f, sl], t[:], pu[:], ALU.mult), reads=[r, pru], writes=[R_act[f][sub]])
                        ws.pump()
                    for dc in range(8):
                        for sub in range(NSUB):
                            sl = slice(sub * 512, (sub + 1) * 512)
                            pt, pr = psB.next()
                            k.mm(pt[:], [(wo_sb[:, f, dc * 128:(dc + 1) * 128], act[:, f, sl]) for f in range(NF)],
                                 reads=R_wo + [R_act[f][sub] for f in range(NF)], writes=[pr])
                            resid_update(l, j, dc, sub, pt, pr)
                if prenorm_l is not None:
                    for sub in range(NSUB):
                        norm_to_hb(prenorm_l, 1, sub)
                    k.dma('sp', hs[:, :, t0:t0 + TT].rearrange("k p t -> p k t"), hb[:], reads=R_hb, writes=[R_hs[t0 // 512 + s] for s in range(NSUB)])
                k.dma('sp', dst[:, :, t0:t0 + TT].rearrange("k p t -> p k t"), xt[:], reads=allx, writes=[R_xs[ti]])
            k.barrier()

    def load_w(sb, srcs):
        out = []
        for i, s_ in enumerate(srcs):
            t = sb('w%d' % i, [128, 8, 512], BF16); r = Res()
            k.dma('pool', t[:], s_, writes=[r], max_dma_last_dim=8192)
            out.append((t, r))
        return out

    def mix_conv(l):
        for hg in range(2):
            es, sb = stage_alloc()
            with es:
                (wbg, rbg), (wcg, rcg), (wu, ru) = load_w(sb, [c_win[i, hg] for i in range(3)])
                cv = sb('cv', [128, 8, 3], F32); R_cv = Res()
                k.dma('sp', cv[:], c_cv[:, :, :], writes=[R_cv])
                up = sb('up', [128, 4, 514], F32); R_up = [Res() for _ in range(4)]
                k.op('dve', lambda e: e.memset(up[:], 0.0), writes=R_up)
                hts = Rot([(sb('ht%d' % i, [128, 8, 512], BF16), Res()) for i in range(2)])
                ots = Rot([(sb('ot%d' % i, [128, 4, 512], BF16), Res()) for i in range(2)])
                for ti in range(NB512):
                    t0 = ti * 512
                    ht, rh = hts.next()
                    k.dma('sp', ht[:], hs[:, :, t0:t0 + 512].rearrange("k p t -> p k t"), reads=[R_hs[ti]], writes=[rh])
                    ot, ro = ots.next()
                    for fc in range(4):
                        gfc = hg * 4 + fc
                        cs = slice(fc * 128, (fc + 1) * 128)
                        pb, prb = psA.next(); pc, prc = psA.next(); pu, pru = psA.next()
                        k.mm(pb[:], [(wbg[:, kc, cs], ht[:, kc, :]) for kc in range(8)], reads=[rbg, rh], writes=[prb])
                        k.mm(pc[:], [(wcg[:, kc, cs], ht[:, kc, :]) for kc in range(8)], reads=[rcg, rh], writes=[prc])
                        k.mm(pu[:], [(wu[:, kc, cs], ht[:, kc, :]) for kc in range(8)], reads=[ru, rh], writes=[pru])
                        t1, r1 = tmp.next()
                        k.op('act', lambda e: e.activation(t1[:], pc[:], AF.Identity), reads=[prc], writes=[r1])
                        k.op('dve', lambda e: e.tensor_tensor(up[:, fc, 2:514], t1[:], pu[:], ALU.mult), reads=[r1, pru], writes=[R_up[fc]])
                        t2, r2 = tmp.next()
                        k.op('dve', lambda e: e.tensor_scalar(t2[:], up[:, fc, 2:514], cv[:, gfc, 2:3], None, ALU.mult), reads=[R_up[fc], R_cv], writes=[r2])
                        k.op('dve', lambda e: e.scalar_tensor_tensor(t2[:], up[:, fc, 1:513], cv[:, gfc, 1:2], t2[:], ALU.mult, ALU.add), reads=[R_up[fc], R_cv, r2], writes=[r2])
                        k.op('dve', lambda e: e.scalar_tensor_tensor(t2[:], up[:, fc, 0:512], cv[:, gfc, 0:1], t2[:], ALU.mult, ALU.add), reads=[R_up[fc], R_cv, r2], writes=[r2])
                        k.op('dve', lambda e: e.tensor_tensor(ot[:, fc, :], t2[:], pb[:], ALU.mult), reads=[r2, prb], writes=[ro])
                        k.op('act', lambda e: e.activation(up[:, fc, 0:2], up[:, fc, 512:514], AF.Identity), reads=[R_up[fc]], writes=[R_up[fc]])
                    k.dma('sp', os_[hg * 4:(hg + 1) * 4, :, t0:t0 + 512].rearrange("k p t -> p k t"), ot[:], reads=[ro], writes=[R_os[hg][ti]])
                k.barrier()

    def rope_stage():
        es, sb = stage_alloc()
        with es:
            tl = Rot([[(sb('posi%d' % i, [128, 512], I32), Res()), (sb('cs%d' % i, [128, 2, 512], F32), Res())] for i in range(2)])
            for ti in range(NB512):
                (pi_, rpi), (cs2, rcs) = tl.next()
                rope_tables([(pi_, rpi), (cs2[:, 0, :], rcs), (cs2[:, 1, :], rcs)], ti * 512)
                k.dma('sp', rp[:, ti, :, :], cs2[:], reads=[rcs], writes=[R_rp[ti]])
            k.barrier()

    def rope_tables(sb_tiles, t0):
        (pi_t, R_pi), (ct_t, R_c), (st_t, R_s) = sb_tiles
        k.dma('sp', pi_t[:], pos_in[0:1, t0:t0 + 512].partition_broadcast(128), writes=[R_pi])
        ang, ra = tmp.next()
        k.op('dve', lambda e: e.tensor_copy(ang[:], pi_t[:]), reads=[R_pi], writes=[ra])
        k.op('dve', lambda e: e.tensor_scalar(ang[:], ang[:], cf[:, 768:769], None, ALU.mult), reads=[ra, R_cf], writes=[ra])
        for which, (dst, rd) in enumerate([(st_t, R_s), (ct_t, R_c)]):
            a2, r2 = tmp.next()
            n_, rn = tmp.next()
            off = 0.0 if which == 0 else 0.5 * math.pi
            k.op('dve', lambda e: e.tensor_scalar(a2[:], ang[:], off, None, ALU.add), reads=[ra], writes=[r2])
            k.op('dve', lambda e: e.tensor_scalar(n_[:], a2[:], 1.0 / TWO_PI, MAGIC, ALU.mult, ALU.add), reads=[r2], writes=[rn])
            k.op('dve', lambda e: e.tensor_scalar(n_[:], n_[:], -MAGIC, None, ALU.add), reads=[rn], writes=[rn])
            k.op('dve', lambda e: e.scalar_tensor_tensor(a2[:], n_[:], -C1, a2[:], ALU.mult, ALU.add), reads=[rn, r2], writes=[r2])
            k.op('dve', lambda e: e.scalar_tensor_tensor(a2[:], n_[:], -C2, a2[:], ALU.mult, ALU.add), reads=[rn, r2], writes=[r2])
            k.op('dve', lambda e: e.tensor_scalar(a2[:], a2[:], -PI_LO, PI_LO, ALU.max, ALU.min), reads=[r2], writes=[r2])
            k.op('act', lambda e: e.activation(dst, a2[:], AF.Sin), reads=[r2], writes=[rd])
        k.op('dve', lambda e: e.tensor_scalar(st_t, st_t, cf[:, 769:770], None, ALU.mult), reads=[R_s, R_cf], writes=[R_s])

    def mix_attn(l):
        lambda_init = 0.8 - 0.6 * math.exp(-0.3 * l)
        scale = 64 ** -0.5
        NBLK = S // 128
        for hg in range(2):
            es, sb = stage_alloc()
            with es:
                (wq, rq), (wk, rk), (wv, rv) = load_w(sb, [b_win[i, hg] for i in range(3)])
                sm = sb('sm', [128, 8], F32); R_sm = Res()
                k.dma('sp', sm[:, 0:2], b_qkg[:, :], writes=[R_sm])
                k.dma('sp', sm[:, 2:3], b_sub[:, :], writes=[R_sm])
                k.op('dve', lambda e: e.tensor_scalar(sm[:, 2:3], sm[:, 2:3], 1.0 - lambda_init, None, ALU.mult), reads=[R_sm], writes=[R_sm])
                lm = sb('lm', [1, 256], F32); R_lm = Res()
                k.dma('sp', lm[:], b_lam[:, :], writes=[R_lm])
                l2 = sb('l2', [1, 8], F32)
                k.op('dve', lambda e: e.tensor_tensor(lm[:, 0:64], lm[:, 0:64], lm[:, 64:128], ALU.mult), reads=[R_lm], writes=[R_lm])
                k.op('dve', lambda e: e.tensor_tensor(lm[:, 128:192], lm[:, 128:192], lm[:, 192:256], ALU.mult), reads=[R_lm], writes=[R_lm])
                k.op('dve', lambda e: e.reduce_sum(l2[:, 0:1], lm[:, 0:64], mybir.AxisListType.X), reads=[R_lm], writes=[R_lm])
                k.op('dve', lambda e: e.reduce_sum(l2[:, 1:2], lm[:, 128:192], mybir.AxisListType.X), reads=[R_lm], writes=[R_lm])
                k.op('act', lambda e: e.activation(l2[:, 0:2], l2[:, 0:2], AF.Exp), reads=[R_lm], writes=[R_lm])
                k.op('dve', lambda e: e.tensor_tensor(l2[:, 2:3], l2[:, 1:2], l2[:, 0:1], ALU.subtract), reads=[R_lm], writes=[R_lm])
                k.op('dve', lambda e: e.tensor_scalar(l2[:, 2:3], l2[:, 2:3], -lambda_init, None, ALU.add), reads=[R_lm], writes=[R_lm])
                pt, pr = psC.next()
                k.mm(pt[:, 0:1], [(cf[0:1, 512:640], l2[0:1, 2:3])], reads=[R_cf, R_lm], writes=[pr])
                k.op('dve', lambda e: e.tensor_copy(sm[:, 3:4], pt[:, 0:1]), reads=[pr], writes=[R_sm])

                kT = sb('kT', [128, 4, S], BF16); R_kT = [Res() for _ in range(NB512)]
                vt = sb('vt', [128, NBLK, 512], BF16); R_vt = [Res() for _ in range(NB512)]
                qt = sb('qt', [128, 4, 512], BF16); R_qt = [Res() for _ in range(4)]
                hts = Rot([(sb('ht%d' % i, [128, 8, 512], BF16), Res()) for i in range(1)])
                ots = Rot([(sb('ot%d' % i, [128, 4, 512], BF16), Res()) for i in range(1)])
                ebuf = Rot([(sb('e%d' % i, [128, 512], BF16), Res()) for i in range(3)])
                sqb = Rot([(sb('sq%d' % i, [128, 512], BF16), Res()) for i in range(2)])
                cs2 = sb('cs2', [128, 2, 512], F32); R_c = Res(); R_s = R_c
                cT = cs2[:, 0, :]; sT = cs2[:, 1, :]
                for ti in range(NB512):
                    t0 = ti * 512
                    ht, rh = hts.next()
                    k.dma('sp', ht[:], hs[:, :, t0:t0 + 512].rearrange("k p t -> p k t"), reads=[R_hs[ti]], writes=[rh])
                    k.dma('sp', cs2[:], rp[:, ti, :, :], reads=[R_rp[ti]], writes=[R_c])
                    for blk in range(4):
                        pv, prv = psC.next()
                        k.mm(pv[:], [(ht[:, kc, blk * 128:(blk + 1) * 128], wv[:, kc, :]) for kc in range(8)], reads=[rh, rv], writes=[prv])
                        k.op('act', lambda e: e.activation(vt[:, ti * 4 + blk, :], pv[:], AF.Identity), reads=[prv], writes=[R_vt[ti]])
                    for h in range(4):
                        cs = slice(h * 128, (h + 1) * 128)
                        for which, (w_, rw_) in enumerate([(wq, rq), (wk, rk)]):
                            pp, prp = psC.next()
                            k.mm(pp[:], [(w_[:, kc, cs], ht[:, kc, :]) for kc in range(8)], reads=[rw_, rh], writes=[prp])
                            sq, rsq = sqb.next()
                            k.op('act', lambda e: e.activation(sq[:], pp[:], AF.Square), reads=[prp], writes=[rsq])
                            pss, prs = psC.next()
                            k.mm(pss[:], [(bones_bf, sq[:])], reads=[R_cb, rsq], writes=[prs])
                            rt, rr = rstd_from_ss(pss[:], prs, 64.0)
                            t1, r1 = tmp.next()
                            k.op('dve', lambda e: e.scalar_tensor_tensor(t1[:], pp[:], sm[:, which:which + 1], rt[:], ALU.mult, ALU.mult), reads=[prp, R_sm, rr], writes=[r1])
                            pq, prq = psC.next()
                            k.mm(pq[:], [(cf[:, 256:384], t1[:])], reads=[R_cf, r1], writes=[prq])
                            t2, r2 = tmp.next()
                            k.op('dve', lambda e: e.tensor_tensor(t2[:], pq[:], sT, ALU.mult), reads=[prq, R_s], writes=[r2])
                            k.op('dve', lambda e: e.tensor_tensor(t1[:], t1[:], cT, ALU.mult), reads=[r1, R_c], writes=[r1])
                            if which == 0:
                                k.op('dve', lambda e: e.tensor_tensor(qt[:, h, :], t1[:], t2[:], ALU.add), reads=[r1, r2], writes=[R_qt[h]])
                            else:
                                k.op('dve', lambda e: e.tensor_tensor(kT[:, h, t0:t0 + 512], t1[:], t2[:], ALU.add), reads=[r1, r2], writes=[R_kT[ti]])
                    ot, ro = ots.next()
                    for h in range(4):
                        acc = [psA.next() for _ in range(4)]
                        nkb = ti * 4 + 4
                        for kb in range(nkb):
                            j = kb - ti * 4
                            n0 = max(0, j) * 128
                            for c in range(2):
                                ps_, prs_ = psB.next()
                                k.mm(ps_[:, n0:512], [(kT[c * 64:(c + 1) * 64, h, kb * 128:(kb + 1) * 128], qt[c * 64:(c + 1) * 64, h, n0:512])],
                                     reads=[R_kT[kb // 4], R_qt[h]], writes=[prs_])
                                eb, re_ = ebuf.next()
                                k.op('act', lambda e: e.activation(eb[:, n0:512], ps_[:, n0:512], AF.Exp, scale=scale), reads=[prs_], writes=[re_])
                                if j >= 0:
                                    k.op('pool', lambda e: e.tensor_tensor(eb[:, n0:n0 + 128], eb[:, n0:n0 + 128], cb[:, 384:512], ALU.mult), reads=[re_, R_cb], writes=[re_])
                                (po, pro), (pl, prl) = acc[2 * c], acc[2 * c + 1]
                                k.mm(po[:, n0:512], [(vt[:, kb, h * 128:(h + 1) * 128], eb[:, n0:512])], reads=[R_vt[kb // 4], re_], writes=[pro],
                                     start=(kb == 0), stop=(kb == nkb - 1))
                                k.mm(pl[:, n0:512], [(ones_bf, eb[:, n0:512])], reads=[R_cb, re_], writes=[prl], start=(kb == 0), stop=(kb == nkb - 1))
                        (po1, pro1), (pl1, prl1), (po2, pro2), (pl2, prl2) = acc
                        ra_, rra = tmp.next(); rb_, rrb = tmp.next()
                        k.op('dve', lambda e: e.reciprocal(ra_[:], pl1[:]), reads=[prl1], writes=[rra])
                        k.op('dve', lambda e: e.reciprocal(rb_[:], pl2[:]), reads=[prl2], writes=[rrb])
                        k.op('dve', lambda e: e.tensor_tensor(ra_[:], po1[:], ra_[:], ALU.mult), reads=[pro1, rra], writes=[rra])
                        k.op('dve', lambda e: e.scalar_tensor_tensor(rb_[:], rb_[:], sm[:, 3:4], po2[:], ALU.mult, ALU.mult), reads=[pro2, rrb, R_sm], writes=[rrb])
                        k.op('dve', lambda e: e.tensor_tensor(ra_[:], ra_[:], rb_[:], ALU.add), reads=[rra, rrb], writes=[rra])
                        sq, rsq = sqb.next()
                        k.op('act', lambda e: e.activation(sq[:], ra_[:], AF.Square), reads=[rra], writes=[rsq])
                        pss, prs = psC.next()
                        k.mm(pss[:], [(ones_bf, sq[:])], reads=[R_cb, rsq], writes=[prs])
                        rt, rr = rstd_from_ss(pss[:], prs, 128.0)
                        k.op('dve', lambda e: e.scalar_tensor_tensor(ot[:, h, :], ra_[:], sm[:, 2:3], rt[:], ALU.mult, ALU.mult), reads=[rra, R_sm, rr], writes=[ro])
                    k.dma('sp', os_[hg * 4:(hg + 1) * 4, :, t0:t0 + 512].rearrange("k p t -> p k t"), ot[:], reads=[ro], writes=[R_os[hg][ti]])
                k.barrier()

    def mix_hgrn(l):
        idx = l // 3
        for hg in range(2):
            es, sb = stage_alloc()
            with es:
                (wq, rq), (wf, rf), (wi_, ri), (wg, rg) = load_w(sb, [a_win[idx, i, hg] for i in range(4)])
                lbf = sb('lbf', [128, 2, 8], F32); R_lbf = Res()
                k.dma('sp', lbf[:], a_lbf[:, :, :], writes=[R_lbf])
                lbr = sb('lbr', [128, 2, 512], F32); R_lbr = Res()
                for i in range(2):
                    k.dma('sp', lbr[:, i, :], a_lbr[i:i + 1, hg * 512:(hg + 1) * 512].partition_broadcast(128), writes=[R_lbr])
                k.op('act', lambda e: e.activation(lbf[:], lbf[:], AF.Exp), reads=[R_lbf], writes=[R_lbf])
                k.op('act', lambda e: e.activation(lbr[:], lbr[:], AF.Exp), reads=[R_lbr], writes=[R_lbr])
                lb_f = sb('lb_f', [128, 8], F32); oml_f = sb('oml_f', [128, 8], F32)
                lb_r = sb('lb_r', [128, 512], F32); oml_r = sb('oml_r', [128, 512], F32)
                for (src_, lb_, oml_, R_) in [(lbf, lb_f, oml_f, R_lbf), (lbr, lb_r, oml_r, R_lbr)]:
                    k.op('dve', lambda e: e.tensor_tensor(oml_[:], src_[:, 0, :], src_[:, 1, :], ALU.add), reads=[R_], writes=[R_])
                    k.op('dve', lambda e: e.reciprocal(oml_[:], oml_[:]), reads=[R_], writes=[R_])
                    if idx == 0:
                        k.op('dve', lambda e: e.tensor_tensor(lb_[:], src_[:, 0, :], src_[:, 0, :], ALU.subtract), reads=[R_], writes=[R_])
                    else:
                        k.op('dve', lambda e: e.tensor_copy(lb_[:], src_[:, 1, :]), reads=[R_], writes=[R_])
                    k.op('dve', lambda e: e.tensor_tensor(lb_[:], lb_[:], oml_[:], ALU.mult), reads=[R_], writes=[R_])
                    k.op('dve', lambda e: e.tensor_scalar(oml_[:], lb_[:], -1.0, 1.0, ALU.mult, ALU.add), reads=[R_], writes=[R_])
                ong = sb('ong', [128, 2], F32); R_on = Res()
                k.dma('sp', ong[:], a_on[:, :], writes=[R_on])
                St = sb('St', [128, 4, 128], F32); R_S = [Res() for _ in range(4)]
                Sb = sb('Sb', [128, 4, 128], BF16); R_Sb = [Res() for _ in range(4)]
                k.op('dve', lambda e: e.memset(St[:], 0.0), writes=R_S)
                k.op('dve', lambda e: e.memset(Sb[:], 0.0), writes=R_Sb)
                hts = Rot([(sb('ht%d' % i, [128, 8, 512], BF16), Res()) for i in range(2)])
                ots = Rot([(sb('ot%d' % i, [128, 4, 512], BF16), Res()) for i in range(2)])
                vtm = sb('vtm', [128, 4, 512], BF16); R_v = [Res() for _ in range(4)]
                khat = sb('khat', [128, 4, 512], BF16); R_kh = [Res() for _ in range(4)]
                lgf = sb('lgf', [128, 4, 512], F32); R_lg = [Res() for _ in range(4)]
                qf = sb('qf', [128, 4, 512], F32); R_qf = [Res() for _ in range(4)]
                sg = sb('sg', [128, 4, 512], F32); R_sg = [Res() for _ in range(4)]
                e1 = sb('e1', [128, 4, 512], F32); R_e1 = [Res() for _ in range(4)]
                qi = sb('qi', [128, 4, 512], BF16); R_qi = [Res() for _ in range(4)]
                qtl = sb('qtl', [128, 4, 512], BF16); R_qtl = [Res() for _ in range(4)]
                ktl = sb('ktl', [128, 4, 512], BF16); R_ktl = [Res() for _ in range(4)]
                nr = sb('nr', [128, 4, 16], F32); R_nr = [Res() for _ in range(4)]
                of = sb('of', [128, 4, 512], F32); R_of = [Res() for _ in range(4)]
                atm = Rot([(sb('atm%d' % i, [128, 128], BF16), Res()) for i in range(3)])
                sqb = Rot([(sb('sq%d' % i, [128, 512], BF16), Res()) for i in range(2)])
                for ti in range(NB512):
                    t0 = ti * 512
                    ht, rh = hts.next()
                    k.dma('sp', ht[:], hs[:, :, t0:t0 + 512].rearrange("k p t -> p k t"), reads=[R_hs[ti]], writes=[rh])
                    for blk in range(4):
                        bs = slice(blk * 128, (blk + 1) * 128)
                        pv, prv = psC.next()
                        k.mm(pv[:], [(ht[:, kc, bs], wi_[:, kc, :]) for kc in range(8)], reads=[rh, ri], writes=[prv])
                        k.op('act', lambda e: e.activation(vtm[:, blk, :], pv[:], AF.Identity), reads=[prv], writes=[R_v[blk]])
                        pf, prf = psC.next()
                        k.mm(pf[:], [(ht[:, kc, bs], wf[:, kc, :]) for kc in range(8)], reads=[rh, rf], writes=[prf])
                        t1, r1 = tmp.next()
                        sig_to(t1[:], pf[:], prf, r1)
                        k.op('dve', lambda e: e.tensor_tensor(t1[:], t1[:], oml_r[:], ALU.mult), reads=[r1, R_lbr], writes=[r1])
                        k.op('dve', lambda e: e.tensor_tensor(t1[:], t1[:], lb_r[:], ALU.add), reads=[r1, R_lbr], writes=[r1])
                        k.op('act', lambda e: e.activation(lgf[:, blk, :], t1[:], AF.Ln), reads=[r1], writes=[R_lg[blk]])
                        k.op('dve', lambda e: e.tensor_scalar(t1[:], t1[:], -1.0, 1.0, ALU.mult, ALU.add), reads=[r1], writes=[r1])
                        pd, prd = psC.next()
                        k.mm(pd[:], [(cf[:, 128:256], lgf[:, blk, :])], reads=[R_cf, R_lg[blk]], writes=[prd])
                        t2, r2 = tmp.next()
                        k.op('act', lambda e: e.activation(t2[:], pd[:], AF.Exp), reads=[prd], writes=[r2])
                        k.op('dve', lambda e: e.tensor_tensor(khat[:, blk, :], t1[:], t2[:], ALU.mult), reads=[r1, r2], writes=[R_kh[blk]])
                    for h in range(4):
                        cs = slice(h * 128, (h + 1) * 128)
                        gh = hg * 4 + h
                        pq, prq = psA.next()
                        k.mm(pq[:], [(wq[:, kc, cs], ht[:, kc, :]) for kc in range(8)], reads=[rq, rh], writes=[prq])
                        sig_to(qf[:, h, :], pq[:], prq, R_qf[h])
                        k.op('dve', lambda e: e.tensor_tensor(qf[:, h, :], qf[:, h, :], pq[:], ALU.mult), reads=[R_qf[h], prq], writes=[R_qf[h]])
                        pg, prg = psA.next()
                        k.mm(pg[:], [(wg[:, kc, cs], ht[:, kc, :]) for kc in range(8)], reads=[rg, rh], writes=[prg])
                        sig_to(sg[:, h, :], pg[:], prg, R_sg[h])
                        k.op('dve', lambda e: e.tensor_tensor(sg[:, h, :], sg[:, h, :], pg[:], ALU.mult), reads=[R_sg[h], prg], writes=[R_sg[h]])
                        pf, prf = psA.next()
                        k.mm(pf[:], [(wf[:, kc, cs], ht[:, kc, :]) for kc in range(8)], reads=[rf, rh], writes=[prf])
                        kf, rkf = tmp.next()
                        sig_to(kf[:], pf[:], prf, rkf)
                        k.op('dve', lambda e: e.tensor_scalar(kf[:], kf[:], oml_f[:, gh:gh + 1], lb_f[:, gh:gh + 1], ALU.mult, ALU.add), reads=[rkf, R_lbf], writes=[rkf])
                        k.op('dve', lambda e: e.tensor_scalar(kf[:], kf[:], -1.0, 1.0, ALU.mult, ALU.add), reads=[rkf], writes=[rkf])
                        pb, prb = psA.next()
                        for blk in range(4):
                            k.mm(pb[:, blk * 128:(blk + 1) * 128], [(lgf[:, blk, cs], cf[:, 0:128])], reads=[R_lg[blk], R_cf], writes=[prb])
                        k.op('act', lambda e: e.activation(e1[:, h, :], pb[:], AF.Exp), reads=[prb], writes=[R_e1[h]])
                        b3 = pb[:].rearrange("p (c t) -> p c t", t=64)
                        k.op('dve', lambda e: e.tensor_copy(nr[:, h, 8:16], b3[:, :, 31]), reads=[prb], writes=[R_nr[h]])
                        k.op('dve', lambda e: e.tensor_scalar(nr[:, h, 0:8], nr[:, h, 8:16], -1.0, None, ALU.mult), reads=[R_nr[h]], writes=[R_nr[h]])
                        eq, req = tmp.next(); ek, rek = tmp.next()
                        for c in range(8):
                            c_ = slice(c * 64, (c + 1) * 64)
                            k.op('act', lambda e: e.activation(eq[:, c_], pb[:, c_], AF.Exp, bias=nr[:, h, c:c + 1], scale=1.0), reads=[prb, R_nr[h]], writes=[req])
                            k.op('act', lambda e: e.activation(ek[:, c_], pb[:, c_], AF.Exp, bias=nr[:, h, 8 + c:9 + c], scale=-1.0), reads=[prb, R_nr[h]], writes=[rek])
                        k.op('dve', lambda e: e.tensor_tensor(qtl[:, h, :], qf[:, h, :], eq[:], ALU.mult), reads=[R_qf[h], req], writes=[R_qtl[h]])
                        k.op('dve', lambda e: e.tensor_tensor(ktl[:, h, :], kf[:], ek[:], ALU.mult), reads=[rkf, rek], writes=[R_ktl[h]])
                        k.op('dve', lambda e: e.tensor_tensor(qi[:, h, :], qf[:, h, :], e1[:, h, :], ALU.mult), reads=[R_qf[h], R_e1[h]], writes=[R_qi[h]])
                    for blk in range(4):
                        bs = slice(blk * 128, (blk + 1) * 128)
                        for h in range(4):
                            cs = slice(h * 128, (h + 1) * 128)
                            pa, pra = psC.next()
                            k.mm(pa[:, 0:128], [(ktl[:, h, bs], qtl[:, h, bs])], reads=[R_ktl[h], R_qtl[h]], writes=[pra])
                            am, ram = atm.next()
                            k.op('dve', lambda e: e.tensor_tensor(am[:], pa[:, 0:128], cf[:, 0:128], ALU.mult), reads=[pra, R_cf], writes=[ram])
                            po, pro = psB.next()
                            for cc in range(2):
                                c = blk * 2 + cc
                                c_ = slice(c * 64, (c + 1) * 64)
                                rows = slice(cc * 64, (cc + 1) * 64)
                                k.mm(po[:, cc * 64:(cc + 1) * 64], [(Sb[:, h, :], qi[:, h, c_])], reads=[R_Sb[h], R_qi[h]], writes=[pro],
                                     start=(cc == 0), stop=False)
                                psn, prsn = psA.next()
                                k.mm(psn[:, 0:128], [(khat[rows, blk, cs], vtm[rows, blk, cs])], reads=[R_kh[blk], R_v[blk]], writes=[prsn])
                                k.op('dve', lambda e: e.scalar_tensor_tensor(St[:, h, :], St[:, h, :], e1[:, h, c * 64 + 63:c * 64 + 64], psn[:, 0:128], ALU.mult, ALU.add),
                                     reads=[R_S[h], R_e1[h], prsn], writes=[R_S[h]])
                                k.op('act', lambda e: e.activation(Sb[:, h, :], St[:, h, :], AF.Identity), reads=[R_S[h]], writes=[R_Sb[h]])
                            k.mm(po[:, 0:128], [(vtm[:, blk, cs], am[:])], reads=[R_v[blk], ram], writes=[pro], start=False, stop=True)
                            k.op('act', lambda e: e.activation(of[:, h, bs], po[:, 0:128], AF.Identity), reads=[pro], writes=[R_of[h]])
                    ot, ro = ots.next()
                    for h in range(4):
                        sq, rsq = sqb.next()
                        k.op('act', lambda e: e.activation(sq[:], of[:, h, :], AF.Square), reads=[R_of[h]], writes=[rsq])
                        pss, prs = psC.next()
                        k.mm(pss[:], [(ones_bf, sq[:])], reads=[R_cb, rsq], writes=[prs])
                        rt, rr = rstd_from_ss(pss[:], prs, 128.0)
                        t1, r1 = tmp.next()
                        k.op('dve', lambda e: e.tensor_tensor(t1[:], of[:, h, :], rt[:], ALU.mult), reads=[R_of[h], rr], writes=[r1])
                        k.op('dve', lambda e: e.scalar_tensor_tensor(ot[:, h, :], t1[:], ong[:, idx:idx + 1], sg[:, h, :], ALU.mult, ALU.mult), reads=[r1, R_on, R_sg[h]], writes=[ro])
                    k.dma('sp', os_[hg * 4:(hg + 1) * 4, :, t0:t0 + 512].rearrange("k p t -> p k t"), ot[:], reads=[ro], writes=[R_os[hg][ti]])
                k.barrier()

    MIX = {0: mix_hgrn, 1: mix_attn, 2: mix_conv}
    for st in stages:
        if st[0] == 'pro':
            prologue()
        elif st[0] == 'rope':
            rope_stage()
        elif st[0] == 'tok':
            _, src, l_out, ffns, pren, dst = st
            tok_stage(xT_in if src == 'in' else xs, l_out, ffns, pren, xs)
        elif st[0] == 'mix':
            MIX[st[1] % 3](st[1])
    k.finish('sp')
    stats = (k.n_instr, k.n_wait)
    k.close()
    return nc, stats


FULL_STAGES = [('rope',), ('pro',), ('tok', 'in', None, [(0, 0)], 0, 'xs')]
for _l in range(DEPTH):
    FULL_STAGES.append(('mix', _l))
    if _l < DEPTH - 1:
        FULL_STAGES.append(('tok', 'xs', _l, [(_l, 2), (_l + 1, 0)], _l + 1, 'xs'))
    else:
        FULL_STAGES.append(('tok', 'xs', _l, [(_l, 2)], None, 'out'))


def make_consts():
    c = np.zeros((128, 1024), np.float32)
    s = np.arange(128)[:, None]; t = np.arange(128)[None, :]
    same = (s // 64) == (t // 64)
    c[:, 0:128] = (same & (s <= t))
    c[:, 128:256] = (same & (s > t))
    P = np.zeros((128, 128), np.float32)
    for m in range(128):
        d = m % 64
        if d < 8:
            P[m, m + 8] = 1.0
        elif d < 16:
            P[m, m - 8] = 1.0
    c[:, 256:384] = P.T
    c[:, 384:512] = (s <= t)
    c[:, 512:640] = 1.0
    c[:, 640:768] = ((s // 64) == (t // 64))
    inv_freq = (500000.0 ** (-np.arange(0, 16, 2, dtype=np.float32) / 16)).astype(np.float32)
    for p in range(128):
        d = p % 64
        if d < 16:
            c[p, 768] = inv_freq[d % 8]
            c[p, 769] = -1.0 if d < 8 else 1.0
    return c


def prep_shared(inp):
    f32 = np.float32
    sh = {}
    sh['ada_w'] = np.ascontiguousarray(inp['ada_w'], f32)
    sh['ada_b_l'] = np.ascontiguousarray(inp['ada_b'].reshape(DEPTH, 72, 128).transpose(2, 0, 1), f32)
    sh['norm_g_l'] = np.ascontiguousarray(inp['norm_g'].reshape(DEPTH, 3, 8, 128).transpose(3, 0, 1, 2), f32)
    wi = inp['ffn_wi'].reshape(DEPTH, 2, 8, 128, 2, NF, 128)
    wi = wi.transpose(0, 1, 5, 3, 4, 2, 6).reshape(DEPTH, 2, NF, 128, 2048)
    wo = inp['ffn_wo'].reshape(DEPTH, 2, NF, 128, D)
    sh['ffw'] = np.ascontiguousarray(np.concatenate([wi, wo], axis=-1), f32)
    wouts = [inp['a_w_out'][0], inp['b_w_out'][0], inp['c_w_out'][0], inp['a_w_out'][1]]
    sh['wout_l'] = np.ascontiguousarray(np.stack([w.reshape(8, 128, D).transpose(1, 0, 2) for w in wouts]), f32)

    def inl(w, nsplit):
        w = w.reshape(8, 128, nsplit, 2, 512)
        return np.ascontiguousarray(w.transpose(2, 3, 1, 0, 4), f32)
    sh['a_win_l'] = np.stack([inl(inp['a_w_in'][i], 4) for i in range(2)])
    sh['b_win_l'] = inl(inp['b_w_in'][0], 3)
    sh['c_win_l'] = inl(inp['c_w_in'][0], 3)
    sh['a_lb_fm'] = np.ascontiguousarray(inp['a_lb'].reshape(2, 8, 128).transpose(2, 0, 1), f32)
    sh['a_lb_row'] = np.ascontiguousarray(inp['a_lb'], f32)
    sh['a_onorm_l'] = np.ascontiguousarray(inp['a_onorm'].T, f32)
    g = inp['b_qk_g'][0]
    sh['b_qkg_l'] = np.ascontiguousarray(np.concatenate([g, g], axis=1).T, f32)
    sh['b_lam'] = np.ascontiguousarray(inp['b_lam'][0].reshape(1, 256), f32)
    sh['b_subln_l'] = np.ascontiguousarray(inp['b_subln'][0].reshape(128, 1), f32)
    sh['c_conv_l'] = np.ascontiguousarray(inp['c_conv'][0].reshape(3, 8, 128).transpose(2, 1, 0), f32)
    sh['consts'] = make_consts()
    return sh


def prep_core(inp, b, S):
    m = {}
    m['xT'] = np.ascontiguousarray(inp['x'][b, :S].T.reshape(8, 128, S), np.float32)
    m['c_l'] = np.ascontiguousarray(inp['c'][b].reshape(8, 128).T, np.float32)
    m['pos'] = np.ascontiguousarray(inp['positions'][b, :S].reshape(1, S), np.int32)
    return m


_CACHE = {}


def kernel(**inputs):
    inp = {k_: np.asarray(v) for k_, v in inputs.items()}
    B, S, _ = inp['x'].shape
    if S not in _CACHE:
        _CACHE[S] = build_program(S, FULL_STAGES)[0]
    nc = _CACHE[S]
    sh = prep_shared(inp)
    in_maps = []
    for b in range(B):
        m = dict(sh)
        m.update(prep_core(inp, b, S))
        in_maps.append(m)
    res = run_bass_kernel_spmd(nc, in_maps, core_ids=list(range(B)))
    out = np.stack([res.results[b]['xT_out'].reshape(D, S).T for b in range(B)])
    return np.ascontiguousarray(out, np.float32)
```

```python
import contextlib
import math
import numpy as np
import ml_dtypes
import concourse.bass as bass
import concourse.mybir as mybir
from concourse.bass_utils import run_bass_kernel_spmd

F32 = mybir.dt.float32
BF16 = mybir.dt.bfloat16
I32 = mybir.dt.int32
AF = mybir.ActivationFunctionType
ALU = mybir.AluOpType

D = 1024
FF = 2816
NF = 22
DEPTH = 4
EPS = 1e-6
TWO_PI = 2.0 * math.pi
CC_INC = 16


class Res:
    __slots__ = ('w', 'r', 'name')

    def __init__(self, name=''):
        self.w = None
        self.r = {}
        self.name = name


class KB:
    NDMA = {'sp': 12, 'pool': 12}

    def __init__(self, nc):
        self.nc = nc
        self.es = contextlib.ExitStack()
        self.engs = {'pe': nc.tensor, 'act': nc.scalar, 'dve': nc.vector, 'pool': nc.gpsimd, 'sp': nc.sync}
        self.sems = {}
        self.cnt = {}
        self.cur = {}
        self.gen = 0
        self.retired = set()
        self._fresh()
        self.dsem = {}
        self.drr = {}
        for q, n in self.NDMA.items():
            self.dsem[q] = []
            self.drr[q] = 0
            for i in range(n):
                nm = 'd_%s%d' % (q, i)
                self.sems[nm] = self.es.enter_context(nc.semaphore(nm))
                self.cnt[nm] = 0
                self.dsem[q].append(nm)
        self.waited = {e: {} for e in self.engs}
        self.n_instr = 0
        self.n_wait = 0
        self.uid = 0

    def _fresh(self):
        for e in ['pe', 'act', 'dve', 'pool']:
            if e in self.cur:
                self.retired.add(self.cur[e])
            nm = '%s@%d' % (e, self.gen)
            self.sems[nm] = self.es.enter_context(self.nc.semaphore('s_%s_%d' % (e, self.gen)))
            self.cnt[nm] = 0
            self.cur[e] = nm
        self.gen += 1

    def sb(self, name, shape, dt):
        return self.es.enter_context(self.nc.sbuf_tensor(name, list(shape), dt))

    def ps(self, name, shape, dt=F32):
        return self.es.enter_context(self.nc.psum_tensor(name, list(shape), dt))

    def _wait(self, eng, dep):
        s, v = dep
        if s in self.retired:
            return
        if eng == 'pe' and s == self.cur['pe']:
            return
        if self.waited[eng].get(s, 0) >= v:
            return
        self.engs[eng].wait_ge(self.sems[s], v)
        self.waited[eng][s] = v
        self.n_wait += 1

    def _deps(self, eng, reads, writes):
        deps = {}
        for r in reads:
            if r.w is not None:
                s, v = r.w
                deps[s] = max(deps.get(s, 0), v)
        for w in writes:
            if w.w is not None:
                s, v = w.w
                deps[s] = max(deps.get(s, 0), v)
            for s, v in w.r.items():
                deps[s] = max(deps.get(s, 0), v)
        for s, v in deps.items():
            self._wait(eng, (s, v))

    def _mark(self, tick, reads, writes):
        s, v = tick
        for r in reads:
            r.r[s] = max(r.r.get(s, 0), v)
        for w in writes:
            w.w = tick
            w.r = {}

    def op(self, eng, fn, reads=(), writes=()):
        self._deps(eng, reads, writes)
        ins = fn(self.engs[eng])
        nm = self.cur[eng]
        ins.then_inc(self.sems[nm], 1)
        self.cnt[nm] += 1
        self.n_instr += 1
        self._mark((nm, self.cnt[nm]), reads, writes)

    def mm(self, out, pairs, reads=(), writes=(), start=True, stop=True):
        self._deps('pe', reads, writes)
        n = len(pairs)
        ins = None
        for i, (l, r) in enumerate(pairs):
            ins = self.nc.tensor.matmul(out, l, r, start=(start and i == 0), stop=(stop and i == n - 1))
            self.n_instr += 1
        nm = self.cur['pe']
        ins.then_inc(self.sems[nm], 1)
        self.cnt[nm] += 1
        self._mark((nm, self.cnt[nm]), reads, writes)

    def dma(self, q, out, in_, reads=(), writes=(), **kw):
        sl = self.dsem[q]
        nm = sl[self.drr[q] % len(sl)]
        self.drr[q] += 1
        if self.cnt[nm] > 0:
            self._wait(q, (nm, self.cnt[nm]))
        self._deps(q, reads, writes)
        self.engs[q].dma_start(out=out, in_=in_, **kw).then_inc(self.sems[nm], 16)
        self.cnt[nm] += 16
        self.n_instr += 1
        self._mark((nm, self.cnt[nm]), reads, writes)

    def coll(self, kind, groups, in_, out, reads=(), writes=()):
        q = 'pool'
        sl = self.dsem[q]
        nm = sl[self.drr[q] % len(sl)]
        self.drr[q] += 1
        if self.cnt[nm] > 0:
            self._wait(q, (nm, self.cnt[nm]))
        self._deps(q, reads, writes)
        self.nc.gpsimd.collective_compute(kind, ALU.bypass, replica_groups=groups, ins=[in_], outs=[out]).then_inc(self.sems[nm], CC_INC)
        self.cnt[nm] += CC_INC
        self.n_instr += 1
        self._mark((nm, self.cnt[nm]), reads, writes)

    def barrier(self, fresh=True):
        for eng in ['pe', 'act', 'dve', 'pool', 'sp']:
            for nm in self.sems:
                if self.cnt[nm] > 0 and nm not in self.retired:
                    if eng == 'pe' and nm == self.cur['pe']:
                        continue
                    self._wait(eng, (nm, self.cnt[nm]))
        if fresh and self.gen < 16:
            self._fresh()

    def finish(self, eng='sp'):
        for nm in self.sems:
            if self.cnt[nm] > 0 and nm not in self.retired:
                self._wait(eng, (nm, self.cnt[nm]))

    def close(self):
        self.es.close()


class Rot:
    def __init__(self, items):
        self.items = items
        self.i = 0

    def next(self):
        it = self.items[self.i % len(self.items)]
        self.i += 1
        return it


class WStream:
    def __init__(self, k, bufs, kw=None):
        self.k = k
        self.bufs = bufs
        self.q = []
        self.issued = 0
        self.used = 0
        self.kw = kw or {}
        self.srcs = []

    def add(self, src):
        self.srcs.append(src)

    def pump(self):
        while self.issued < len(self.srcs) and self.issued - self.used < len(self.bufs):
            t, r = self.bufs[self.issued % len(self.bufs)]
            src = self.srcs[self.issued]
            self.k.dma('pool', t[:], src, writes=[r], **self.kw)
            self.issued += 1

    def get(self):
        self.pump()
        assert self.used < self.issued
        it = self.bufs[self.used % len(self.bufs)]
        self.used += 1
        return it


MAGIC = 12582912.0
C1 = 6.28125
C2 = TWO_PI - 6.28125
PI_LO = 3.1415925


def build_program(S, stages):
    nc = bass.Bass("TRN2", target_bir_lowering=False)
    k = KB(nc)
    TT = min(1024, S)
    NSUB = TT // 512
    NTILE = S // TT
    NB512 = S // 512

    def din(name, shape, dt=F32):
        return nc.dram_tensor(name, list(shape), dt, kind="ExternalInput").ap()

    xT_in = din("xT", [8, 128, S])
    c_in = din("c_l", [128, 8])
    adaw = din("ada_w", [DEPTH, D, 9 * D])
    adab = din("ada_b_l", [128, DEPTH, 72])
    normg = din("norm_g_l", [128, DEPTH, 3, 8])
    ffw = din("ffw", [DEPTH, 2, NF, 128, 3072])
    woutm = din("wout_l", [DEPTH, 128, 8, D])
    a_win = din("a_win_l", [2, 4, 2, 128, 8, 512])
    a_lbf = din("a_lb_fm", [128, 2, 8])
    a_lbr = din("a_lb_row", [2, D])
    a_on = din("a_onorm_l", [128, 2])
    b_win = din("b_win_l", [3, 2, 128, 8, 512])
    b_qkg = din("b_qkg_l", [128, 2])
    b_lam = din("b_lam", [1, 256])
    b_sub = din("b_subln_l", [128, 1])
    c_win = din("c_win_l", [3, 2, 128, 8, 512])
    c_cv = din("c_conv_l", [128, 8, 3])
    pos_in = din("pos", [1, S], I32)
    cst = din("consts", [128, 1024])
    xT_out = nc.dram_tensor("xT_out", [8, 128, S], F32, kind="ExternalOutput").ap()
    xs = xT_out
    hs = nc.dram_tensor("hs", [8, 128, S], BF16).ap()
    os_ = nc.dram_tensor("os", [8, 128, S], BF16).ap()
    R_xs = [Res() for _ in range(NTILE)]
    R_hs = [Res() for _ in range(NB512)]
    R_os = [[Res() for _ in range(NB512)] for _ in range(2)]
    rp = nc.dram_tensor("rp", [128, NB512, 2, 512], F32).ap()
    R_rp = [Res() for _ in range(NB512)]

    cf = k.sb('cf', [128, 1024], F32); R_cf = Res()
    k.dma('sp', cf[:], cst[:, :], writes=[R_cf])
    cb = k.sb('cb', [128, 1024], BF16); R_cb = Res()
    k.op('dve', lambda e: e.tensor_copy(cb[:], cf[:]), reads=[R_cf], writes=[R_cb])
    ones_bf = cb[:, 512:640]
    bones_bf = cb[:, 640:768]
    modp = k.sb('modp', [128, DEPTH, 3, 3, 8], F32); R_mod = Res()
    epsb = k.sb('epsb', [128, 1], F32)
    k.op('dve', lambda e: e.memset(epsb[:], EPS), writes=[R_cf])

    PS = [k.ps('ps%d' % i, [128, 512]) for i in range(8)]
    RPS = [Res() for _ in range(8)]
    psA = Rot([(PS[i], RPS[i]) for i in range(4)])
    psB = Rot([(PS[i], RPS[i]) for i in range(4, 6)])
    psC = Rot([(PS[i], RPS[i]) for i in range(6, 8)])
    TMP = [(k.sb('tmp%d' % i, [128, 512], F32), Res()) for i in range(6)]
    tmp = Rot(TMP)
    rsp = Rot([(k.sb('rsp%d' % i, [128, 512], F32), Res()) for i in range(2)])

    def rstd_from_ss(ss_ps, R_ss, n, width=512):
        t, r = rsp.next()
        k.op('act', lambda e: e.activation(t[:, :width], ss_ps, AF.Ln, bias=epsb[:, 0:1], scale=1.0 / n), reads=[R_ss, R_cf], writes=[r])
        k.op('act', lambda e: e.activation(t[:, :width], t[:, :width], AF.Exp, scale=-0.5), reads=[r], writes=[r])
        return t, r

    def sig_to(dst, src_ap, R_src, R_dst, extra_reads=()):
        k.op('act', lambda e: e.activation(dst, src_ap, AF.Exp, scale=-1.0), reads=[R_src] + list(extra_reads), writes=[R_dst])
        k.op('act', lambda e: e.activation(dst, dst, AF.Identity, bias=cf[:, 512:513], scale=1.0), reads=[R_dst, R_cf], writes=[R_dst])
        k.op('dve', lambda e: e.reciprocal(dst, dst), reads=[R_dst], writes=[R_dst])

    def stage_alloc():
        es = contextlib.ExitStack()

        def sb(name, shape, dt):
            k.uid += 1
            return es.enter_context(nc.sbuf_tensor('%s_%d' % (name, k.uid), list(shape), dt))
        return es, sb

    def prologue():
        es, sb = stage_alloc()
        with es:
            ct = sb('ct', [128, 8], F32); R_ct = Res()
            k.dma('sp', ct[:], c_in[:, :], writes=[R_ct])
            cs_ = sb('cs_', [128, 8], F32)
            sig_to(cs_[:], ct[:], R_ct, R_ct)
            k.op('dve', lambda e: e.tensor_tensor(ct[:], ct[:], cs_[:], ALU.mult), reads=[R_ct], writes=[R_ct])
            ab = sb('ab', [128, DEPTH, 72], F32); R_ab = Res()
            k.dma('sp', ab[:], adab[:, :, :], writes=[R_ab])
            ng = sb('ng', [128, DEPTH, 3, 8], F32); R_ng = Res()
            k.dma('sp', ng[:], normg[:, :, :, :], writes=[R_ng])
            CW = 1152
            awb = Rot([(sb('awb%d' % i, [128, 8, CW], F32), Res()) for i in range(2)])
            ncc = CW // 128
            for l in range(DEPTH):
                for g in range(9 * D // CW):
                    t, r = awb.next()
                    src = adaw[l, :, g * CW:(g + 1) * CW].rearrange("(k p) c -> p k c", p=128)
                    k.dma('sp', t[:], src, writes=[r])
                    pt, pr = psC.next()
                    for cc in range(ncc):
                        k.mm(pt[:, cc:cc + 1], [(t[:, kc, cc * 128:(cc + 1) * 128], ct[:, kc:kc + 1]) for kc in range(8)],
                             reads=[r, R_ct], writes=[pr])
                    k.op('dve', lambda e: e.tensor_tensor(ab[:, l, g * ncc:(g + 1) * ncc], pt[:, 0:ncc], ab[:, l, g * ncc:(g + 1) * ncc], ALU.add),
                         reads=[pr, R_ab], writes=[R_ab])
            for l in range(DEPTH):
                for j in range(3):
                    base = j * 24
                    k.op('dve', lambda e: e.tensor_copy(modp[:, l, j, 0, :], ab[:, l, base:base + 8]), reads=[R_ab], writes=[R_mod])
                    k.op('dve', lambda e: e.scalar_tensor_tensor(modp[:, l, j, 1, :], ab[:, l, base + 8:base + 16], 1.0, ng[:, l, j, :], ALU.add, ALU.mult),
                         reads=[R_ab, R_ng], writes=[R_mod])
                    cj = 1.0 if j == 1 else 0.5
                    k.op('dve', lambda e: e.tensor_scalar(modp[:, l, j, 2, :], ab[:, l, base + 16:base + 24], 1.0, cj, ALU.add, ALU.mult),
                         reads=[R_ab], writes=[R_mod])
            k.barrier()

    def tok_stage(src, l_out, ffns, prenorm_l, dst):
        es, sb = stage_alloc()
        with es:
            xt = sb('xt', [128, 8, TT], F32); R_xt = [[Res() for _ in range(NSUB)] for _ in range(8)]
            hb = sb('hb', [128, 8, TT], BF16); R_hb = [Res() for _ in range(NSUB)]
            act = sb('actT', [128, NF, TT], BF16); R_act = [[Res() for _ in range(NSUB)] for _ in range(NF)]
            wo_sb = sb('wo_sb', [128, NF, D], BF16); R_wo = [Res() for _ in range(NF)]
            wi_bufs = [(sb('wi%d' % i, [128, 2048], BF16), Res()) for i in range(6)]
            wm = sb('wm', [128, 8, D], BF16); R_wm = Res()
            allx = [R_xt[dc][s] for dc in range(8) for s in range(NSUB)]

            def norm_to_hb(l, j, sub):
                sl = slice(sub * 512, (sub + 1) * 512)
                for dc in range(8):
                    k.op('act', lambda e: e.activation(act[:, dc, sl], xt[:, dc, sl], AF.Square), reads=[R_xt[dc][sub]], writes=[R_act[dc][sub]])
                pt, pr = psC.next()
                k.mm(pt[:], [(ones_bf, act[:, dc, sl]) for dc in range(8)], reads=[R_cb] + [R_act[dc][sub] for dc in range(8)], writes=[pr])
                rt, rr = rstd_from_ss(pt[:], pr, float(D))
                for dc in range(8):
                    t, r = tmp.next()
                    k.op('dve', lambda e: e.scalar_tensor_tensor(t[:], xt[:, dc, sl], modp[:, l, j, 1, dc:dc + 1], rt[:], ALU.mult, ALU.mult),
                         reads=[R_xt[dc][sub], R_mod, rr], writes=[r])
                    k.op('act', lambda e: e.activation(hb[:, dc, sl], t[:], AF.Identity, bias=modp[:, l, j, 0, dc:dc + 1], scale=1.0),
                         reads=[r, R_mod], writes=[R_hb[sub]])

            def resid_update(l, j, dc, sub, pt, pr):
                sl = slice(sub * 512, (sub + 1) * 512)
                k.op('dve', lambda e: e.scalar_tensor_tensor(xt[:, dc, sl], pt[:], modp[:, l, j, 2, dc:dc + 1], xt[:, dc, sl], ALU.mult, ALU.add),
                     reads=[pr, R_mod, R_xt[dc][sub]], writes=[R_xt[dc][sub]])

            ws = WStream(k, wi_bufs, kw=dict(max_dma_last_dim=8192))
            for ti in range(NTILE):
                for (l, j) in ffns:
                    for f in range(NF):
                        ws.add(ffw[l, j // 2, f, :, 0:2048])
            for ti in range(NTILE):
                t0 = ti * TT
                k.dma('sp', xt[:], src[:, :, t0:t0 + TT].rearrange("k p t -> p k t"), reads=[R_xs[ti]], writes=allx)
                if l_out is not None:
                    if ti == 0:
                        k.dma('pool', wm[:], woutm[l_out], writes=[R_wm], max_dma_last_dim=8192)
                    k.dma('sp', hb[:], os_[:, :, t0:t0 + TT].rearrange("k p t -> p k t"),
                          reads=[R_os[g][t0 // 512 + s] for s in range(NSUB) for g in range(2)], writes=R_hb)
                    for dc in range(8):
                        for sub in range(NSUB):
                            sl = slice(sub * 512, (sub + 1) * 512)
                            pt, pr = psB.next()
                            k.mm(pt[:], [(wm[:, kc, dc * 128:(dc + 1) * 128], hb[:, kc, sl]) for kc in range(8)], reads=[R_wm, R_hb[sub]], writes=[pr])
                            resid_update(l_out, 1, dc, sub, pt, pr)
                for (l, j) in ffns:
                    for sub in range(NSUB):
                        norm_to_hb(l, j, sub)
                    for f in range(NF):
                        wt, wr = ws.get()
                        k.dma('pool', wo_sb[:, f, :], ffw[l, j // 2, f, :, 2048:3072], writes=[R_wo[f]], max_dma_last_dim=8192)
                        for sub in range(NSUB):
                            sl = slice(sub * 512, (sub + 1) * 512)
                            pg, prg = psA.next()
                            pu, pru = psA.next()
                            k.mm(pg[:], [(wt[:, kc * 128:(kc + 1) * 128], hb[:, kc, sl]) for kc in range(8)], reads=[wr, R_hb[sub]], writes=[prg])
                            k.mm(pu[:], [(wt[:, 1024 + kc * 128:1024 + (kc + 1) * 128], hb[:, kc, sl]) for kc in range(8)], reads=[wr, R_hb[sub]], writes=[pru])
                            t, r = tmp.next()
                            sig_to(t[:], pg[:], prg, r)
                            k.op('dve', lambda e: e.tensor_tensor(t[:], t[:], pg[:], ALU.mult), reads=[r, prg], writes=[r])
                            k.op('dve', lambda e: e.tensor_tensor(act[:, f, sl], t[:], pu[:], ALU.mult), reads=[r, pru], writes=[R_act[f][sub]])
                        ws.pump()
                    for dc in range(8):
                        for sub in range(NSUB):
                            sl = slice(sub * 512, (sub + 1) * 512)
                            pt, pr = psB.next()
                            k.mm(pt[:], [(wo_sb[:, f, dc * 128:(dc + 1) * 128], act[:, f, sl]) for f in range(NF)],
                                 reads=R_wo + [R_act[f][sub] for f in range(NF)], writes=[pr])
                            resid_update(l, j, dc, sub, pt, pr)
                if prenorm_l is not None:
                    for sub in range(NSUB):
                        norm_to_hb(prenorm_l, 1, sub)
                    k.dma('sp', hs[:, :, t0:t0 + TT].rearrange("k p t -> p k t"), hb[:], reads=R_hb, writes=[R_hs[t0 // 512 + s] for s in range(NSUB)])
                k.dma('sp', dst[:, :, t0:t0 + TT].rearrange("k p t -> p k t"), xt[:], reads=allx, writes=[R_xs[ti]])
            k.barrier()

    def load_w(sb, srcs):
        out = []
        for i, s_ in enumerate(srcs):
            t = sb('w%d' % i, [128, 8, 512], BF16); r = Res()
            k.dma('pool', t[:], s_, writes=[r], max_dma_last_dim=8192)
            out.append((t, r))
        return out

    def mix_conv(l):
        for hg in range(2):
            es, sb = stage_alloc()
            with es:
                (wbg, rbg), (wcg, rcg), (wu, ru) = load_w(sb, [c_win[i, hg] for i in range(3)])
                cv = sb('cv', [128, 8, 3], F32); R_cv = Res()
                k.dma('sp', cv[:], c_cv[:, :, :], writes=[R_cv])
                up = sb('up', [128, 4, 514], F32); R_up = [Res() for _ in range(4)]
                k.op('dve', lambda e: e.memset(up[:], 0.0), writes=R_up)
                hts = Rot([(sb('ht%d' % i, [128, 8, 512], BF16), Res()) for i in range(2)])
                ots = Rot([(sb('ot%d' % i, [128, 4, 512], BF16), Res()) for i in range(2)])
                for ti in range(NB512):
                    t0 = ti * 512
                    ht, rh = hts.next()
                    k.dma('sp', ht[:], hs[:, :, t0:t0 + 512].rearrange("k p t -> p k t"), reads=[R_hs[ti]], writes=[rh])
                    ot, ro = ots.next()
                    for fc in range(4):
                        gfc = hg * 4 + fc
                        cs = slice(fc * 128, (fc + 1) * 128)
                        pb, prb = psA.next(); pc, prc = psA.next(); pu, pru = psA.next()
                        k.mm(pb[:], [(wbg[:, kc, cs], ht[:, kc, :]) for kc in range(8)], reads=[rbg, rh], writes=[prb])
                        k.mm(pc[:], [(wcg[:, kc, cs], ht[:, kc, :]) for kc in range(8)], reads=[rcg, rh], writes=[prc])
                        k.mm(pu[:], [(wu[:, kc, cs], ht[:, kc, :]) for kc in range(8)], reads=[ru, rh], writes=[pru])
                        t1, r1 = tmp.next()
                        k.op('act', lambda e: e.activation(t1[:], pc[:], AF.Identity), reads=[prc], writes=[r1])
                        k.op('dve', lambda e: e.tensor_tensor(up[:, fc, 2:514], t1[:], pu[:], ALU.mult), reads=[r1, pru], writes=[R_up[fc]])
                        t2, r2 = tmp.next()
                        k.op('dve', lambda e: e.tensor_scalar(t2[:], up[:, fc, 2:514], cv[:, gfc, 2:3], None, ALU.mult), reads=[R_up[fc], R_cv], writes=[r2])
                        k.op('dve', lambda e: e.scalar_tensor_tensor(t2[:], up[:, fc, 1:513], cv[:, gfc, 1:2], t2[:], ALU.mult, ALU.add), reads=[R_up[fc], R_cv, r2], writes=[r2])
                        k.op('dve', lambda e: e.scalar_tensor_tensor(t2[:], up[:, fc, 0:512], cv[:, gfc, 0:1], t2[:], ALU.mult, ALU.add), reads=[R_up[fc], R_cv, r2], writes=[r2])
                        k.op('dve', lambda e: e.tensor_tensor(ot[:, fc, :], t2[:], pb[:], ALU.mult), reads=[r2, prb], writes=[ro])
                        k.op('act', lambda e: e.activation(up[:, fc, 0:2], up[:, fc, 512:514], AF.Identity), reads=[R_up[fc]], writes=[R_up[fc]])
                    k.dma('sp', os_[hg * 4:(hg + 1) * 4, :, t0:t0 + 512].rearrange("k p t -> p k t"), ot[:], reads=[ro], writes=[R_os[hg][ti]])
                k.barrier()

    def rope_stage():
        es, sb = stage_alloc()
        with es:
            tl = Rot([[(sb('posi%d' % i, [128, 512], I32), Res()), (sb('cs%d' % i, [128, 2, 512], F32), Res())] for i in range(2)])
            for ti in range(NB512):
                (pi_, rpi), (cs2, rcs) = tl.next()
                rope_tables([(pi_, rpi), (cs2[:, 0, :], rcs), (cs2[:, 1, :], rcs)], ti * 512)
                k.dma('sp', rp[:, ti, :, :], cs2[:], reads=[rcs], writes=[R_rp[ti]])
            k.barrier()

    def rope_tables(sb_tiles, t0):
        (pi_t, R_pi), (ct_t, R_c), (st_t, R_s) = sb_tiles
        k.dma('sp', pi_t[:], pos_in[0:1, t0:t0 + 512].partition_broadcast(128), writes=[R_pi])
        ang, ra = tmp.next()
        k.op('dve', lambda e: e.tensor_copy(ang[:], pi_t[:]), reads=[R_pi], writes=[ra])
        k.op('dve', lambda e: e.tensor_scalar(ang[:], ang[:], cf[:, 768:769], None, ALU.mult), reads=[ra, R_cf], writes=[ra])
        for which, (dst, rd) in enumerate([(st_t, R_s), (ct_t, R_c)]):
            a2, r2 = tmp.next()
            n_, rn = tmp.next()
            off = 0.0 if which == 0 else 0.5 * math.pi
            k.op('dve', lambda e: e.tensor_scalar(a2[:], ang[:], off, None, ALU.add), reads=[ra], writes=[r2])
            k.op('dve', lambda e: e.tensor_scalar(n_[:], a2[:], 1.0 / TWO_PI, MAGIC, ALU.mult, ALU.add), reads=[r2], writes=[rn])
            k.op('dve', lambda e: e.tensor_scalar(n_[:], n_[:], -MAGIC, None, ALU.add), reads=[rn], writes=[rn])
            k.op('dve', lambda e: e.scalar_tensor_tensor(a2[:], n_[:], -C1, a2[:], ALU.mult, ALU.add), reads=[rn, r2], writes=[r2])
            k.op('dve', lambda e: e.scalar_tensor_tensor(a2[:], n_[:], -C2, a2[:], ALU.mult, ALU.add), reads=[rn, r2], writes=[r2])
            k.op('dve', lambda e: e.tensor_scalar(a2[:], a2[:], -PI_LO, PI_LO, ALU.max, ALU.min), reads=[r2], writes=[r2])
            k.op('act', lambda e: e.activation(dst, a2[:], AF.Sin), reads=[r2], writes=[rd])
        k.op('dve', lambda e: e.tensor_scalar(st_t, st_t, cf[:, 769:770], None, ALU.mult), reads=[R_s, R_cf], writes=[R_s])

    def mix_attn(l):
        lambda_init = 0.8 - 0.6 * math.exp(-0.3 * l)
        scale = 64 ** -0.5
        NBLK = S // 128
        for hg in range(2):
            es, sb = stage_alloc()
            with es:
                (wq, rq), (wk, rk), (wv, rv) = load_w(sb, [b_win[i, hg] for i in range(3)])
                sm = sb('sm', [128, 8], F32); R_sm = Res()
                k.dma('sp', sm[:, 0:2], b_qkg[:, :], writes=[R_sm])
                k.dma('sp', sm[:, 2:3], b_sub[:, :], writes=[R_sm])
                k.op('dve', lambda e: e.tensor_scalar(sm[:, 2:3], sm[:, 2:3], 1.0 - lambda_init, None, ALU.mult), reads=[R_sm], writes=[R_sm])
                lm = sb('lm', [1, 256], F32); R_lm = Res()
                k.dma('sp', lm[:], b_lam[:, :], writes=[R_lm])
                l2 = sb('l2', [1, 8], F32)
                k.op('dve', lambda e: e.tensor_tensor(lm[:, 0:64], lm[:, 0:64], lm[:, 64:128], ALU.mult), reads=[R_lm], writes=[R_lm])
                k.op('dve', lambda e: e.tensor_tensor(lm[:, 128:192], lm[:, 128:192], lm[:, 192:256], ALU.mult), reads=[R_lm], writes=[R_lm])
                k.op('dve', lambda e: e.reduce_sum(l2[:, 0:1], lm[:, 0:64], mybir.AxisListType.X), reads=[R_lm], writes=[R_lm])
                k.op('dve', lambda e: e.reduce_sum(l2[:, 1:2], lm[:, 128:192], mybir.AxisListType.X), reads=[R_lm], writes=[R_lm])
                k.op('act', lambda e: e.activation(l2[:, 0:2], l2[:, 0:2], AF.Exp), reads=[R_lm], writes=[R_lm])
                k.op('dve', lambda e: e.tensor_tensor(l2[:, 2:3], l2[:, 1:2], l2[:, 0:1], ALU.subtract), reads=[R_lm], writes=[R_lm])
                k.op('dve', lambda e: e.tensor_scalar(l2[:, 2:3], l2[:, 2:3], -lambda_init, None, ALU.add), reads=[R_lm], writes=[R_lm])
                pt, pr = psC.next()
                k.mm(pt[:, 0:1], [(cf[0:1, 512:640], l2[0:1, 2:3])], reads=[R_cf, R_lm], writes=[pr])
                k.op('dve', lambda e: e.tensor_copy(sm[:, 3:4], pt[:, 0:1]), reads=[pr], writes=[R_sm])

                kT = sb('kT', [128, 4, S], BF16); R_kT = [Res() for _ in range(NB512)]
                vt = sb('vt', [128, NBLK, 512], BF16); R_vt = [Res() for _ in range(NB512)]
                qt = sb('qt', [128, 4, 512], BF16); R_qt = [Res() for _ in range(4)]
                hts = Rot([(sb('ht%d' % i, [128, 8, 512], BF16), Res()) for i in range(1)])
                ots = Rot([(sb('ot%d' % i, [128, 4, 512], BF16), Res()) for i in range(1)])
                ebuf = Rot([(sb('e%d' % i, [128, 512], BF16), Res()) for i in range(4)])
                spool = Rot([(PS[i], RPS[i]) for i in range(4, 8)])
                sqb = Rot([(sb('sq%d' % i, [128, 512], BF16), Res()) for i in range(2)])
                cs2 = sb('cs2', [128, 2, 512], F32); R_c = Res(); R_s = R_c
                cT = cs2[:, 0, :]; sT = cs2[:, 1, :]
                for ti in range(NB512):
                    t0 = ti * 512
                    ht, rh = hts.next()
                    k.dma('sp', ht[:], hs[:, :, t0:t0 + 512].rearrange("k p t -> p k t"), reads=[R_hs[ti]], writes=[rh])
                    k.dma('sp', cs2[:], rp[:, ti, :, :], reads=[R_rp[ti]], writes=[R_c])
                    for blk in range(4):
                        pv, prv = psC.next()
                        k.mm(pv[:], [(ht[:, kc, blk * 128:(blk + 1) * 128], wv[:, kc, :]) for kc in range(8)], reads=[rh, rv], writes=[prv])
                        k.op('act', lambda e: e.activation(vt[:, ti * 4 + blk, :], pv[:], AF.Identity), reads=[prv], writes=[R_vt[ti]])
                    for h in range(4):
                        cs = slice(h * 128, (h + 1) * 128)
                        for which, (w_, rw_) in enumerate([(wq, rq), (wk, rk)]):
                            pp, prp = psC.next()
                            k.mm(pp[:], [(w_[:, kc, cs], ht[:, kc, :]) for kc in range(8)], reads=[rw_, rh], writes=[prp])
                            sq, rsq = sqb.next()
                            k.op('act', lambda e: e.activation(sq[:], pp[:], AF.Square), reads=[prp], writes=[rsq])
                            pss, prs = psC.next()
                            k.mm(pss[:], [(bones_bf, sq[:])], reads=[R_cb, rsq], writes=[prs])
                            rt, rr = rstd_from_ss(pss[:], prs, 64.0)
                            t1, r1 = tmp.next()
                            k.op('dve', lambda e: e.scalar_tensor_tensor(t1[:], pp[:], sm[:, which:which + 1], rt[:], ALU.mult, ALU.mult), reads=[prp, R_sm, rr], writes=[r1])
                            pq, prq = psC.next()
                            k.mm(pq[:], [(cf[:, 256:384], t1[:])], reads=[R_cf, r1], writes=[prq])
                            t2, r2 = tmp.next()
                            k.op('dve', lambda e: e.tensor_tensor(t2[:], pq[:], sT, ALU.mult), reads=[prq, R_s], writes=[r2])
                            k.op('dve', lambda e: e.tensor_tensor(t1[:], t1[:], cT, ALU.mult), reads=[r1, R_c], writes=[r1])
                            if which == 0:
                                k.op('dve', lambda e: e.tensor_tensor(qt[:, h, :], t1[:], t2[:], ALU.add), reads=[r1, r2], writes=[R_qt[h]])
                            else:
                                k.op('dve', lambda e: e.tensor_tensor(kT[:, h, t0:t0 + 512], t1[:], t2[:], ALU.add), reads=[r1, r2], writes=[R_kT[ti]])
                    ot, ro = ots.next()
                    for h in range(4):
                        acc = [psA.next() for _ in range(4)]
                        nkb = ti * 4 + 4
                        units = [(kb, c) for kb in range(nkb) for c in range(2)]
                        pend = {}

                        def emit_s(u):
                            kb, c = u
                            j = kb - ti * 4
                            n0 = max(0, j) * 128
                            ps_, prs_ = spool.next()
                            k.mm(ps_[:, n0:512], [(kT[c * 64:(c + 1) * 64, h, kb * 128:(kb + 1) * 128], qt[c * 64:(c + 1) * 64, h, n0:512])],
                                 reads=[R_kT[kb // 4], R_qt[h]], writes=[prs_])
                            eb, re_ = ebuf.next()
                            k.op('act', lambda e: e.activation(eb[:, n0:512], ps_[:, n0:512], AF.Exp, scale=scale), reads=[prs_], writes=[re_])
                            if j >= 0:
                                k.op('pool', lambda e: e.tensor_tensor(eb[:, n0:n0 + 128], eb[:, n0:n0 + 128], cb[:, 384:512], ALU.mult), reads=[re_, R_cb], writes=[re_])
                            pend[u] = (eb, re_, n0)

                        def emit_pv(u):
                            kb, c = u
                            eb, re_, n0 = pend.pop(u)
                            (po, pro), (pl, prl) = acc[2 * c], acc[2 * c + 1]
                            k.mm(po[:, n0:512], [(vt[:, kb, h * 128:(h + 1) * 128], eb[:, n0:512])], reads=[R_vt[kb // 4], re_], writes=[pro],
                                 start=(kb == 0), stop=(kb == nkb - 1))
                            k.mm(pl[:, n0:512], [(ones_bf, eb[:, n0:512])], reads=[R_cb, re_], writes=[prl], start=(kb == 0), stop=(kb == nkb - 1))

                        LOOK = 2
                        for i in range(len(units) + LOOK):
                            if i < len(units):
                                emit_s(units[i])
                            if i >= LOOK:
                                emit_pv(units[i - LOOK])
                        (po1, pro1), (pl1, prl1), (po2, pro2), (pl2, prl2) = acc
                        ra_, rra = tmp.next(); rb_, rrb = tmp.next()
                        k.op('dve', lambda e: e.reciprocal(ra_[:], pl1[:]), reads=[prl1], writes=[rra])
                        k.op('dve', lambda e: e.reciprocal(rb_[:], pl2[:]), reads=[prl2], writes=[rrb])
                        k.op('dve', lambda e: e.tensor_tensor(ra_[:], po1[:], ra_[:], ALU.mult), reads=[pro1, rra], writes=[rra])
                        k.op('dve', lambda e: e.scalar_tensor_tensor(rb_[:], rb_[:], sm[:, 3:4], po2[:], ALU.mult, ALU.mult), reads=[pro2, rrb, R_sm], writes=[rrb])
                        k.op('dve', lambda e: e.tensor_tensor(ra_[:], ra_[:], rb_[:], ALU.add), reads=[rra, rrb], writes=[rra])
                        sq, rsq = sqb.next()
                        k.op('act', lambda e: e.activation(sq[:], ra_[:], AF.Square), reads=[rra], writes=[rsq])
                        pss, prs = psC.next()
                        k.mm(pss[:], [(ones_bf, sq[:])], reads=[R_cb, rsq], writes=[prs])
                        rt, rr = rstd_from_ss(pss[:], prs, 128.0)
                        k.op('dve', lambda e: e.scalar_tensor_tensor(ot[:, h, :], ra_[:], sm[:, 2:3], rt[:], ALU.mult, ALU.mult), reads=[rra, R_sm, rr], writes=[ro])
                    k.dma('sp', os_[hg * 4:(hg + 1) * 4, :, t0:t0 + 512].rearrange("k p t -> p k t"), ot[:], reads=[ro], writes=[R_os[hg][ti]])
                k.barrier()

    def mix_hgrn(l):
        idx = l // 3
        for hg in range(2):
            es, sb = stage_alloc()
            with es:
                (wq, rq), (wf, rf), (wi_, ri), (wg, rg) = load_w(sb, [a_win[idx, i, hg] for i in range(4)])
                lbf = sb('lbf', [128, 2, 8], F32); R_lbf = Res()
                k.dma('sp', lbf[:], a_lbf[:, :, :], writes=[R_lbf])
                lbr = sb('lbr', [128, 2, 512], F32); R_lbr = Res()
                for i in range(2):
                    k.dma('sp', lbr[:, i, :], a_lbr[i:i + 1, hg * 512:(hg + 1) * 512].partition_broadcast(128), writes=[R_lbr])
                k.op('act', lambda e: e.activation(lbf[:], lbf[:], AF.Exp), reads=[R_lbf], writes=[R_lbf])
                k.op('act', lambda e: e.activation(lbr[:], lbr[:], AF.Exp), reads=[R_lbr], writes=[R_lbr])
                lb_f = sb('lb_f', [128, 8], F32); oml_f = sb('oml_f', [128, 8], F32)
                lb_r = sb('lb_r', [128, 512], F32); oml_r = sb('oml_r', [128, 512], F32)
                for (src_, lb_, oml_, R_) in [(lbf, lb_f, oml_f, R_lbf), (lbr, lb_r, oml_r, R_lbr)]:
                    k.op('dve', lambda e: e.tensor_tensor(oml_[:], src_[:, 0, :], src_[:, 1, :], ALU.add), reads=[R_], writes=[R_])
                    k.op('dve', lambda e: e.reciprocal(oml_[:], oml_[:]), reads=[R_], writes=[R_])
                    if idx == 0:
                        k.op('dve', lambda e: e.tensor_tensor(lb_[:], src_[:, 0, :], src_[:, 0, :], ALU.subtract), reads=[R_], writes=[R_])
                    else:
                        k.op('dve', lambda e: e.tensor_copy(lb_[:], src_[:, 1, :]), reads=[R_], writes=[R_])
                    k.op('dve', lambda e: e.tensor_tensor(lb_[:], lb_[:], oml_[:], ALU.mult), reads=[R_], writes=[R_])
                    k.op('dve', lambda e: e.tensor_scalar(oml_[:], lb_[:], -1.0, 1.0, ALU.mult, ALU.add), reads=[R_], writes=[R_])
                ong = sb('ong', [128, 2], F32); R_on = Res()
                k.dma('sp', ong[:], a_on[:, :], writes=[R_on])
                St = sb('St', [128, 4, 128], F32); R_S = [Res() for _ in range(4)]
                Sb = sb('Sb', [128, 4, 128], BF16); R_Sb = [Res() for _ in range(4)]
                k.op('dve', lambda e: e.memset(St[:], 0.0), writes=R_S)
                k.op('dve', lambda e: e.memset(Sb[:], 0.0), writes=R_Sb)
                hts = Rot([(sb('ht%d' % i, [128, 8, 512], BF16), Res()) for i in range(2)])
                ots = Rot([(sb('ot%d' % i, [128, 4, 512], BF16), Res()) for i in range(2)])
                vtm = sb('vtm', [128, 4, 512], BF16); R_v = [Res() for _ in range(4)]
                khat = sb('khat', [128, 4, 512], BF16); R_kh = [Res() for _ in range(4)]
                lgf = sb('lgf', [128, 4, 512], F32); R_lg = [Res() for _ in range(4)]
                qf = sb('qf', [128, 4, 512], F32); R_qf = [Res() for _ in range(4)]
                sg = sb('sg', [128, 4, 512], F32); R_sg = [Res() for _ in range(4)]
                e1 = sb('e1', [128, 4, 512], F32); R_e1 = [Res() for _ in range(4)]
                qi = sb('qi', [128, 4, 512], BF16); R_qi = [Res() for _ in range(4)]
                qtl = sb('qtl', [128, 4, 512], BF16); R_qtl = [Res() for _ in range(4)]
                ktl = sb('ktl', [128, 4, 512], BF16); R_ktl = [Res() for _ in range(4)]
                nr = sb('nr', [128, 4, 16], F32); R_nr = [Res() for _ in range(4)]
                of = sb('of', [128, 4, 512], F32); R_of = [Res() for _ in range(4)]
                atm = Rot([(sb('atm%d' % i, [128, 128], BF16), Res()) for i in range(3)])
                sqb = Rot([(sb('sq%d' % i, [128, 512], BF16), Res()) for i in range(2)])
                for ti in range(NB512):
                    t0 = ti * 512
                    ht, rh = hts.next()
                    k.dma('sp', ht[:], hs[:, :, t0:t0 + 512].rearrange("k p t -> p k t"), reads=[R_hs[ti]], writes=[rh])
                    for blk in range(4):
                        bs = slice(blk * 128, (blk + 1) * 128)
                        pv, prv = psC.next()
                        k.mm(pv[:], [(ht[:, kc, bs], wi_[:, kc, :]) for kc in range(8)], reads=[rh, ri], writes=[prv])
                        k.op('act', lambda e: e.activation(vtm[:, blk, :], pv[:], AF.Identity), reads=[prv], writes=[R_v[blk]])
                        pf, prf = psC.next()
                        k.mm(pf[:], [(ht[:, kc, bs], wf[:, kc, :]) for kc in range(8)], reads=[rh, rf], writes=[prf])
                        t1, r1 = tmp.next()
                        sig_to(t1[:], pf[:], prf, r1)
                        k.op('dve', lambda e: e.tensor_tensor(t1[:], t1[:], oml_r[:], ALU.mult), reads=[r1, R_lbr], writes=[r1])
                        k.op('dve', lambda e: e.tensor_tensor(t1[:], t1[:], lb_r[:], ALU.add), reads=[r1, R_lbr], writes=[r1])
                        k.op('act', lambda e: e.activation(lgf[:, blk, :], t1[:], AF.Ln), reads=[r1], writes=[R_lg[blk]])
                        k.op('dve', lambda e: e.tensor_scalar(t1[:], t1[:], -1.0, 1.0, ALU.mult, ALU.add), reads=[r1], writes=[r1])
                        pd, prd = psC.next()
                        k.mm(pd[:], [(cf[:, 128:256], lgf[:, blk, :])], reads=[R_cf, R_lg[blk]], writes=[prd])
                        t2, r2 = tmp.next()
                        k.op('act', lambda e: e.activation(t2[:], pd[:], AF.Exp), reads=[prd], writes=[r2])
                        k.op('dve', lambda e: e.tensor_tensor(khat[:, blk, :], t1[:], t2[:], ALU.mult), reads=[r1, r2], writes=[R_kh[blk]])
                    for h in range(4):
                        cs = slice(h * 128, (h + 1) * 128)
                        gh = hg * 4 + h
                        pq, prq = psA.next()
                        k.mm(pq[:], [(wq[:, kc, cs], ht[:, kc, :]) for kc in range(8)], reads=[rq, rh], writes=[prq])
                        sig_to(qf[:, h, :], pq[:], prq, R_qf[h])
                        k.op('dve', lambda e: e.tensor_tensor(qf[:, h, :], qf[:, h, :], pq[:], ALU.mult), reads=[R_qf[h], prq], writes=[R_qf[h]])
                        pg, prg = psA.next()
                        k.mm(pg[:], [(wg[:, kc, cs], ht[:, kc, :]) for kc in range(8)], reads=[rg, rh], writes=[prg])
                        sig_to(sg[:, h, :], pg[:], prg, R_sg[h])
                        k.op('dve', lambda e: e.tensor_tensor(sg[:, h, :], sg[:, h, :], pg[:], ALU.mult), reads=[R_sg[h], prg], writes=[R_sg[h]])
                        pf, prf = psA.next()
                        k.mm(pf[:], [(wf[:, kc, cs], ht[:, kc, :]) for kc in range(8)], reads=[rf, rh], writes=[prf])
                        kf, rkf = tmp.next()
                        sig_to(kf[:], pf[:], prf, rkf)
                        k.op('dve', lambda e: e.tensor_scalar(kf[:], kf[:], oml_f[:, gh:gh + 1], lb_f[:, gh:gh + 1], ALU.mult, ALU.add), reads=[rkf, R_lbf], writes=[rkf])
                        k.op('dve', lambda e: e.tensor_scalar(kf[:], kf[:], -1.0, 1.0, ALU.mult, ALU.add), reads=[rkf], writes=[rkf])
                        pb, prb = psA.next()
                        for blk in range(4):
                            k.mm(pb[:, blk * 128:(blk + 1) * 128], [(lgf[:, blk, cs], cf[:, 0:128])], reads=[R_lg[blk], R_cf], writes=[prb])
                        k.op('act', lambda e: e.activation(e1[:, h, :], pb[:], AF.Exp), reads=[prb], writes=[R_e1[h]])
                        b3 = pb[:].rearrange("p (c t) -> p c t", t=64)
                        k.op('dve', lambda e: e.tensor_copy(nr[:, h, 8:16], b3[:, :, 31]), reads=[prb], writes=[R_nr[h]])
                        k.op('dve', lambda e: e.tensor_scalar(nr[:, h, 0:8], nr[:, h, 8:16], -1.0, None, ALU.mult), reads=[R_nr[h]], writes=[R_nr[h]])
                        eq, req = tmp.next(); ek, rek = tmp.next()
                        for c in range(8):
                            c_ = slice(c * 64, (c + 1) * 64)
                            k.op('act', lambda e: e.activation(eq[:, c_], pb[:, c_], AF.Exp, bias=nr[:, h, c:c + 1], scale=1.0), reads=[prb, R_nr[h]], writes=[req])
                            k.op('act', lambda e: e.activation(ek[:, c_], pb[:, c_], AF.Exp, bias=nr[:, h, 8 + c:9 + c], scale=-1.0), reads=[prb, R_nr[h]], writes=[rek])
                        k.op('dve', lambda e: e.tensor_tensor(qtl[:, h, :], qf[:, h, :], eq[:], ALU.mult), reads=[R_qf[h], req], writes=[R_qtl[h]])
                        k.op('dve', lambda e: e.tensor_tensor(ktl[:, h, :], kf[:], ek[:], ALU.mult), reads=[rkf, rek], writes=[R_ktl[h]])
                        k.op('dve', lambda e: e.tensor_tensor(qi[:, h, :], qf[:, h, :], e1[:, h, :], ALU.mult), reads=[R_qf[h], R_e1[h]], writes=[R_qi[h]])
                    for blk in range(4):
                        bs = slice(blk * 128, (blk + 1) * 128)
                        for h in range(4):
                            cs = slice(h * 128, (h + 1) * 128)
                            pa, pra = psC.next()
                            k.mm(pa[:, 0:128], [(ktl[:, h, bs], qtl[:, h, bs])], reads=[R_ktl[h], R_qtl[h]], writes=[pra])
                            am, ram = atm.next()
                            k.op('dve', lambda e: e.tensor_tensor(am[:], pa[:, 0:128], cf[:, 0:128], ALU.mult), reads=[pra, R_cf], writes=[ram])
                            po, pro = psB.next()
                            for cc in range(2):
                                c = blk * 2 + cc
                                c_ = slice(c * 64, (c + 1) * 64)
                                rows = slice(cc * 64, (cc + 1) * 64)
                                k.mm(po[:, cc * 64:(cc + 1) * 64], [(Sb[:, h, :], qi[:, h, c_])], reads=[R_Sb[h], R_qi[h]], writes=[pro],
                                     start=(cc == 0), stop=False)
                                psn, prsn = psA.next()
                                k.mm(psn[:, 0:128], [(khat[rows, blk, cs], vtm[rows, blk, cs])], reads=[R_kh[blk], R_v[blk]], writes=[prsn])
                                k.op('dve', lambda e: e.scalar_tensor_tensor(St[:, h, :], St[:, h, :], e1[:, h, c * 64 + 63:c * 64 + 64], psn[:, 0:128], ALU.mult, ALU.add),
                                     reads=[R_S[h], R_e1[h], prsn], writes=[R_S[h]])
                                k.op('act', lambda e: e.activation(Sb[:, h, :], St[:, h, :], AF.Identity), reads=[R_S[h]], writes=[R_Sb[h]])
                            k.mm(po[:, 0:128], [(vtm[:, blk, cs], am[:])], reads=[R_v[blk], ram], writes=[pro], start=False, stop=True)
                            k.op('act', lambda e: e.activation(of[:, h, bs], po[:, 0:128], AF.Identity), reads=[pro], writes=[R_of[h]])
                    ot, ro = ots.next()
                    for h in range(4):
                        sq, rsq = sqb.next()
                        k.op('act', lambda e: e.activation(sq[:], of[:, h, :], AF.Square), reads=[R_of[h]], writes=[rsq])
                        pss, prs = psC.next()
                        k.mm(pss[:], [(ones_bf, sq[:])], reads=[R_cb, rsq], writes=[prs])
                        rt, rr = rstd_from_ss(pss[:], prs, 128.0)
                        t1, r1 = tmp.next()
                        k.op('dve', lambda e: e.tensor_tensor(t1[:], of[:, h, :], rt[:], ALU.mult), reads=[R_of[h], rr], writes=[r1])
                        k.op('dve', lambda e: e.scalar_tensor_tensor(ot[:, h, :], t1[:], ong[:, idx:idx + 1], sg[:, h, :], ALU.mult, ALU.mult), reads=[r1, R_on, R_sg[h]], writes=[ro])
                    k.dma('sp', os_[hg * 4:(hg + 1) * 4, :, t0:t0 + 512].rearrange("k p t -> p k t"), ot[:], reads=[ro], writes=[R_os[hg][ti]])
                k.barrier()

    MIX = {0: mix_hgrn, 1: mix_attn, 2: mix_conv}
    for st in stages:
        if st[0] == 'pro':
            prologue()
        elif st[0] == 'rope':
            rope_stage()
        elif st[0] == 'tok':
            _, src, l_out, ffns, pren, dst = st
            tok_stage(xT_in if src == 'in' else xs, l_out, ffns, pren, xs)
        elif st[0] == 'mix':
            MIX[st[1] % 3](st[1])
    k.finish('sp')
    stats = (k.n_instr, k.n_wait)
    k.close()
    return nc, stats


FULL_STAGES = [('rope',), ('pro',), ('tok', 'in', None, [(0, 0)], 0, 'xs')]
for _l in range(DEPTH):
    FULL_STAGES.append(('mix', _l))
    if _l < DEPTH - 1:
        FULL_STAGES.append(('tok', 'xs', _l, [(_l, 2), (_l + 1, 0)], _l + 1, 'xs'))
    else:
        FULL_STAGES.append(('tok', 'xs', _l, [(_l, 2)], None, 'out'))


def make_consts():
    c = np.zeros((128, 1024), np.float32)
    s = np.arange(128)[:, None]; t = np.arange(128)[None, :]
    same = (s // 64) == (t // 64)
    c[:, 0:128] = (same & (s <= t))
    c[:, 128:256] = (same & (s > t))
    P = np.zeros((128, 128), np.float32)
    for m in range(128):
        d = m % 64
        if d < 8:
            P[m, m + 8] = 1.0
        elif d < 16:
            P[m, m - 8] = 1.0
    c[:, 256:384] = P.T
    c[:, 384:512] = (s <= t)
    c[:, 512:640] = 1.0
    c[:, 640:768] = ((s // 64) == (t // 64))
    inv_freq = (500000.0 ** (-np.arange(0, 16, 2, dtype=np.float32) / 16)).astype(np.float32)
    for p in range(128):
        d = p % 64
        if d < 16:
            c[p, 768] = inv_freq[d % 8]
            c[p, 769] = -1.0 if d < 8 else 1.0
    return c


def prep_shared(inp):
    f32 = np.float32
    sh = {}
    sh['ada_w'] = np.ascontiguousarray(inp['ada_w'], f32)
    sh['ada_b_l'] = np.ascontiguousarray(inp['ada_b'].reshape(DEPTH, 72, 128).transpose(2, 0, 1), f32)
    sh['norm_g_l'] = np.ascontiguousarray(inp['norm_g'].reshape(DEPTH, 3, 8, 128).transpose(3, 0, 1, 2), f32)
    wi = inp['ffn_wi'].reshape(DEPTH, 2, 8, 128, 2, NF, 128)
    wi = wi.transpose(0, 1, 5, 3, 4, 2, 6).reshape(DEPTH, 2, NF, 128, 2048)
    wo = inp['ffn_wo'].reshape(DEPTH, 2, NF, 128, D)
    sh['ffw'] = np.ascontiguousarray(np.concatenate([wi, wo], axis=-1), f32)
    wouts = [inp['a_w_out'][0], inp['b_w_out'][0], inp['c_w_out'][0], inp['a_w_out'][1]]
    sh['wout_l'] = np.ascontiguousarray(np.stack([w.reshape(8, 128, D).transpose(1, 0, 2) for w in wouts]), f32)

    def inl(w, nsplit):
        w = w.reshape(8, 128, nsplit, 2, 512)
        return np.ascontiguousarray(w.transpose(2, 3, 1, 0, 4), f32)
    sh['a_win_l'] = np.stack([inl(inp['a_w_in'][i], 4) for i in range(2)])
    sh['b_win_l'] = inl(inp['b_w_in'][0], 3)
    sh['c_win_l'] = inl(inp['c_w_in'][0], 3)
    sh['a_lb_fm'] = np.ascontiguousarray(inp['a_lb'].reshape(2, 8, 128).transpose(2, 0, 1), f32)
    sh['a_lb_row'] = np.ascontiguousarray(inp['a_lb'], f32)
    sh['a_onorm_l'] = np.ascontiguousarray(inp['a_onorm'].T, f32)
    g = inp['b_qk_g'][0]
    sh['b_qkg_l'] = np.ascontiguousarray(np.concatenate([g, g], axis=1).T, f32)
    sh['b_lam'] = np.ascontiguousarray(inp['b_lam'][0].reshape(1, 256), f32)
    sh['b_subln_l'] = np.ascontiguousarray(inp['b_subln'][0].reshape(128, 1), f32)
    sh['c_conv_l'] = np.ascontiguousarray(inp['c_conv'][0].reshape(3, 8, 128).transpose(2, 1, 0), f32)
    sh['consts'] = make_consts()
    return sh


def prep_core(inp, b, S):
    m = {}
    m['xT'] = np.ascontiguousarray(inp['x'][b, :S].T.reshape(8, 128, S), np.float32)
    m['c_l'] = np.ascontiguousarray(inp['c'][b].reshape(8, 128).T, np.float32)
    m['pos'] = np.ascontiguousarray(inp['positions'][b, :S].reshape(1, S), np.int32)
    return m


_CACHE = {}


def kernel(**inputs):
    inp = {k_: np.asarray(v) for k_, v in inputs.items()}
    B, S, _ = inp['x'].shape
    if S not in _CACHE:
        _CACHE[S] = build_program(S, FULL_STAGES)[0]
    nc = _CACHE[S]
    sh = prep_shared(inp)
    in_maps = []
    for b in range(B):
        m = dict(sh)
        m.update(prep_core(inp, b, S))
        in_maps.append(m)
    res = run_bass_kernel_spmd(nc, in_maps, core_ids=list(range(B)))
    out = np.stack([res.results[b]['xT_out'].reshape(D, S).T for b in range(B)])
    return np.ascontiguousarray(out, np.float32)
```

```python
import contextlib
import math
import numpy as np
import ml_dtypes
import concourse.bass as bass
import concourse.mybir as mybir
from concourse.bass_utils import run_bass_kernel_spmd

F32 = mybir.dt.float32
BF16 = mybir.dt.bfloat16
I32 = mybir.dt.int32
AF = mybir.ActivationFunctionType
ALU = mybir.AluOpType

D = 1024
FF = 2816
NF = 22
DEPTH = 4
EPS = 1e-6
TWO_PI = 2.0 * math.pi
CC_INC = 16


class Res:
    __slots__ = ('w', 'r', 'name')

    def __init__(self, name=''):
        self.w = None
        self.r = {}
        self.name = name


class KB:
    NDMA = {'sp': 12, 'pool': 12}

    def __init__(self, nc):
        self.nc = nc
        self.es = contextlib.ExitStack()
        self.engs = {'pe': nc.tensor, 'act': nc.scalar, 'dve': nc.vector, 'pool': nc.gpsimd, 'sp': nc.sync}
        self.sems = {}
        self.cnt = {}
        self.cur = {}
        self.gen = 0
        self.retired = set()
        self._fresh()
        self.dsem = {}
        self.drr = {}
        for q, n in self.NDMA.items():
            self.dsem[q] = []
            self.drr[q] = 0
            for i in range(n):
                nm = 'd_%s%d' % (q, i)
                self.sems[nm] = self.es.enter_context(nc.semaphore(nm))
                self.cnt[nm] = 0
                self.dsem[q].append(nm)
        self.waited = {e: {} for e in self.engs}
        self.n_instr = 0
        self.n_wait = 0
        self.uid = 0

    def _fresh(self):
        for e in ['pe', 'act', 'dve', 'pool']:
            if e in self.cur:
                self.retired.add(self.cur[e])
            nm = '%s@%d' % (e, self.gen)
            self.sems[nm] = self.es.enter_context(self.nc.semaphore('s_%s_%d' % (e, self.gen)))
            self.cnt[nm] = 0
            self.cur[e] = nm
        self.gen += 1

    def sb(self, name, shape, dt):
        return self.es.enter_context(self.nc.sbuf_tensor(name, list(shape), dt))

    def ps(self, name, shape, dt=F32):
        return self.es.enter_context(self.nc.psum_tensor(name, list(shape), dt))

    def _wait(self, eng, dep):
        s, v = dep
        if s in self.retired:
            return
        if eng == 'pe' and s == self.cur['pe']:
            return
        if self.waited[eng].get(s, 0) >= v:
            return
        self.engs[eng].wait_ge(self.sems[s], v)
        self.waited[eng][s] = v
        self.n_wait += 1

    def _deps(self, eng, reads, writes):
        deps = {}
        for r in reads:
            if r.w is not None:
                s, v = r.w
                deps[s] = max(deps.get(s, 0), v)
        for w in writes:
            if w.w is not None:
                s, v = w.w
                deps[s] = max(deps.get(s, 0), v)
            for s, v in w.r.items():
                deps[s] = max(deps.get(s, 0), v)
        for s, v in deps.items():
            self._wait(eng, (s, v))

    def _mark(self, tick, reads, writes):
        s, v = tick
        for r in reads:
            r.r[s] = max(r.r.get(s, 0), v)
        for w in writes:
            w.w = tick
            w.r = {}

    def op(self, eng, fn, reads=(), writes=()):
        self._deps(eng, reads, writes)
        ins = fn(self.engs[eng])
        nm = self.cur[eng]
        ins.then_inc(self.sems[nm], 1)
        self.cnt[nm] += 1
        self.n_instr += 1
        self._mark((nm, self.cnt[nm]), reads, writes)

    def mm(self, out, pairs, reads=(), writes=(), start=True, stop=True):
        self._deps('pe', reads, writes)
        n = len(pairs)
        ins = None
        for i, (l, r) in enumerate(pairs):
            ins = self.nc.tensor.matmul(out, l, r, start=(start and i == 0), stop=(stop and i == n - 1))
            self.n_instr += 1
        nm = self.cur['pe']
        ins.then_inc(self.sems[nm], 1)
        self.cnt[nm] += 1
        self._mark((nm, self.cnt[nm]), reads, writes)

    def dma(self, q, out, in_, reads=(), writes=(), **kw):
        sl = self.dsem[q]
        nm = sl[self.drr[q] % len(sl)]
        self.drr[q] += 1
        if self.cnt[nm] > 0:
            self._wait(q, (nm, self.cnt[nm]))
        self._deps(q, reads, writes)
        self.engs[q].dma_start(out=out, in_=in_, **kw).then_inc(self.sems[nm], 16)
        self.cnt[nm] += 16
        self.n_instr += 1
        self._mark((nm, self.cnt[nm]), reads, writes)

    def coll(self, kind, groups, in_, out, reads=(), writes=()):
        q = 'pool'
        sl = self.dsem[q]
        nm = sl[self.drr[q] % len(sl)]
        self.drr[q] += 1
        if self.cnt[nm] > 0:
            self._wait(q, (nm, self.cnt[nm]))
        self._deps(q, reads, writes)
        self.nc.gpsimd.collective_compute(kind, ALU.bypass, replica_groups=groups, ins=[in_], outs=[out]).then_inc(self.sems[nm], CC_INC)
        self.cnt[nm] += CC_INC
        self.n_instr += 1
        self._mark((nm, self.cnt[nm]), reads, writes)

    def barrier(self, fresh=True):
        for eng in ['pe', 'act', 'dve', 'pool', 'sp']:
            for nm in self.sems:
                if self.cnt[nm] > 0 and nm not in self.retired:
                    if eng == 'pe' and nm == self.cur['pe']:
                        continue
                    self._wait(eng, (nm, self.cnt[nm]))
        if fresh and self.gen < 16:
            self._fresh()

    def finish(self, eng='sp'):
        for nm in self.sems:
            if self.cnt[nm] > 0 and nm not in self.retired:
                self._wait(eng, (nm, self.cnt[nm]))

    def close(self):
        self.es.close()


class Rot:
    def __init__(self, items):
        self.items = items
        self.i = 0

    def next(self):
        it = self.items[self.i % len(self.items)]
        self.i += 1
        return it


class WStream:
    def __init__(self, k, bufs, kw=None):
        self.k = k
        self.bufs = bufs
        self.q = []
        self.issued = 0
        self.used = 0
        self.kw = kw or {}
        self.srcs = []

    def add(self, src):
        self.srcs.append(src)

    def pump(self):
        while self.issued < len(self.srcs) and self.issued - self.used < len(self.bufs):
            t, r = self.bufs[self.issued % len(self.bufs)]
            src = self.srcs[self.issued]
            self.k.dma('pool', t[:], src, writes=[r], **self.kw)
            self.issued += 1

    def get(self):
        self.pump()
        assert self.used < self.issued
        it = self.bufs[self.used % len(self.bufs)]
        self.used += 1
        return it


MAGIC = 12582912.0
C1 = 6.28125
C2 = TWO_PI - 6.28125
PI_LO = 3.1415925


def build_program(S, stages):
    nc = bass.Bass("TRN2", target_bir_lowering=False)
    k = KB(nc)
    TT = min(1024, S)
    NSUB = TT // 512
    NTILE = S // TT
    NB512 = S // 512

    def din(name, shape, dt=F32):
        return nc.dram_tensor(name, list(shape), dt, kind="ExternalInput").ap()

    xT_in = din("xT", [8, 128, S])
    c_in = din("c_l", [128, 8])
    adaw = din("ada_w", [DEPTH, D, 9 * D])
    adab = din("ada_b_l", [128, DEPTH, 72])
    normg = din("norm_g_l", [128, DEPTH, 3, 8])
    ffw = din("ffw", [DEPTH, 2, NF, 128, 3072])
    woutm = din("wout_l", [DEPTH, 128, 8, D])
    a_win = din("a_win_l", [2, 4, 2, 128, 8, 512])
    a_lbf = din("a_lb_fm", [128, 2, 8])
    a_lbr = din("a_lb_row", [2, D])
    a_on = din("a_onorm_l", [128, 2])
    b_win = din("b_win_l", [3, 2, 128, 8, 512])
    b_qkg = din("b_qkg_l", [128, 2])
    b_lam = din("b_lam", [1, 256])
    b_sub = din("b_subln_l", [128, 1])
    c_win = din("c_win_l", [3, 2, 128, 8, 512])
    c_cv = din("c_conv_l", [128, 8, 3])
    pos_in = din("pos", [1, S], I32)
    cst = din("consts", [128, 1024])
    xT_out = nc.dram_tensor("xT_out", [8, 128, S], F32, kind="ExternalOutput").ap()
    xs = xT_out
    hs = nc.dram_tensor("hs", [8, 128, S], BF16).ap()
    os_ = nc.dram_tensor("os", [8, 128, S], BF16).ap()
    R_xs = [Res() for _ in range(NTILE)]
    R_hs = [Res() for _ in range(NB512)]
    R_os = [[Res() for _ in range(NB512)] for _ in range(2)]
    rp = nc.dram_tensor("rp", [128, NB512, 2, 512], F32).ap()
    R_rp = [Res() for _ in range(NB512)]

    cf = k.sb('cf', [128, 1024], F32); R_cf = Res()
    k.dma('sp', cf[:], cst[:, :], writes=[R_cf])
    cb = k.sb('cb', [128, 1024], BF16); R_cb = Res()
    k.op('dve', lambda e: e.tensor_copy(cb[:], cf[:]), reads=[R_cf], writes=[R_cb])
    ones_bf = cb[:, 512:640]
    bones_bf = cb[:, 640:768]
    modp = k.sb('modp', [128, DEPTH, 3, 3, 8], F32); R_mod = Res()
    epsb = k.sb('epsb', [128, 1], F32)
    k.op('dve', lambda e: e.memset(epsb[:], EPS), writes=[R_cf])

    PS = [k.ps('ps%d' % i, [128, 512]) for i in range(8)]
    RPS = [Res() for _ in range(8)]
    psA = Rot([(PS[i], RPS[i]) for i in range(4)])
    psB = Rot([(PS[i], RPS[i]) for i in range(4, 6)])
    psC = Rot([(PS[i], RPS[i]) for i in range(6, 8)])
    TMP = [(k.sb('tmp%d' % i, [128, 512], F32), Res()) for i in range(6)]
    tmp = Rot(TMP)
    rsp = Rot([(k.sb('rsp%d' % i, [128, 512], F32), Res()) for i in range(2)])

    def rstd_from_ss(ss_ps, R_ss, n, width=512):
        t, r = rsp.next()
        k.op('act', lambda e: e.activation(t[:, :width], ss_ps, AF.Ln, bias=epsb[:, 0:1], scale=1.0 / n), reads=[R_ss, R_cf], writes=[r])
        k.op('act', lambda e: e.activation(t[:, :width], t[:, :width], AF.Exp, scale=-0.5), reads=[r], writes=[r])
        return t, r

    def sig_to(dst, src_ap, R_src, R_dst, extra_reads=()):
        k.op('act', lambda e: e.activation(dst, src_ap, AF.Exp, scale=-1.0), reads=[R_src] + list(extra_reads), writes=[R_dst])
        k.op('act', lambda e: e.activation(dst, dst, AF.Identity, bias=cf[:, 512:513], scale=1.0), reads=[R_dst, R_cf], writes=[R_dst])
        k.op('dve', lambda e: e.reciprocal(dst, dst), reads=[R_dst], writes=[R_dst])

    def stage_alloc():
        es = contextlib.ExitStack()

        def sb(name, shape, dt):
            k.uid += 1
            return es.enter_context(nc.sbuf_tensor('%s_%d' % (name, k.uid), list(shape), dt))
        return es, sb

    def prologue():
        es, sb = stage_alloc()
        with es:
            ct = sb('ct', [128, 8], F32); R_ct = Res()
            k.dma('sp', ct[:], c_in[:, :], writes=[R_ct])
            cs_ = sb('cs_', [128, 8], F32)
            sig_to(cs_[:], ct[:], R_ct, R_ct)
            k.op('dve', lambda e: e.tensor_tensor(ct[:], ct[:], cs_[:], ALU.mult), reads=[R_ct], writes=[R_ct])
            ab = sb('ab', [128, DEPTH, 72], F32); R_ab = Res()
            k.dma('sp', ab[:], adab[:, :, :], writes=[R_ab])
            ng = sb('ng', [128, DEPTH, 3, 8], F32); R_ng = Res()
            k.dma('sp', ng[:], normg[:, :, :, :], writes=[R_ng])
            CW = 1152
            awb = Rot([(sb('awb%d' % i, [128, 8, CW], F32), Res()) for i in range(2)])
            ncc = CW // 128
            for l in range(DEPTH):
                for g in range(9 * D // CW):
                    t, r = awb.next()
                    src = adaw[l, :, g * CW:(g + 1) * CW].rearrange("(k p) c -> p k c", p=128)
                    k.dma('sp', t[:], src, writes=[r])
                    pt, pr = psC.next()
                    for cc in range(ncc):
                        k.mm(pt[:, cc:cc + 1], [(t[:, kc, cc * 128:(cc + 1) * 128], ct[:, kc:kc + 1]) for kc in range(8)],
                             reads=[r, R_ct], writes=[pr])
                    k.op('dve', lambda e: e.tensor_tensor(ab[:, l, g * ncc:(g + 1) * ncc], pt[:, 0:ncc], ab[:, l, g * ncc:(g + 1) * ncc], ALU.add),
                         reads=[pr, R_ab], writes=[R_ab])
            for l in range(DEPTH):
                for j in range(3):
                    base = j * 24
                    k.op('dve', lambda e: e.tensor_copy(modp[:, l, j, 0, :], ab[:, l, base:base + 8]), reads=[R_ab], writes=[R_mod])
                    k.op('dve', lambda e: e.scalar_tensor_tensor(modp[:, l, j, 1, :], ab[:, l, base + 8:base + 16], 1.0, ng[:, l, j, :], ALU.add, ALU.mult),
                         reads=[R_ab, R_ng], writes=[R_mod])
                    cj = 1.0 if j == 1 else 0.5
                    k.op('dve', lambda e: e.tensor_scalar(modp[:, l, j, 2, :], ab[:, l, base + 16:base + 24], 1.0, cj, ALU.add, ALU.mult),
                         reads=[R_ab], writes=[R_mod])
            k.barrier()

    def tok_stage(src, l_out, ffns, prenorm_l, dst):
        es, sb = stage_alloc()
        with es:
            xt = sb('xt', [128, 8, TT], F32); R_xt = [[Res() for _ in range(NSUB)] for _ in range(8)]
            hb = sb('hb', [128, 8, TT], BF16); R_hb = [Res() for _ in range(NSUB)]
            act = sb('actT', [128, NF, TT], BF16); R_act = [[Res() for _ in range(NSUB)] for _ in range(NF)]
            wo_sb = sb('wo_sb', [128, NF, D], BF16); R_wo = [Res() for _ in range(NF)]
            wi_bufs = [(sb('wi%d' % i, [128, 2048], BF16), Res()) for i in range(6)]
            wm = sb('wm', [128, 8, D], BF16); R_wm = Res()
            allx = [R_xt[dc][s] for dc in range(8) for s in range(NSUB)]

            def norm_to_hb(l, j, sub):
                sl = slice(sub * 512, (sub + 1) * 512)
                for dc in range(8):
                    k.op('act', lambda e: e.activation(act[:, dc, sl], xt[:, dc, sl], AF.Square), reads=[R_xt[dc][sub]], writes=[R_act[dc][sub]])
                pt, pr = psC.next()
                k.mm(pt[:], [(ones_bf, act[:, dc, sl]) for dc in range(8)], reads=[R_cb] + [R_act[dc][sub] for dc in range(8)], writes=[pr])
                rt, rr = rstd_from_ss(pt[:], pr, float(D))
                for dc in range(8):
                    t, r = tmp.next()
                    k.op('dve', lambda e: e.scalar_tensor_tensor(t[:], xt[:, dc, sl], modp[:, l, j, 1, dc:dc + 1], rt[:], ALU.mult, ALU.mult),
                         reads=[R_xt[dc][sub], R_mod, rr], writes=[r])
                    k.op('act', lambda e: e.activation(hb[:, dc, sl], t[:], AF.Identity, bias=modp[:, l, j, 0, dc:dc + 1], scale=1.0),
                         reads=[r, R_mod], writes=[R_hb[sub]])

            def resid_update(l, j, dc, sub, pt, pr):
                sl = slice(sub * 512, (sub + 1) * 512)
                k.op('dve', lambda e: e.scalar_tensor_tensor(xt[:, dc, sl], pt[:], modp[:, l, j, 2, dc:dc + 1], xt[:, dc, sl], ALU.mult, ALU.add),
                     reads=[pr, R_mod, R_xt[dc][sub]], writes=[R_xt[dc][sub]])

            ws = WStream(k, wi_bufs, kw=dict(max_dma_last_dim=8192))
            for ti in range(NTILE):
                for (l, j) in ffns:
                    for f in range(NF):
                        ws.add(ffw[l, j // 2, f, :, 0:2048])
            for ti in range(NTILE):
                t0 = ti * TT
                k.dma('sp', xt[:], src[:, :, t0:t0 + TT].rearrange("k p t -> p k t"), reads=[R_xs[ti]], writes=allx)
                if l_out is not None:
                    if ti == 0:
                        k.dma('pool', wm[:], woutm[l_out], writes=[R_wm], max_dma_last_dim=8192)
                    k.dma('sp', hb[:], os_[:, :, t0:t0 + TT].rearrange("k p t -> p k t"),
                          reads=[R_os[g][t0 // 512 + s] for s in range(NSUB) for g in range(2)], writes=R_hb)
                    for dc in range(8):
                        for sub in range(NSUB):
                            sl = slice(sub * 512, (sub + 1) * 512)
                            pt, pr = psB.next()
                            k.mm(pt[:], [(wm[:, kc, dc * 128:(dc + 1) * 128], hb[:, kc, sl]) for kc in range(8)], reads=[R_wm, R_hb[sub]], writes=[pr])
                            resid_update(l_out, 1, dc, sub, pt, pr)
                for (l, j) in ffns:
                    for sub in range(NSUB):
                        norm_to_hb(l, j, sub)
                    for f in range(NF):
                        wt, wr = ws.get()
                        k.dma('pool', wo_sb[:, f, :], ffw[l, j // 2, f, :, 2048:3072], writes=[R_wo[f]], max_dma_last_dim=8192)
                        for sub in range(NSUB):
                            sl = slice(sub * 512, (sub + 1) * 512)
                            pg, prg = psA.next()
                            pu, pru = psA.next()
                            k.mm(pg[:], [(wt[:, kc * 128:(kc + 1) * 128], hb[:, kc, sl]) for kc in range(8)], reads=[wr, R_hb[sub]], writes=[prg])
                            k.mm(pu[:], [(wt[:, 1024 + kc * 128:1024 + (kc + 1) * 128], hb[:, kc, sl]) for kc in range(8)], reads=[wr, R_hb[sub]], writes=[pru])
                            t, r = tmp.next()
                            sig_to(t[:], pg[:], prg, r)
                            k.op('dve', lambda e: e.tensor_tensor(t[:], t[:], pg[:], ALU.mult), reads=[r, prg], writes=[r])
                            k.op('dve', lambda e: e.tensor_tensor(act[:, f, sl], t[:], pu[:], ALU.mult), reads=[r, pru], writes=[R_act[f][sub]])
                        ws.pump()
                    for dc in range(8):
                        for sub in range(NSUB):
                            sl = slice(sub * 512, (sub + 1) * 512)
                            pt, pr = psB.next()
                            k.mm(pt[:], [(wo_sb[:, f, dc * 128:(dc + 1) * 128], act[:, f, sl]) for f in range(NF)],
                                 reads=R_wo + [R_act[f][sub] for f in range(NF)], writes=[pr])
                            resid_update(l, j, dc, sub, pt, pr)
                if prenorm_l is not None:
                    for sub in range(NSUB):
                        norm_to_hb(prenorm_l, 1, sub)
                    k.dma('sp', hs[:, :, t0:t0 + TT].rearrange("k p t -> p k t"), hb[:], reads=R_hb, writes=[R_hs[t0 // 512 + s] for s in range(NSUB)])
                k.dma('sp', dst[:, :, t0:t0 + TT].rearrange("k p t -> p k t"), xt[:], reads=allx, writes=[R_xs[ti]])
            k.barrier()

    def load_w(sb, srcs):
        out = []
        for i, s_ in enumerate(srcs):
            t = sb('w%d' % i, [128, 8, 512], BF16); r = Res()
            k.dma('pool', t[:], s_, writes=[r], max_dma_last_dim=8192)
            out.append((t, r))
        return out

    def mix_conv(l):
        for hg in range(2):
            es, sb = stage_alloc()
            with es:
                (wbg, rbg), (wcg, rcg), (wu, ru) = load_w(sb, [c_win[i, hg] for i in range(3)])
                cv = sb('cv', [128, 8, 3], F32); R_cv = Res()
                k.dma('sp', cv[:], c_cv[:, :, :], writes=[R_cv])
                up = sb('up', [128, 4, 514], F32); R_up = [Res() for _ in range(4)]
                k.op('dve', lambda e: e.memset(up[:], 0.0), writes=R_up)
                hts = Rot([(sb('ht%d' % i, [128, 8, 512], BF16), Res()) for i in range(2)])
                ots = Rot([(sb('ot%d' % i, [128, 4, 512], BF16), Res()) for i in range(2)])
                for ti in range(NB512):
                    t0 = ti * 512
                    ht, rh = hts.next()
                    k.dma('sp', ht[:], hs[:, :, t0:t0 + 512].rearrange("k p t -> p k t"), reads=[R_hs[ti]], writes=[rh])
                    ot, ro = ots.next()
                    for fc in range(4):
                        gfc = hg * 4 + fc
                        cs = slice(fc * 128, (fc + 1) * 128)
                        pb, prb = psA.next(); pc, prc = psA.next(); pu, pru = psA.next()
                        k.mm(pb[:], [(wbg[:, kc, cs], ht[:, kc, :]) for kc in range(8)], reads=[rbg, rh], writes=[prb])
                        k.mm(pc[:], [(wcg[:, kc, cs], ht[:, kc, :]) for kc in range(8)], reads=[rcg, rh], writes=[prc])
                        k.mm(pu[:], [(wu[:, kc, cs], ht[:, kc, :]) for kc in range(8)], reads=[ru, rh], writes=[pru])
                        t1, r1 = tmp.next()
                        k.op('act', lambda e: e.activation(t1[:], pc[:], AF.Identity), reads=[prc], writes=[r1])
                        k.op('dve', lambda e: e.tensor_tensor(up[:, fc, 2:514], t1[:], pu[:], ALU.mult), reads=[r1, pru], writes=[R_up[fc]])
                        t2, r2 = tmp.next()
                        k.op('dve', lambda e: e.tensor_scalar(t2[:], up[:, fc, 2:514], cv[:, gfc, 2:3], None, ALU.mult), reads=[R_up[fc], R_cv], writes=[r2])
                        k.op('dve', lambda e: e.scalar_tensor_tensor(t2[:], up[:, fc, 1:513], cv[:, gfc, 1:2], t2[:], ALU.mult, ALU.add), reads=[R_up[fc], R_cv, r2], writes=[r2])
                        k.op('dve', lambda e: e.scalar_tensor_tensor(t2[:], up[:, fc, 0:512], cv[:, gfc, 0:1], t2[:], ALU.mult, ALU.add), reads=[R_up[fc], R_cv, r2], writes=[r2])
                        k.op('dve', lambda e: e.tensor_tensor(ot[:, fc, :], t2[:], pb[:], ALU.mult), reads=[r2, prb], writes=[ro])
                        k.op('act', lambda e: e.activation(up[:, fc, 0:2], up[:, fc, 512:514], AF.Identity), reads=[R_up[fc]], writes=[R_up[fc]])
                    k.dma('sp', os_[hg * 4:(hg + 1) * 4, :, t0:t0 + 512].rearrange("k p t -> p k t"), ot[:], reads=[ro], writes=[R_os[hg][ti]])
                k.barrier()

    def rope_stage():
        es, sb = stage_alloc()
        with es:
            tl = Rot([[(sb('posi%d' % i, [128, 512], I32), Res()), (sb('cs%d' % i, [128, 2, 512], F32), Res())] for i in range(2)])
            for ti in range(NB512):
                (pi_, rpi), (cs2, rcs) = tl.next()
                rope_tables([(pi_, rpi), (cs2[:, 0, :], rcs), (cs2[:, 1, :], rcs)], ti * 512)
                k.dma('sp', rp[:, ti, :, :], cs2[:], reads=[rcs], writes=[R_rp[ti]])
            k.barrier()

    def rope_tables(sb_tiles, t0):
        (pi_t, R_pi), (ct_t, R_c), (st_t, R_s) = sb_tiles
        k.dma('sp', pi_t[:], pos_in[0:1, t0:t0 + 512].partition_broadcast(128), writes=[R_pi])
        ang, ra = tmp.next()
        k.op('dve', lambda e: e.tensor_copy(ang[:], pi_t[:]), reads=[R_pi], writes=[ra])
        k.op('dve', lambda e: e.tensor_scalar(ang[:], ang[:], cf[:, 768:769], None, ALU.mult), reads=[ra, R_cf], writes=[ra])
        for which, (dst, rd) in enumerate([(st_t, R_s), (ct_t, R_c)]):
            a2, r2 = tmp.next()
            n_, rn = tmp.next()
            off = 0.0 if which == 0 else 0.5 * math.pi
            k.op('dve', lambda e: e.tensor_scalar(a2[:], ang[:], off, None, ALU.add), reads=[ra], writes=[r2])
            k.op('dve', lambda e: e.tensor_scalar(n_[:], a2[:], 1.0 / TWO_PI, MAGIC, ALU.mult, ALU.add), reads=[r2], writes=[rn])
            k.op('dve', lambda e: e.tensor_scalar(n_[:], n_[:], -MAGIC, None, ALU.add), reads=[rn], writes=[rn])
            k.op('dve', lambda e: e.scalar_tensor_tensor(a2[:], n_[:], -C1, a2[:], ALU.mult, ALU.add), reads=[rn, r2], writes=[r2])
            k.op('dve', lambda e: e.scalar_tensor_tensor(a2[:], n_[:], -C2, a2[:], ALU.mult, ALU.add), reads=[rn, r2], writes=[r2])
            k.op('dve', lambda e: e.tensor_scalar(a2[:], a2[:], -PI_LO, PI_LO, ALU.max, ALU.min), reads=[r2], writes=[r2])
            k.op('act', lambda e: e.activation(dst, a2[:], AF.Sin), reads=[r2], writes=[rd])
        k.op('dve', lambda e: e.tensor_scalar(st_t, st_t, cf[:, 769:770], None, ALU.mult), reads=[R_s, R_cf], writes=[R_s])

    def mix_attn(l):
        lambda_init = 0.8 - 0.6 * math.exp(-0.3 * l)
        scale = 64 ** -0.5
        NBLK = S // 128
        for hg in range(2):
            es, sb = stage_alloc()
            with es:
                (wq, rq), (wk, rk), (wv, rv) = load_w(sb, [b_win[i, hg] for i in range(3)])
                sm = sb('sm', [128, 8], F32); R_sm = Res()
                k.dma('sp', sm[:, 0:2], b_qkg[:, :], writes=[R_sm])
                k.dma('sp', sm[:, 2:3], b_sub[:, :], writes=[R_sm])
                k.op('dve', lambda e: e.tensor_scalar(sm[:, 2:3], sm[:, 2:3], 1.0 - lambda_init, None, ALU.mult), reads=[R_sm], writes=[R_sm])
                lm = sb('lm', [1, 256], F32); R_lm = Res()
                k.dma('sp', lm[:], b_lam[:, :], writes=[R_lm])
                l2 = sb('l2', [1, 8], F32)
                k.op('dve', lambda e: e.tensor_tensor(lm[:, 0:64], lm[:, 0:64], lm[:, 64:128], ALU.mult), reads=[R_lm], writes=[R_lm])
                k.op('dve', lambda e: e.tensor_tensor(lm[:, 128:192], lm[:, 128:192], lm[:, 192:256], ALU.mult), reads=[R_lm], writes=[R_lm])
                k.op('dve', lambda e: e.reduce_sum(l2[:, 0:1], lm[:, 0:64], mybir.AxisListType.X), reads=[R_lm], writes=[R_lm])
                k.op('dve', lambda e: e.reduce_sum(l2[:, 1:2], lm[:, 128:192], mybir.AxisListType.X), reads=[R_lm], writes=[R_lm])
                k.op('act', lambda e: e.activation(l2[:, 0:2], l2[:, 0:2], AF.Exp), reads=[R_lm], writes=[R_lm])
                k.op('dve', lambda e: e.tensor_tensor(l2[:, 2:3], l2[:, 1:2], l2[:, 0:1], ALU.subtract), reads=[R_lm], writes=[R_lm])
                k.op('dve', lambda e: e.tensor_scalar(l2[:, 2:3], l2[:, 2:3], -lambda_init, None, ALU.add), reads=[R_lm], writes=[R_lm])
                pt, pr = psC.next()
                k.mm(pt[:, 0:1], [(cf[0:1, 512:640], l2[0:1, 2:3])], reads=[R_cf, R_lm], writes=[pr])
                k.op('dve', lambda e: e.tensor_copy(sm[:, 3:4], pt[:, 0:1]), reads=[pr], writes=[R_sm])

                kT = sb('kT', [128, 4, S], BF16); R_kT = [Res() for _ in range(NB512)]
                vt = sb('vt', [128, NBLK, 512], BF16); R_vt = [Res() for _ in range(NB512)]
                qt = sb('qt', [128, 4, 2, 512], BF16); R_qt = [Res() for _ in range(4)]
                k.op('dve', lambda e: e.memset(qt[:], 0.0), writes=R_qt)
                hts = Rot([(sb('ht%d' % i, [128, 8, 512], BF16), Res()) for i in range(1)])
                ots = Rot([(sb('ot%d' % i, [128, 4, 512], BF16), Res()) for i in range(1)])
                ebuf = Rot([(sb('e%d' % i, [128, 512], BF16), Res()) for i in range(4)])
                spool = Rot([(PS[i], RPS[i]) for i in range(4, 8)])
                sqb = Rot([(sb('sq%d' % i, [128, 512], BF16), Res()) for i in range(2)])
                cs2 = sb('cs2', [128, 2, 512], F32); R_c = Res(); R_s = R_c
                cT = cs2[:, 0, :]; sT = cs2[:, 1, :]
                for ti in range(NB512):
                    t0 = ti * 512
                    ht, rh = hts.next()
                    k.dma('sp', ht[:], hs[:, :, t0:t0 + 512].rearrange("k p t -> p k t"), reads=[R_hs[ti]], writes=[rh])
                    k.dma('sp', cs2[:], rp[:, ti, :, :], reads=[R_rp[ti]], writes=[R_c])
                    for blk in range(4):
                        pv, prv = psC.next()
                        k.mm(pv[:], [(ht[:, kc, blk * 128:(blk + 1) * 128], wv[:, kc, :]) for kc in range(8)], reads=[rh, rv], writes=[prv])
                        k.op('act', lambda e: e.activation(vt[:, ti * 4 + blk, :], pv[:], AF.Identity), reads=[prv], writes=[R_vt[ti]])
                    for h in range(4):
                        cs = slice(h * 128, (h + 1) * 128)
                        for which, (w_, rw_) in enumerate([(wq, rq), (wk, rk)]):
                            pp, prp = psC.next()
                            k.mm(pp[:], [(w_[:, kc, cs], ht[:, kc, :]) for kc in range(8)], reads=[rw_, rh], writes=[prp])
                            sq, rsq = sqb.next()
                            k.op('act', lambda e: e.activation(sq[:], pp[:], AF.Square), reads=[prp], writes=[rsq])
                            pss, prs = psC.next()
                            k.mm(pss[:], [(bones_bf, sq[:])], reads=[R_cb, rsq], writes=[prs])
                            rt, rr = rstd_from_ss(pss[:], prs, 64.0)
                            t1, r1 = tmp.next()
                            k.op('dve', lambda e: e.scalar_tensor_tensor(t1[:], pp[:], sm[:, which:which + 1], rt[:], ALU.mult, ALU.mult), reads=[prp, R_sm, rr], writes=[r1])
                            pq, prq = psC.next()
                            k.mm(pq[:], [(cf[:, 256:384], t1[:])], reads=[R_cf, r1], writes=[prq])
                            t2, r2 = tmp.next()
                            k.op('dve', lambda e: e.tensor_tensor(t2[:], pq[:], sT, ALU.mult), reads=[prq, R_s], writes=[r2])
                            k.op('dve', lambda e: e.tensor_tensor(t1[:], t1[:], cT, ALU.mult), reads=[r1, R_c], writes=[r1])
                            if which == 0:
                                k.op('dve', lambda e: e.tensor_tensor(qt[0:64, h, 0, :], t1[0:64, :], t2[0:64, :], ALU.add), reads=[r1, r2], writes=[R_qt[h]])
                                k.op('dve', lambda e: e.tensor_tensor(qt[64:128, h, 1, :], t1[64:128, :], t2[64:128, :], ALU.add), reads=[r1, r2], writes=[R_qt[h]])
                            else:
                                k.op('dve', lambda e: e.tensor_tensor(kT[:, h, t0:t0 + 512], t1[:], t2[:], ALU.add), reads=[r1, r2], writes=[R_kT[ti]])
                    ot, ro = ots.next()
                    for h in range(4):
                        acc = [psA.next() for _ in range(4)]
                        nkb = ti * 4 + 4
                        units = [(kb, c) for kb in range(nkb) for c in range(2)]
                        pend = {}

                        def emit_s(u):
                            kb, c = u
                            j = kb - ti * 4
                            n0 = max(0, j) * 128
                            ps_, prs_ = spool.next()
                            k.mm(ps_[:, n0:512], [(kT[:, h, kb * 128:(kb + 1) * 128], qt[:, h, c, n0:512])],
                                 reads=[R_kT[kb // 4], R_qt[h]], writes=[prs_])
                            eb, re_ = ebuf.next()
                            k.op('act', lambda e: e.activation(eb[:, n0:512], ps_[:, n0:512], AF.Exp, scale=scale), reads=[prs_], writes=[re_])
                            if j >= 0:
                                k.op('pool', lambda e: e.tensor_tensor(eb[:, n0:n0 + 128], eb[:, n0:n0 + 128], cb[:, 384:512], ALU.mult), reads=[re_, R_cb], writes=[re_])
                            pend[u] = (eb, re_, n0)

                        def emit_pv(u):
                            kb, c = u
                            eb, re_, n0 = pend.pop(u)
                            (po, pro), (pl, prl) = acc[2 * c], acc[2 * c + 1]
                            k.mm(po[:, n0:512], [(vt[:, kb, h * 128:(h + 1) * 128], eb[:, n0:512])], reads=[R_vt[kb // 4], re_], writes=[pro],
                                 start=(kb == 0), stop=(kb == nkb - 1))
                            k.mm(pl[:, n0:512], [(ones_bf, eb[:, n0:512])], reads=[R_cb, re_], writes=[prl], start=(kb == 0), stop=(kb == nkb - 1))

                        LOOK = 2
                        for i in range(len(units) + LOOK):
                            if i < len(units):
                                emit_s(units[i])
                            if i >= LOOK:
                                emit_pv(units[i - LOOK])
                        (po1, pro1), (pl1, prl1), (po2, pro2), (pl2, prl2) = acc
                        ra_, rra = tmp.next(); rb_, rrb = tmp.next()
                        k.op('dve', lambda e: e.reciprocal(ra_[:], pl1[:]), reads=[prl1], writes=[rra])
                        k.op('dve', lambda e: e.reciprocal(rb_[:], pl2[:]), reads=[prl2], writes=[rrb])
                        k.op('dve', lambda e: e.tensor_tensor(ra_[:], po1[:], ra_[:], ALU.mult), reads=[pro1, rra], writes=[rra])
                        k.op('dve', lambda e: e.scalar_tensor_tensor(rb_[:], rb_[:], sm[:, 3:4], po2[:], ALU.mult, ALU.mult), reads=[pro2, rrb, R_sm], writes=[rrb])
                        k.op('dve', lambda e: e.tensor_tensor(ra_[:], ra_[:], rb_[:], ALU.add), reads=[rra, rrb], writes=[rra])
                        sq, rsq = sqb.next()
                        k.op('act', lambda e: e.activation(sq[:], ra_[:], AF.Square), reads=[rra], writes=[rsq])
                        pss, prs = psC.next()
                        k.mm(pss[:], [(ones_bf, sq[:])], reads=[R_cb, rsq], writes=[prs])
                        rt, rr = rstd_from_ss(pss[:], prs, 128.0)
                        k.op('dve', lambda e: e.scalar_tensor_tensor(ot[:, h, :], ra_[:], sm[:, 2:3], rt[:], ALU.mult, ALU.mult), reads=[rra, R_sm, rr], writes=[ro])
                    k.dma('sp', os_[hg * 4:(hg + 1) * 4, :, t0:t0 + 512].rearrange("k p t -> p k t"), ot[:], reads=[ro], writes=[R_os[hg][ti]])
                k.barrier()

    def mix_hgrn(l):
        idx = l // 3
        for hg in range(2):
            es, sb = stage_alloc()
            with es:
                (wq, rq), (wf, rf), (wi_, ri), (wg, rg) = load_w(sb, [a_win[idx, i, hg] for i in range(4)])
                lbf = sb('lbf', [128, 2, 8], F32); R_lbf = Res()
                k.dma('sp', lbf[:], a_lbf[:, :, :], writes=[R_lbf])
                lbr = sb('lbr', [128, 2, 512], F32); R_lbr = Res()
                for i in range(2):
                    k.dma('sp', lbr[:, i, :], a_lbr[i:i + 1, hg * 512:(hg + 1) * 512].partition_broadcast(128), writes=[R_lbr])
                k.op('act', lambda e: e.activation(lbf[:], lbf[:], AF.Exp), reads=[R_lbf], writes=[R_lbf])
                k.op('act', lambda e: e.activation(lbr[:], lbr[:], AF.Exp), reads=[R_lbr], writes=[R_lbr])
                lb_f = sb('lb_f', [128, 8], F32); oml_f = sb('oml_f', [128, 8], F32)
                lb_r = sb('lb_r', [128, 512], F32); oml_r = sb('oml_r', [128, 512], F32)
                for (src_, lb_, oml_, R_) in [(lbf, lb_f, oml_f, R_lbf), (lbr, lb_r, oml_r, R_lbr)]:
                    k.op('dve', lambda e: e.tensor_tensor(oml_[:], src_[:, 0, :], src_[:, 1, :], ALU.add), reads=[R_], writes=[R_])
                    k.op('dve', lambda e: e.reciprocal(oml_[:], oml_[:]), reads=[R_], writes=[R_])
                    if idx == 0:
                        k.op('dve', lambda e: e.tensor_tensor(lb_[:], src_[:, 0, :], src_[:, 0, :], ALU.subtract), reads=[R_], writes=[R_])
                    else:
                        k.op('dve', lambda e: e.tensor_copy(lb_[:], src_[:, 1, :]), reads=[R_], writes=[R_])
                    k.op('dve', lambda e: e.tensor_tensor(lb_[:], lb_[:], oml_[:], ALU.mult), reads=[R_], writes=[R_])
                    k.op('dve', lambda e: e.tensor_scalar(oml_[:], lb_[:], -1.0, 1.0, ALU.mult, ALU.add), reads=[R_], writes=[R_])
                ong = sb('ong', [128, 2], F32); R_on = Res()
                k.dma('sp', ong[:], a_on[:, :], writes=[R_on])
                St = sb('St', [128, 4, 128], F32); R_S = [Res() for _ in range(4)]
                Sb = sb('Sb', [128, 4, 128], BF16); R_Sb = [Res() for _ in range(4)]
                k.op('dve', lambda e: e.memset(St[:], 0.0), writes=R_S)
                k.op('dve', lambda e: e.memset(Sb[:], 0.0), writes=R_Sb)
                hts = Rot([(sb('ht%d' % i, [128, 8, 512], BF16), Res()) for i in range(2)])
                ots = Rot([(sb('ot%d' % i, [128, 4, 512], BF16), Res()) for i in range(2)])
                vtm = sb('vtm', [128, 4, 512], BF16); R_v = [Res() for _ in range(4)]
                khat = sb('khat', [128, 4, 2, 512], BF16); R_kh = [Res() for _ in range(4)]
                k.op('dve', lambda e: e.memset(khat[:], 0.0), writes=R_kh)
                lgf = sb('lgf', [128, 4, 512], F32); R_lg = [Res() for _ in range(4)]
                qf = sb('qf', [128, 4, 512], F32); R_qf = [Res() for _ in range(4)]
                sg = sb('sg', [128, 4, 512], F32); R_sg = [Res() for _ in range(4)]
                e1 = sb('e1', [128, 4, 512], F32); R_e1 = [Res() for _ in range(4)]
                qi = sb('qi', [128, 4, 512], BF16); R_qi = [Res() for _ in range(4)]
                qtl = sb('qtl', [128, 4, 512], BF16); R_qtl = [Res() for _ in range(4)]
                ktl = sb('ktl', [128, 4, 512], BF16); R_ktl = [Res() for _ in range(4)]
                nr = sb('nr', [128, 4, 16], F32); R_nr = [Res() for _ in range(4)]
                of = sb('of', [128, 4, 512], F32); R_of = [Res() for _ in range(4)]
                atm = Rot([(sb('atm%d' % i, [128, 128], BF16), Res()) for i in range(3)])
                sqb = Rot([(sb('sq%d' % i, [128, 512], BF16), Res()) for i in range(2)])
                for ti in range(NB512):
                    t0 = ti * 512
                    ht, rh = hts.next()
                    k.dma('sp', ht[:], hs[:, :, t0:t0 + 512].rearrange("k p t -> p k t"), reads=[R_hs[ti]], writes=[rh])
                    for blk in range(4):
                        bs = slice(blk * 128, (blk + 1) * 128)
                        pv, prv = psC.next()
                        k.mm(pv[:], [(ht[:, kc, bs], wi_[:, kc, :]) for kc in range(8)], reads=[rh, ri], writes=[prv])
                        k.op('act', lambda e: e.activation(vtm[:, blk, :], pv[:], AF.Identity), reads=[prv], writes=[R_v[blk]])
                        pf, prf = psC.next()
                        k.mm(pf[:], [(ht[:, kc, bs], wf[:, kc, :]) for kc in range(8)], reads=[rh, rf], writes=[prf])
                        t1, r1 = tmp.next()
                        sig_to(t1[:], pf[:], prf, r1)
                        k.op('dve', lambda e: e.tensor_tensor(t1[:], t1[:], oml_r[:], ALU.mult), reads=[r1, R_lbr], writes=[r1])
                        k.op('dve', lambda e: e.tensor_tensor(t1[:], t1[:], lb_r[:], ALU.add), reads=[r1, R_lbr], writes=[r1])
                        k.op('act', lambda e: e.activation(lgf[:, blk, :], t1[:], AF.Ln), reads=[r1], writes=[R_lg[blk]])
                        k.op('dve', lambda e: e.tensor_scalar(t1[:], t1[:], -1.0, 1.0, ALU.mult, ALU.add), reads=[r1], writes=[r1])
                        pd, prd = psC.next()
                        k.mm(pd[:], [(cf[:, 128:256], lgf[:, blk, :])], reads=[R_cf, R_lg[blk]], writes=[prd])
                        t2, r2 = tmp.next()
                        k.op('act', lambda e: e.activation(t2[:], pd[:], AF.Exp), reads=[prd], writes=[r2])
                        k.op('dve', lambda e: e.tensor_tensor(khat[0:64, blk, 0, :], t1[0:64, :], t2[0:64, :], ALU.mult), reads=[r1, r2], writes=[R_kh[blk]])
                        k.op('dve', lambda e: e.tensor_tensor(khat[64:128, blk, 1, :], t1[64:128, :], t2[64:128, :], ALU.mult), reads=[r1, r2], writes=[R_kh[blk]])
                    for h in range(4):
                        cs = slice(h * 128, (h + 1) * 128)
                        gh = hg * 4 + h
                        pq, prq = psA.next()
                        k.mm(pq[:], [(wq[:, kc, cs], ht[:, kc, :]) for kc in range(8)], reads=[rq, rh], writes=[prq])
                        sig_to(qf[:, h, :], pq[:], prq, R_qf[h])
                        k.op('dve', lambda e: e.tensor_tensor(qf[:, h, :], qf[:, h, :], pq[:], ALU.mult), reads=[R_qf[h], prq], writes=[R_qf[h]])
                        pg, prg = psA.next()
                        k.mm(pg[:], [(wg[:, kc, cs], ht[:, kc, :]) for kc in range(8)], reads=[rg, rh], writes=[prg])
                        sig_to(sg[:, h, :], pg[:], prg, R_sg[h])
                        k.op('dve', lambda e: e.tensor_tensor(sg[:, h, :], sg[:, h, :], pg[:], ALU.mult), reads=[R_sg[h], prg], writes=[R_sg[h]])
                        pf, prf = psA.next()
                        k.mm(pf[:], [(wf[:, kc, cs], ht[:, kc, :]) for kc in range(8)], reads=[rf, rh], writes=[prf])
                        kf, rkf = tmp.next()
                        sig_to(kf[:], pf[:], prf, rkf)
                        k.op('dve', lambda e: e.tensor_scalar(kf[:], kf[:], oml_f[:, gh:gh + 1], lb_f[:, gh:gh + 1], ALU.mult, ALU.add), reads=[rkf, R_lbf], writes=[rkf])
                        k.op('dve', lambda e: e.tensor_scalar(kf[:], kf[:], -1.0, 1.0, ALU.mult, ALU.add), reads=[rkf], writes=[rkf])
                        pb, prb = psA.next()
                        for blk in range(4):
                            k.mm(pb[:, blk * 128:(blk + 1) * 128], [(lgf[:, blk, cs], cf[:, 0:128])], reads=[R_lg[blk], R_cf], writes=[prb])
                        k.op('act', lambda e: e.activation(e1[:, h, :], pb[:], AF.Exp), reads=[prb], writes=[R_e1[h]])
                        b3 = pb[:].rearrange("p (c t) -> p c t", t=64)
                        k.op('dve', lambda e: e.tensor_copy(nr[:, h, 8:16], b3[:, :, 31]), reads=[prb], writes=[R_nr[h]])
                        k.op('dve', lambda e: e.tensor_scalar(nr[:, h, 0:8], nr[:, h, 8:16], -1.0, None, ALU.mult), reads=[R_nr[h]], writes=[R_nr[h]])
                        eq, req = tmp.next(); ek, rek = tmp.next()
                        for c in range(8):
                            c_ = slice(c * 64, (c + 1) * 64)
                            k.op('act', lambda e: e.activation(eq[:, c_], pb[:, c_], AF.Exp, bias=nr[:, h, c:c + 1], scale=1.0), reads=[prb, R_nr[h]], writes=[req])
                            k.op('act', lambda e: e.activation(ek[:, c_], pb[:, c_], AF.Exp, bias=nr[:, h, 8 + c:9 + c], scale=-1.0), reads=[prb, R_nr[h]], writes=[rek])
                        k.op('dve', lambda e: e.tensor_tensor(qtl[:, h, :], qf[:, h, :], eq[:], ALU.mult), reads=[R_qf[h], req], writes=[R_qtl[h]])
                        k.op('dve', lambda e: e.tensor_tensor(ktl[:, h, :], kf[:], ek[:], ALU.mult), reads=[rkf, rek], writes=[R_ktl[h]])
                        k.op('dve', lambda e: e.tensor_tensor(qi[:, h, :], qf[:, h, :], e1[:, h, :], ALU.mult), reads=[R_qf[h], R_e1[h]], writes=[R_qi[h]])
                    for blk in range(4):
                        bs = slice(blk * 128, (blk + 1) * 128)
                        for h in range(4):
                            cs = slice(h * 128, (h + 1) * 128)
                            pa, pra = psC.next()
                            k.mm(pa[:, 0:128], [(ktl[:, h, bs], qtl[:, h, bs])], reads=[R_ktl[h], R_qtl[h]], writes=[pra])
                            am, ram = atm.next()
                            k.op('dve', lambda e: e.tensor_tensor(am[:], pa[:, 0:128], cf[:, 0:128], ALU.mult), reads=[pra, R_cf], writes=[ram])
                            po, pro = psB.next()
                            for cc in range(2):
                                c = blk * 2 + cc
                                c_ = slice(c * 64, (c + 1) * 64)
                                rows = slice(cc * 64, (cc + 1) * 64)
                                k.mm(po[:, cc * 64:(cc + 1) * 64], [(Sb[:, h, :], qi[:, h, c_])], reads=[R_Sb[h], R_qi[h]], writes=[pro],
                                     start=(cc == 0), stop=False)
                                psn, prsn = psA.next()
                                k.mm(psn[:, 0:128], [(khat[:, blk, cc, cs], vtm[:, blk, cs])], reads=[R_kh[blk], R_v[blk]], writes=[prsn])
                                k.op('dve', lambda e: e.scalar_tensor_tensor(St[:, h, :], St[:, h, :], e1[:, h, c * 64 + 63:c * 64 + 64], psn[:, 0:128], ALU.mult, ALU.add),
                                     reads=[R_S[h], R_e1[h], prsn], writes=[R_S[h]])
                                k.op('act', lambda e: e.activation(Sb[:, h, :], St[:, h, :], AF.Identity), reads=[R_S[h]], writes=[R_Sb[h]])
                            k.mm(po[:, 0:128], [(vtm[:, blk, cs], am[:])], reads=[R_v[blk], ram], writes=[pro], start=False, stop=True)
                            k.op('act', lambda e: e.activation(of[:, h, bs], po[:, 0:128], AF.Identity), reads=[pro], writes=[R_of[h]])
                    ot, ro = ots.next()
                    for h in range(4):
                        sq, rsq = sqb.next()
                        k.op('act', lambda e: e.activation(sq[:], of[:, h, :], AF.Square), reads=[R_of[h]], writes=[rsq])
                        pss, prs = psC.next()
                        k.mm(pss[:], [(ones_bf, sq[:])], reads=[R_cb, rsq], writes=[prs])
                        rt, rr = rstd_from_ss(pss[:], prs, 128.0)
                        t1, r1 = tmp.next()
                        k.op('dve', lambda e: e.tensor_tensor(t1[:], of[:, h, :], rt[:], ALU.mult), reads=[R_of[h], rr], writes=[r1])
                        k.op('dve', lambda e: e.scalar_tensor_tensor(ot[:, h, :], t1[:], ong[:, idx:idx + 1], sg[:, h, :], ALU.mult, ALU.mult), reads=[r1, R_on, R_sg[h]], writes=[ro])
                    k.dma('sp', os_[hg * 4:(hg + 1) * 4, :, t0:t0 + 512].rearrange("k p t -> p k t"), ot[:], reads=[ro], writes=[R_os[hg][ti]])
                k.barrier()

    MIX = {0: mix_hgrn, 1: mix_attn, 2: mix_conv}
    for st in stages:
        if st[0] == 'pro':
            prologue()
        elif st[0] == 'rope':
            rope_stage()
        elif st[0] == 'tok':
            _, src, l_out, ffns, pren, dst = st
            tok_stage(xT_in if src == 'in' else xs, l_out, ffns, pren, xs)
        elif st[0] == 'mix':
            MIX[st[1] % 3](st[1])
    k.finish('sp')
    stats = (k.n_instr, k.n_wait)
    k.close()
    return nc, stats


FULL_STAGES = [('rope',), ('pro',), ('tok', 'in', None, [(0, 0)], 0, 'xs')]
for _l in range(DEPTH):
    FULL_STAGES.append(('mix', _l))
    if _l < DEPTH - 1:
        FULL_STAGES.append(('tok', 'xs', _l, [(_l, 2), (_l + 1, 0)], _l + 1, 'xs'))
    else:
        FULL_STAGES.append(('tok', 'xs', _l, [(_l, 2)], None, 'out'))


def make_consts():
    c = np.zeros((128, 1024), np.float32)
    s = np.arange(128)[:, None]; t = np.arange(128)[None, :]
    same = (s // 64) == (t // 64)
    c[:, 0:128] = (same & (s <= t))
    c[:, 128:256] = (same & (s > t))
    P = np.zeros((128, 128), np.float32)
    for m in range(128):
        d = m % 64
        if d < 8:
            P[m, m + 8] = 1.0
        elif d < 16:
            P[m, m - 8] = 1.0
    c[:, 256:384] = P.T
    c[:, 384:512] = (s <= t)
    c[:, 512:640] = 1.0
    c[:, 640:768] = ((s // 64) == (t // 64))
    inv_freq = (500000.0 ** (-np.arange(0, 16, 2, dtype=np.float32) / 16)).astype(np.float32)
    for p in range(128):
        d = p % 64
        if d < 16:
            c[p, 768] = inv_freq[d % 8]
            c[p, 769] = -1.0 if d < 8 else 1.0
    return c


def prep_shared(inp):
    f32 = np.float32
    sh = {}
    sh['ada_w'] = np.ascontiguousarray(inp['ada_w'], f32)
    sh['ada_b_l'] = np.ascontiguousarray(inp['ada_b'].reshape(DEPTH, 72, 128).transpose(2, 0, 1), f32)
    sh['norm_g_l'] = np.ascontiguousarray(inp['norm_g'].reshape(DEPTH, 3, 8, 128).transpose(3, 0, 1, 2), f32)
    wi = inp['ffn_wi'].reshape(DEPTH, 2, 8, 128, 2, NF, 128)
    wi = wi.transpose(0, 1, 5, 3, 4, 2, 6).reshape(DEPTH, 2, NF, 128, 2048)
    wo = inp['ffn_wo'].reshape(DEPTH, 2, NF, 128, D)
    sh['ffw'] = np.ascontiguousarray(np.concatenate([wi, wo], axis=-1), f32)
    wouts = [inp['a_w_out'][0], inp['b_w_out'][0], inp['c_w_out'][0], inp['a_w_out'][1]]
    sh['wout_l'] = np.ascontiguousarray(np.stack([w.reshape(8, 128, D).transpose(1, 0, 2) for w in wouts]), f32)

    def inl(w, nsplit):
        w = w.reshape(8, 128, nsplit, 2, 512)
        return np.ascontiguousarray(w.transpose(2, 3, 1, 0, 4), f32)
    sh['a_win_l'] = np.stack([inl(inp['a_w_in'][i], 4) for i in range(2)])
    sh['b_win_l'] = inl(inp['b_w_in'][0], 3)
    sh['c_win_l'] = inl(inp['c_w_in'][0], 3)
    sh['a_lb_fm'] = np.ascontiguousarray(inp['a_lb'].reshape(2, 8, 128).transpose(2, 0, 1), f32)
    sh['a_lb_row'] = np.ascontiguousarray(inp['a_lb'], f32)
    sh['a_onorm_l'] = np.ascontiguousarray(inp['a_onorm'].T, f32)
    g = inp['b_qk_g'][0]
    sh['b_qkg_l'] = np.ascontiguousarray(np.concatenate([g, g], axis=1).T, f32)
    sh['b_lam'] = np.ascontiguousarray(inp['b_lam'][0].reshape(1, 256), f32)
    sh['b_subln_l'] = np.ascontiguousarray(inp['b_subln'][0].reshape(128, 1), f32)
    sh['c_conv_l'] = np.ascontiguousarray(inp['c_conv'][0].reshape(3, 8, 128).transpose(2, 1, 0), f32)
    sh['consts'] = make_consts()
    return sh


def prep_core(inp, b, S):
    m = {}
    m['xT'] = np.ascontiguousarray(inp['x'][b, :S].T.reshape(8, 128, S), np.float32)
    m['c_l'] = np.ascontiguousarray(inp['c'][b].reshape(8, 128).T, np.float32)
    m['pos'] = np.ascontiguousarray(inp['positions'][b, :S].reshape(1, S), np.int32)
    return m


_CACHE = {}


def kernel(**inputs):
    inp = {k_: np.asarray(v) for k_, v in inputs.items()}
    B, S, _ = inp['x'].shape
    if S not in _CACHE:
        _CACHE[S] = build_program(S, FULL_STAGES)[0]
    nc = _CACHE[S]
    sh = prep_shared(inp)
    in_maps = []
    for b in range(B):
        m = dict(sh)
        m.update(prep_core(inp, b, S))
        in_maps.append(m)
    res = run_bass_kernel_spmd(nc, in_maps, core_ids=list(range(B)))
    out = np.stack([res.results[b]['xT_out'].reshape(D, S).T for b in range(B)])
    return np.ascontiguousarray(out, np.float32)
```

```python
import contextlib
import math
import numpy as np
import ml_dtypes
import concourse.bass as bass
import concourse.mybir as mybir
from concourse.bass_utils import run_bass_kernel_spmd

F32 = mybir.dt.float32
BF16 = mybir.dt.bfloat16
I32 = mybir.dt.int32
AF = mybir.ActivationFunctionType
ALU = mybir.AluOpType

D = 1024
FF = 2816
NF = 22
DEPTH = 4
EPS = 1e-6
TWO_PI = 2.0 * math.pi
CC_INC = 16


class Res:
    __slots__ = ('w', 'r', 'name')

    def __init__(self, name=''):
        self.w = None
        self.r = {}
        self.name = name


class KB:
    NDMA = {'sp': 12, 'pool': 12}

    def __init__(self, nc):
        self.nc = nc
        self.es = contextlib.ExitStack()
        self.engs = {'pe': nc.tensor, 'act': nc.scalar, 'dve': nc.vector, 'pool': nc.gpsimd, 'sp': nc.sync}
        self.sems = {}
        self.cnt = {}
        self.cur = {}
        self.gen = 0
        self.retired = set()
        self._fresh()
        self.dsem = {}
        self.drr = {}
        for q, n in self.NDMA.items():
            self.dsem[q] = []
            self.drr[q] = 0
            for i in range(n):
                nm = 'd_%s%d' % (q, i)
                self.sems[nm] = self.es.enter_context(nc.semaphore(nm))
                self.cnt[nm] = 0
                self.dsem[q].append(nm)
        self.waited = {e: {} for e in self.engs}
        self.n_instr = 0
        self.n_wait = 0
        self.uid = 0

    def _fresh(self):
        for e in ['pe', 'act', 'dve', 'pool']:
            if e in self.cur:
                self.retired.add(self.cur[e])
            nm = '%s@%d' % (e, self.gen)
            self.sems[nm] = self.es.enter_context(self.nc.semaphore('s_%s_%d' % (e, self.gen)))
            self.cnt[nm] = 0
            self.cur[e] = nm
        self.gen += 1

    def sb(self, name, shape, dt):
        return self.es.enter_context(self.nc.sbuf_tensor(name, list(shape), dt))

    def ps(self, name, shape, dt=F32):
        return self.es.enter_context(self.nc.psum_tensor(name, list(shape), dt))

    def _wait(self, eng, dep):
        s, v = dep
        if s in self.retired:
            return
        if eng == 'pe' and s == self.cur['pe']:
            return
        if self.waited[eng].get(s, 0) >= v:
            return
        self.engs[eng].wait_ge(self.sems[s], v)
        self.waited[eng][s] = v
        self.n_wait += 1

    def _deps(self, eng, reads, writes):
        deps = {}
        for r in reads:
            if r.w is not None:
                s, v = r.w
                deps[s] = max(deps.get(s, 0), v)
        for w in writes:
            if w.w is not None:
                s, v = w.w
                deps[s] = max(deps.get(s, 0), v)
            for s, v in w.r.items():
                deps[s] = max(deps.get(s, 0), v)
        for s, v in deps.items():
            self._wait(eng, (s, v))

    def _mark(self, tick, reads, writes):
        s, v = tick
        for r in reads:
            r.r[s] = max(r.r.get(s, 0), v)
        for w in writes:
            w.w = tick
            w.r = {}

    def op(self, eng, fn, reads=(), writes=()):
        self._deps(eng, reads, writes)
        ins = fn(self.engs[eng])
        nm = self.cur[eng]
        ins.then_inc(self.sems[nm], 1)
        self.cnt[nm] += 1
        self.n_instr += 1
        self._mark((nm, self.cnt[nm]), reads, writes)

    def mm(self, out, pairs, reads=(), writes=(), start=True, stop=True):
        self._deps('pe', reads, writes)
        n = len(pairs)
        ins = None
        for i, (l, r) in enumerate(pairs):
            ins = self.nc.tensor.matmul(out, l, r, start=(start and i == 0), stop=(stop and i == n - 1))
            self.n_instr += 1
        nm = self.cur['pe']
        ins.then_inc(self.sems[nm], 1)
        self.cnt[nm] += 1
        self._mark((nm, self.cnt[nm]), reads, writes)

    def dma(self, q, out, in_, reads=(), writes=(), **kw):
        sl = self.dsem[q]
        nm = sl[self.drr[q] % len(sl)]
        self.drr[q] += 1
        if self.cnt[nm] > 0:
            self._wait(q, (nm, self.cnt[nm]))
        self._deps(q, reads, writes)
        self.engs[q].dma_start(out=out, in_=in_, **kw).then_inc(self.sems[nm], 16)
        self.cnt[nm] += 16
        self.n_instr += 1
        self._mark((nm, self.cnt[nm]), reads, writes)

    def coll(self, kind, groups, in_, out, reads=(), writes=()):
        q = 'pool'
        sl = self.dsem[q]
        nm = sl[self.drr[q] % len(sl)]
        self.drr[q] += 1
        if self.cnt[nm] > 0:
            self._wait(q, (nm, self.cnt[nm]))
        self._deps(q, reads, writes)
        self.nc.gpsimd.collective_compute(kind, ALU.bypass, replica_groups=groups, ins=[in_], outs=[out]).then_inc(self.sems[nm], CC_INC)
        self.cnt[nm] += CC_INC
        self.n_instr += 1
        self._mark((nm, self.cnt[nm]), reads, writes)

    def barrier(self, fresh=True):
        for eng in ['pe', 'act', 'dve', 'pool', 'sp']:
            for nm in self.sems:
                if self.cnt[nm] > 0 and nm not in self.retired:
                    if eng == 'pe' and nm == self.cur['pe']:
                        continue
                    self._wait(eng, (nm, self.cnt[nm]))
        if fresh and self.gen < 16:
            self._fresh()

    def finish(self, eng='sp'):
        for nm in self.sems:
            if self.cnt[nm] > 0 and nm not in self.retired:
                self._wait(eng, (nm, self.cnt[nm]))

    def close(self):
        self.es.close()


class Rot:
    def __init__(self, items):
        self.items = items
        self.i = 0

    def next(self):
        it = self.items[self.i % len(self.items)]
        self.i += 1
        return it


class WStream:
    def __init__(self, k, bufs, kw=None):
        self.k = k
        self.bufs = bufs
        self.q = []
        self.issued = 0
        self.used = 0
        self.kw = kw or {}
        self.srcs = []

    def add(self, src):
        self.srcs.append(src)

    def pump(self):
        while self.issued < len(self.srcs) and self.issued - self.used < len(self.bufs):
            t, r = self.bufs[self.issued % len(self.bufs)]
            src = self.srcs[self.issued]
            self.k.dma('pool', t[:], src, writes=[r], **self.kw)
            self.issued += 1

    def get(self):
        self.pump()
        assert self.used < self.issued
        it = self.bufs[self.used % len(self.bufs)]
        self.used += 1
        return it


MAGIC = 12582912.0
C1 = 6.28125
C2 = TWO_PI - 6.28125
PI_LO = 3.1415925


def build_program(S, stages):
    nc = bass.Bass("TRN2", target_bir_lowering=False)
    k = KB(nc)
    TT = min(1024, S)
    NSUB = TT // 512
    NTILE = S // TT
    NB512 = S // 512

    def din(name, shape, dt=F32):
        return nc.dram_tensor(name, list(shape), dt, kind="ExternalInput").ap()

    xT_in = din("xT", [8, 128, S])
    c_in = din("c_l", [128, 8])
    adaw = din("ada_w", [DEPTH, D, 9 * D])
    adab = din("ada_b_l", [128, DEPTH, 72])
    normg = din("norm_g_l", [128, DEPTH, 3, 8])
    ffw = din("ffw", [DEPTH, 2, NF, 128, 3072])
    woutm = din("wout_l", [DEPTH, 128, 8, D])
    a_win = din("a_win_l", [2, 4, 2, 128, 8, 512])
    a_lbf = din("a_lb_fm", [128, 2, 8])
    a_lbr = din("a_lb_row", [2, D])
    a_on = din("a_onorm_l", [128, 2])
    b_win = din("b_win_l", [3, 2, 128, 8, 512])
    b_qkg = din("b_qkg_l", [128, 2])
    b_lam = din("b_lam", [1, 256])
    b_sub = din("b_subln_l", [128, 1])
    c_win = din("c_win_l", [3, 2, 128, 8, 512])
    c_cv = din("c_conv_l", [128, 8, 3])
    pos_in = din("pos", [1, S], I32)
    cst = din("consts", [128, 1024])
    xT_out = nc.dram_tensor("xT_out", [8, 128, S], F32, kind="ExternalOutput").ap()
    xs = xT_out
    hs = nc.dram_tensor("hs", [8, 128, S], BF16).ap()
    os_ = nc.dram_tensor("os", [8, 128, S], BF16).ap()
    R_xs = [Res() for _ in range(NTILE)]
    R_hs = [Res() for _ in range(NB512)]
    R_os = [[Res() for _ in range(NB512)] for _ in range(2)]
    rp = nc.dram_tensor("rp", [128, NB512, 2, 512], F32).ap()
    R_rp = [Res() for _ in range(NB512)]

    cf = k.sb('cf', [128, 1024], F32); R_cf = Res()
    k.dma('sp', cf[:], cst[:, :], writes=[R_cf])
    cb = k.sb('cb', [128, 1024], BF16); R_cb = Res()
    k.op('dve', lambda e: e.tensor_copy(cb[:], cf[:]), reads=[R_cf], writes=[R_cb])
    ones_bf = cb[:, 512:640]
    bones_bf = cb[:, 640:768]
    modp = k.sb('modp', [128, DEPTH, 3, 3, 8], F32); R_mod = Res()
    epsb = k.sb('epsb', [128, 1], F32)
    k.op('dve', lambda e: e.memset(epsb[:], EPS), writes=[R_cf])

    PS = [k.ps('ps%d' % i, [128, 512]) for i in range(8)]
    RPS = [Res() for _ in range(8)]
    psA = Rot([(PS[i], RPS[i]) for i in range(4)])
    psB = Rot([(PS[i], RPS[i]) for i in range(4, 6)])
    psC = Rot([(PS[i], RPS[i]) for i in range(6, 8)])
    TMP = [(k.sb('tmp%d' % i, [128, 512], F32), Res()) for i in range(6)]
    tmp = Rot(TMP)
    rsp = Rot([(k.sb('rsp%d' % i, [128, 512], F32), Res()) for i in range(2)])

    def rstd_from_ss(ss_ps, R_ss, n, width=512):
        t, r = rsp.next()
        k.op('act', lambda e: e.activation(t[:, :width], ss_ps, AF.Ln, bias=epsb[:, 0:1], scale=1.0 / n), reads=[R_ss, R_cf], writes=[r])
        k.op('act', lambda e: e.activation(t[:, :width], t[:, :width], AF.Exp, scale=-0.5), reads=[r], writes=[r])
        return t, r

    def sig_to(dst, src_ap, R_src, R_dst, extra_reads=()):
        k.op('act', lambda e: e.activation(dst, src_ap, AF.Exp, scale=-1.0), reads=[R_src] + list(extra_reads), writes=[R_dst])
        k.op('act', lambda e: e.activation(dst, dst, AF.Identity, bias=cf[:, 512:513], scale=1.0), reads=[R_dst, R_cf], writes=[R_dst])
        k.op('dve', lambda e: e.reciprocal(dst, dst), reads=[R_dst], writes=[R_dst])

    def stage_alloc():
        es = contextlib.ExitStack()

        def sb(name, shape, dt):
            k.uid += 1
            return es.enter_context(nc.sbuf_tensor('%s_%d' % (name, k.uid), list(shape), dt))
        return es, sb

    def prologue():
        es, sb = stage_alloc()
        with es:
            ct = sb('ct', [128, 8], F32); R_ct = Res()
            k.dma('sp', ct[:], c_in[:, :], writes=[R_ct])
            cs_ = sb('cs_', [128, 8], F32)
            sig_to(cs_[:], ct[:], R_ct, R_ct)
            k.op('dve', lambda e: e.tensor_tensor(ct[:], ct[:], cs_[:], ALU.mult), reads=[R_ct], writes=[R_ct])
            ab = sb('ab', [128, DEPTH, 72], F32); R_ab = Res()
            k.dma('sp', ab[:], adab[:, :, :], writes=[R_ab])
            ng = sb('ng', [128, DEPTH, 3, 8], F32); R_ng = Res()
            k.dma('sp', ng[:], normg[:, :, :, :], writes=[R_ng])
            CW = 1152
            awb = Rot([(sb('awb%d' % i, [128, 8, CW], F32), Res()) for i in range(2)])
            ncc = CW // 128
            for l in range(DEPTH):
                for g in range(9 * D // CW):
                    t, r = awb.next()
                    src = adaw[l, :, g * CW:(g + 1) * CW].rearrange("(k p) c -> p k c", p=128)
                    k.dma('sp', t[:], src, writes=[r])
                    pt, pr = psC.next()
                    for cc in range(ncc):
                        k.mm(pt[:, cc:cc + 1], [(t[:, kc, cc * 128:(cc + 1) * 128], ct[:, kc:kc + 1]) for kc in range(8)],
                             reads=[r, R_ct], writes=[pr])
                    k.op('dve', lambda e: e.tensor_tensor(ab[:, l, g * ncc:(g + 1) * ncc], pt[:, 0:ncc], ab[:, l, g * ncc:(g + 1) * ncc], ALU.add),
                         reads=[pr, R_ab], writes=[R_ab])
            for l in range(DEPTH):
                for j in range(3):
                    base = j * 24
                    k.op('dve', lambda e: e.tensor_copy(modp[:, l, j, 0, :], ab[:, l, base:base + 8]), reads=[R_ab], writes=[R_mod])
                    k.op('dve', lambda e: e.scalar_tensor_tensor(modp[:, l, j, 1, :], ab[:, l, base + 8:base + 16], 1.0, ng[:, l, j, :], ALU.add, ALU.mult),
                         reads=[R_ab, R_ng], writes=[R_mod])
                    cj = 1.0 if j == 1 else 0.5
                    k.op('dve', lambda e: e.tensor_scalar(modp[:, l, j, 2, :], ab[:, l, base + 16:base + 24], 1.0, cj, ALU.add, ALU.mult),
                         reads=[R_ab], writes=[R_mod])
            k.barrier()

    def tok_stage(src, l_out, ffns, prenorm_l, dst):
        es, sb = stage_alloc()
        with es:
            xt = sb('xt', [128, 8, TT], F32); R_xt = [[Res() for _ in range(NSUB)] for _ in range(8)]
            hb = sb('hb', [128, 8, TT], BF16); R_hb = [Res() for _ in range(NSUB)]
            act = sb('actT', [128, NF, TT], BF16); R_act = [[Res() for _ in range(NSUB)] for _ in range(NF)]
            wo_sb = sb('wo_sb', [128, NF, D], BF16); R_wo = [Res() for _ in range(NF)]
            wi_bufs = [(sb('wi%d' % i, [128, 2048], BF16), Res()) for i in range(6)]
            wm = sb('wm', [128, 8, D], BF16); R_wm = Res()
            allx = [R_xt[dc][s] for dc in range(8) for s in range(NSUB)]

            def norm_to_hb(l, j, sub):
                sl = slice(sub * 512, (sub + 1) * 512)
                for dc in range(8):
                    k.op('act', lambda e: e.activation(act[:, dc, sl], xt[:, dc, sl], AF.Square), reads=[R_xt[dc][sub]], writes=[R_act[dc][sub]])
                pt, pr = psC.next()
                k.mm(pt[:], [(ones_bf, act[:, dc, sl]) for dc in range(8)], reads=[R_cb] + [R_act[dc][sub] for dc in range(8)], writes=[pr])
                rt, rr = rstd_from_ss(pt[:], pr, float(D))
                for dc in range(8):
                    t, r = tmp.next()
                    k.op('dve', lambda e: e.scalar_tensor_tensor(t[:], xt[:, dc, sl], modp[:, l, j, 1, dc:dc + 1], rt[:], ALU.mult, ALU.mult),
                         reads=[R_xt[dc][sub], R_mod, rr], writes=[r])
                    k.op('act', lambda e: e.activation(hb[:, dc, sl], t[:], AF.Identity, bias=modp[:, l, j, 0, dc:dc + 1], scale=1.0),
                         reads=[r, R_mod], writes=[R_hb[sub]])

            def resid_update(l, j, dc, sub, pt, pr):
                sl = slice(sub * 512, (sub + 1) * 512)
                k.op('dve', lambda e: e.scalar_tensor_tensor(xt[:, dc, sl], pt[:], modp[:, l, j, 2, dc:dc + 1], xt[:, dc, sl], ALU.mult, ALU.add),
                     reads=[pr, R_mod, R_xt[dc][sub]], writes=[R_xt[dc][sub]])

            ws = WStream(k, wi_bufs, kw=dict(max_dma_last_dim=8192))
            for ti in range(NTILE):
                for (l, j) in ffns:
                    for f in range(NF):
                        ws.add(ffw[l, j // 2, f, :, 0:2048])
            for ti in range(NTILE):
                t0 = ti * TT
                k.dma('sp', xt[:], src[:, :, t0:t0 + TT].rearrange("k p t -> p k t"), reads=[R_xs[ti]], writes=allx)
                if l_out is not None:
                    if ti == 0:
                        k.dma('pool', wm[:], woutm[l_out], writes=[R_wm], max_dma_last_dim=8192)
                    k.dma('sp', hb[:], os_[:, :, t0:t0 + TT].rearrange("k p t -> p k t"),
                          reads=[R_os[g][t0 // 512 + s] for s in range(NSUB) for g in range(2)], writes=R_hb)
                    for dc in range(8):
                        for sub in range(NSUB):
                            sl = slice(sub * 512, (sub + 1) * 512)
                            pt, pr = psB.next()
                            k.mm(pt[:], [(wm[:, kc, dc * 128:(dc + 1) * 128], hb[:, kc, sl]) for kc in range(8)], reads=[R_wm, R_hb[sub]], writes=[pr])
                            resid_update(l_out, 1, dc, sub, pt, pr)
                for (l, j) in ffns:
                    for sub in range(NSUB):
                        norm_to_hb(l, j, sub)
                    for f in range(NF):
                        wt, wr = ws.get()
                        k.dma('pool', wo_sb[:, f, :], ffw[l, j // 2, f, :, 2048:3072], writes=[R_wo[f]], max_dma_last_dim=8192)
                        for sub in range(NSUB):
                            sl = slice(sub * 512, (sub + 1) * 512)
                            pg, prg = psA.next()
                            pu, pru = psA.next()
                            k.mm(pg[:], [(wt[:, kc * 128:(kc + 1) * 128], hb[:, kc, sl]) for kc in range(8)], reads=[wr, R_hb[sub]], writes=[prg])
                            k.mm(pu[:], [(wt[:, 1024 + kc * 128:1024 + (kc + 1) * 128], hb[:, kc, sl]) for kc in range(8)], reads=[wr, R_hb[sub]], writes=[pru])
                            t, r = tmp.next()
                            sig_to(t[:], pg[:], prg, r)
                            k.op('dve', lambda e: e.tensor_tensor(t[:], t[:], pg[:], ALU.mult), reads=[r, prg], writes=[r])
                            k.op('dve', lambda e: e.tensor_tensor(act[:, f, sl], t[:], pu[:], ALU.mult), reads=[r, pru], writes=[R_act[f][sub]])
                        ws.pump()
                    for dc in range(8):
                        for sub in range(NSUB):
                            sl = slice(sub * 512, (sub + 1) * 512)
                            pt, pr = psB.next()
                            k.mm(pt[:], [(wo_sb[:, f, dc * 128:(dc + 1) * 128], act[:, f, sl]) for f in range(NF)],
                                 reads=R_wo + [R_act[f][sub] for f in range(NF)], writes=[pr])
                            resid_update(l, j, dc, sub, pt, pr)
                if prenorm_l is not None:
                    for sub in range(NSUB):
                        norm_to_hb(prenorm_l, 1, sub)
                    k.dma('sp', hs[:, :, t0:t0 + TT].rearrange("k p t -> p k t"), hb[:], reads=R_hb, writes=[R_hs[t0 // 512 + s] for s in range(NSUB)])
                k.dma('sp', dst[:, :, t0:t0 + TT].rearrange("k p t -> p k t"), xt[:], reads=allx, writes=[R_xs[ti]])
            k.barrier()

    def load_w(sb, srcs):
        out = []
        for i, s_ in enumerate(srcs):
            t = sb('w%d' % i, [128, 8, 512], BF16); r = Res()
            k.dma('pool', t[:], s_, writes=[r], max_dma_last_dim=8192)
            out.append((t, r))
        return out

    def mix_conv(l):
        for hg in range(2):
            es, sb = stage_alloc()
            with es:
                (wbg, rbg), (wcg, rcg), (wu, ru) = load_w(sb, [c_win[i, hg] for i in range(3)])
                cv = sb('cv', [128, 8, 3], F32); R_cv = Res()
                k.dma('sp', cv[:], c_cv[:, :, :], writes=[R_cv])
                up = sb('up', [128, 4, 514], F32); R_up = [Res() for _ in range(4)]
                k.op('dve', lambda e: e.memset(up[:], 0.0), writes=R_up)
                hts = Rot([(sb('ht%d' % i, [128, 8, 512], BF16), Res()) for i in range(2)])
                ots = Rot([(sb('ot%d' % i, [128, 4, 512], BF16), Res()) for i in range(2)])
                for ti in range(NB512):
                    t0 = ti * 512
                    ht, rh = hts.next()
                    k.dma('sp', ht[:], hs[:, :, t0:t0 + 512].rearrange("k p t -> p k t"), reads=[R_hs[ti]], writes=[rh])
                    ot, ro = ots.next()
                    for fc in range(4):
                        gfc = hg * 4 + fc
                        cs = slice(fc * 128, (fc + 1) * 128)
                        pb, prb = psA.next(); pc, prc = psA.next(); pu, pru = psA.next()
                        k.mm(pb[:], [(wbg[:, kc, cs], ht[:, kc, :]) for kc in range(8)], reads=[rbg, rh], writes=[prb])
                        k.mm(pc[:], [(wcg[:, kc, cs], ht[:, kc, :]) for kc in range(8)], reads=[rcg, rh], writes=[prc])
                        k.mm(pu[:], [(wu[:, kc, cs], ht[:, kc, :]) for kc in range(8)], reads=[ru, rh], writes=[pru])
                        t1, r1 = tmp.next()
                        k.op('act', lambda e: e.activation(t1[:], pc[:], AF.Identity), reads=[prc], writes=[r1])
                        k.op('dve', lambda e: e.tensor_tensor(up[:, fc, 2:514], t1[:], pu[:], ALU.mult), reads=[r1, pru], writes=[R_up[fc]])
                        t2, r2 = tmp.next()
                        k.op('dve', lambda e: e.tensor_scalar(t2[:], up[:, fc, 2:514], cv[:, gfc, 2:3], None, ALU.mult), reads=[R_up[fc], R_cv], writes=[r2])
                        k.op('dve', lambda e: e.scalar_tensor_tensor(t2[:], up[:, fc, 1:513], cv[:, gfc, 1:2], t2[:], ALU.mult, ALU.add), reads=[R_up[fc], R_cv, r2], writes=[r2])
                        k.op('dve', lambda e: e.scalar_tensor_tensor(t2[:], up[:, fc, 0:512], cv[:, gfc, 0:1], t2[:], ALU.mult, ALU.add), reads=[R_up[fc], R_cv, r2], writes=[r2])
                        k.op('dve', lambda e: e.tensor_tensor(ot[:, fc, :], t2[:], pb[:], ALU.mult), reads=[r2, prb], writes=[ro])
                        k.op('act', lambda e: e.activation(up[:, fc, 0:2], up[:, fc, 512:514], AF.Identity), reads=[R_up[fc]], writes=[R_up[fc]])
                    k.dma('sp', os_[hg * 4:(hg + 1) * 4, :, t0:t0 + 512].rearrange("k p t -> p k t"), ot[:], reads=[ro], writes=[R_os[hg][ti]])
                k.barrier()

    def rope_stage():
        es, sb = stage_alloc()
        with es:
            tl = Rot([[(sb('posi%d' % i, [128, 512], I32), Res()), (sb('cs%d' % i, [128, 2, 512], F32), Res())] for i in range(2)])
            for ti in range(NB512):
                (pi_, rpi), (cs2, rcs) = tl.next()
                rope_tables([(pi_, rpi), (cs2[:, 0, :], rcs), (cs2[:, 1, :], rcs)], ti * 512)
                k.dma('sp', rp[:, ti, :, :], cs2[:], reads=[rcs], writes=[R_rp[ti]])
            k.barrier()

    def rope_tables(sb_tiles, t0):
        (pi_t, R_pi), (ct_t, R_c), (st_t, R_s) = sb_tiles
        k.dma('sp', pi_t[:], pos_in[0:1, t0:t0 + 512].partition_broadcast(128), writes=[R_pi])
        ang, ra = tmp.next()
        k.op('dve', lambda e: e.tensor_copy(ang[:], pi_t[:]), reads=[R_pi], writes=[ra])
        k.op('dve', lambda e: e.tensor_scalar(ang[:], ang[:], cf[:, 768:769], None, ALU.mult), reads=[ra, R_cf], writes=[ra])
        for which, (dst, rd) in enumerate([(st_t, R_s), (ct_t, R_c)]):
            a2, r2 = tmp.next()
            n_, rn = tmp.next()
            off = 0.0 if which == 0 else 0.5 * math.pi
            k.op('dve', lambda e: e.tensor_scalar(a2[:], ang[:], off, None, ALU.add), reads=[ra], writes=[r2])
            k.op('dve', lambda e: e.tensor_scalar(n_[:], a2[:], 1.0 / TWO_PI, MAGIC, ALU.mult, ALU.add), reads=[r2], writes=[rn])
            k.op('dve', lambda e: e.tensor_scalar(n_[:], n_[:], -MAGIC, None, ALU.add), reads=[rn], writes=[rn])
            k.op('dve', lambda e: e.scalar_tensor_tensor(a2[:], n_[:], -C1, a2[:], ALU.mult, ALU.add), reads=[rn, r2], writes=[r2])
            k.op('dve', lambda e: e.scalar_tensor_tensor(a2[:], n_[:], -C2, a2[:], ALU.mult, ALU.add), reads=[rn, r2], writes=[r2])
            k.op('dve', lambda e: e.tensor_scalar(a2[:], a2[:], -PI_LO, PI_LO, ALU.max, ALU.min), reads=[r2], writes=[r2])
            k.op('act', lambda e: e.activation(dst, a2[:], AF.Sin), reads=[r2], writes=[rd])
        k.op('dve', lambda e: e.tensor_scalar(st_t, st_t, cf[:, 769:770], None, ALU.mult), reads=[R_s, R_cf], writes=[R_s])

    def mix_attn(l):
        lambda_init = 0.8 - 0.6 * math.exp(-0.3 * l)
        scale = 64 ** -0.5
        NBLK = S // 128
        for hg in range(2):
            es, sb = stage_alloc()
            with es:
                (wq, rq), (wk, rk), (wv, rv) = load_w(sb, [b_win[i, hg] for i in range(3)])
                sm = sb('sm', [128, 8], F32); R_sm = Res()
                k.dma('sp', sm[:, 0:2], b_qkg[:, :], writes=[R_sm])
                k.dma('sp', sm[:, 2:3], b_sub[:, :], writes=[R_sm])
                k.op('dve', lambda e: e.tensor_scalar(sm[:, 2:3], sm[:, 2:3], 1.0 - lambda_init, None, ALU.mult), reads=[R_sm], writes=[R_sm])
                lm = sb('lm', [1, 256], F32); R_lm = Res()
                k.dma('sp', lm[:], b_lam[:, :], writes=[R_lm])
                l2 = sb('l2', [1, 8], F32)
                k.op('dve', lambda e: e.tensor_tensor(lm[:, 0:64], lm[:, 0:64], lm[:, 64:128], ALU.mult), reads=[R_lm], writes=[R_lm])
                k.op('dve', lambda e: e.tensor_tensor(lm[:, 128:192], lm[:, 128:192], lm[:, 192:256], ALU.mult), reads=[R_lm], writes=[R_lm])
                k.op('dve', lambda e: e.reduce_sum(l2[:, 0:1], lm[:, 0:64], mybir.AxisListType.X), reads=[R_lm], writes=[R_lm])
                k.op('dve', lambda e: e.reduce_sum(l2[:, 1:2], lm[:, 128:192], mybir.AxisListType.X), reads=[R_lm], writes=[R_lm])
                k.op('act', lambda e: e.activation(l2[:, 0:2], l2[:, 0:2], AF.Exp), reads=[R_lm], writes=[R_lm])
                k.op('dve', lambda e: e.tensor_tensor(l2[:, 2:3], l2[:, 1:2], l2[:, 0:1], ALU.subtract), reads=[R_lm], writes=[R_lm])
                k.op('dve', lambda e: e.tensor_scalar(l2[:, 2:3], l2[:, 2:3], -lambda_init, None, ALU.add), reads=[R_lm], writes=[R_lm])
                pt, pr = psC.next()
                k.mm(pt[:, 0:1], [(cf[0:1, 512:640], l2[0:1, 2:3])], reads=[R_cf, R_lm], writes=[pr])
                k.op('dve', lambda e: e.tensor_copy(sm[:, 3:4], pt[:, 0:1]), reads=[pr], writes=[R_sm])

                kT = sb('kT', [128, 4, S], BF16); R_kT = [Res() for _ in range(NB512)]
                vt = sb('vt', [128, NBLK, 512], BF16); R_vt = [Res() for _ in range(NB512)]
                qt = sb('qt', [128, 4, 2, 512], BF16); R_qt = [Res() for _ in range(4)]
                k.op('dve', lambda e: e.memset(qt[:], 0.0), writes=R_qt)
                hts = Rot([(sb('ht%d' % i, [128, 8, 512], BF16), Res()) for i in range(1)])
                ots = Rot([(sb('ot%d' % i, [128, 512], BF16), Res()) for i in range(2)])
                ebuf = Rot([(sb('e%d' % i, [128, 512], BF16), Res()) for i in range(3)])
                lacc = sb('lacc', [128, 2, 512], F32); R_la = [Res(), Res()]
                spool = Rot([(PS[i], RPS[i]) for i in range(4, 8)])
                sqb = Rot([(sb('sq%d' % i, [128, 512], BF16), Res()) for i in range(1)])
                cs2 = sb('cs2', [128, 2, 512], F32); R_c = Res(); R_s = R_c
                cT = cs2[:, 0, :]; sT = cs2[:, 1, :]
                for ti in range(NB512):
                    t0 = ti * 512
                    ht, rh = hts.next()
                    k.dma('sp', ht[:], hs[:, :, t0:t0 + 512].rearrange("k p t -> p k t"), reads=[R_hs[ti]], writes=[rh])
                    k.dma('sp', cs2[:], rp[:, ti, :, :], reads=[R_rp[ti]], writes=[R_c])
                    for blk in range(4):
                        pv, prv = psC.next()
                        k.mm(pv[:], [(ht[:, kc, blk * 128:(blk + 1) * 128], wv[:, kc, :]) for kc in range(8)], reads=[rh, rv], writes=[prv])
                        k.op('act', lambda e: e.activation(vt[:, ti * 4 + blk, :], pv[:], AF.Identity), reads=[prv], writes=[R_vt[ti]])
                    for h in range(4):
                        cs = slice(h * 128, (h + 1) * 128)
                        for which, (w_, rw_) in enumerate([(wq, rq), (wk, rk)]):
                            pp, prp = psC.next()
                            k.mm(pp[:], [(w_[:, kc, cs], ht[:, kc, :]) for kc in range(8)], reads=[rw_, rh], writes=[prp])
                            sq, rsq = sqb.next()
                            k.op('act', lambda e: e.activation(sq[:], pp[:], AF.Square), reads=[prp], writes=[rsq])
                            pss, prs = psC.next()
                            k.mm(pss[:], [(bones_bf, sq[:])], reads=[R_cb, rsq], writes=[prs])
                            rt, rr = rstd_from_ss(pss[:], prs, 64.0)
                            t1, r1 = tmp.next()
                            k.op('dve', lambda e: e.scalar_tensor_tensor(t1[:], pp[:], sm[:, which:which + 1], rt[:], ALU.mult, ALU.mult), reads=[prp, R_sm, rr], writes=[r1])
                            pq, prq = psC.next()
                            k.mm(pq[:], [(cf[:, 256:384], t1[:])], reads=[R_cf, r1], writes=[prq])
                            t2, r2 = tmp.next()
                            k.op('dve', lambda e: e.tensor_tensor(t2[:], pq[:], sT, ALU.mult), reads=[prq, R_s], writes=[r2])
                            k.op('dve', lambda e: e.tensor_tensor(t1[:], t1[:], cT, ALU.mult), reads=[r1, R_c], writes=[r1])
                            if which == 0:
                                k.op('dve', lambda e: e.tensor_tensor(qt[0:64, h, 0, :], t1[0:64, :], t2[0:64, :], ALU.add), reads=[r1, r2], writes=[R_qt[h]])
                                k.op('dve', lambda e: e.tensor_tensor(qt[64:128, h, 1, :], t1[64:128, :], t2[64:128, :], ALU.add), reads=[r1, r2], writes=[R_qt[h]])
                            else:
                                k.op('dve', lambda e: e.tensor_tensor(kT[:, h, t0:t0 + 512], t1[:], t2[:], ALU.add), reads=[r1, r2], writes=[R_kT[ti]])
                    for h in range(4):
                        ot, ro = ots.next()
                        acc = [psA.next() for _ in range(4)]
                        nkb = ti * 4 + 4
                        units = [(kb, c) for kb in range(nkb) for c in range(2)]
                        pend = {}

                        def emit_s(u):
                            kb, c = u
                            j = kb - ti * 4
                            n0 = max(0, j) * 128
                            ps_, prs_ = spool.next()
                            k.mm(ps_[:, n0:512], [(kT[:, h, kb * 128:(kb + 1) * 128], qt[:, h, c, n0:512])],
                                 reads=[R_kT[kb // 4], R_qt[h]], writes=[prs_])
                            eb, re_ = ebuf.next()
                            k.op('act', lambda e: e.activation(eb[:, n0:512], ps_[:, n0:512], AF.Exp, scale=scale), reads=[prs_], writes=[re_])
                            if j >= 0:
                                k.op('pool', lambda e: e.tensor_tensor(eb[:, n0:n0 + 128], eb[:, n0:n0 + 128], cb[:, 384:512], ALU.mult), reads=[re_, R_cb], writes=[re_])
                            if kb == 0:
                                k.op('dve', lambda e: e.tensor_copy(lacc[:, c, :], eb[:, :]), reads=[re_], writes=[R_la[c]])
                            else:
                                k.op('dve', lambda e: e.tensor_tensor(lacc[:, c, n0:512], lacc[:, c, n0:512], eb[:, n0:512], ALU.add), reads=[re_, R_la[c]], writes=[R_la[c]])
                            pend[u] = (eb, re_, n0)

                        def emit_pv(u):
                            kb, c = u
                            eb, re_, n0 = pend.pop(u)
                            (po, pro), (pl, prl) = acc[2 * c], acc[2 * c + 1]
                            k.mm(po[:, n0:512], [(vt[:, kb, h * 128:(h + 1) * 128], eb[:, n0:512])], reads=[R_vt[kb // 4], re_], writes=[pro],
                                 start=(kb == 0), stop=(kb == nkb - 1))

                        LOOK = 2
                        for i in range(len(units) + LOOK):
                            if i < len(units):
                                emit_s(units[i])
                            if i >= LOOK:
                                emit_pv(units[i - LOOK])
                        (po1, pro1), (pl1, prl1), (po2, pro2), (pl2, prl2) = acc
                        k.mm(pl1[:], [(cf[:, 512:640], lacc[:, 0, :])], reads=[R_cf, R_la[0]], writes=[prl1])
                        k.mm(pl2[:], [(cf[:, 512:640], lacc[:, 1, :])], reads=[R_cf, R_la[1]], writes=[prl2])
                        ra_, rra = tmp.next(); rb_, rrb = tmp.next()
                        k.op('dve', lambda e: e.reciprocal(ra_[:], pl1[:]), reads=[prl1], writes=[rra])
                        k.op('dve', lambda e: e.reciprocal(rb_[:], pl2[:]), reads=[prl2], writes=[rrb])
                        k.op('dve', lambda e: e.tensor_tensor(ra_[:], po1[:], ra_[:], ALU.mult), reads=[pro1, rra], writes=[rra])
                        k.op('dve', lambda e: e.scalar_tensor_tensor(rb_[:], rb_[:], sm[:, 3:4], po2[:], ALU.mult, ALU.mult), reads=[pro2, rrb, R_sm], writes=[rrb])
                        k.op('dve', lambda e: e.tensor_tensor(ra_[:], ra_[:], rb_[:], ALU.add), reads=[rra, rrb], writes=[rra])
                        sq, rsq = sqb.next()
                        k.op('act', lambda e: e.activation(sq[:], ra_[:], AF.Square), reads=[rra], writes=[rsq])
                        pss, prs = psC.next()
                        k.mm(pss[:], [(ones_bf, sq[:])], reads=[R_cb, rsq], writes=[prs])
                        rt, rr = rstd_from_ss(pss[:], prs, 128.0)
                        k.op('dve', lambda e: e.scalar_tensor_tensor(ot[:], ra_[:], sm[:, 2:3], rt[:], ALU.mult, ALU.mult), reads=[rra, R_sm, rr], writes=[ro])
                        k.dma('sp', os_[hg * 4 + h, :, t0:t0 + 512], ot[:], reads=[ro], writes=[R_os[hg][ti]])
                k.barrier()

    def mix_hgrn(l):
        idx = l // 3
        for hg in range(2):
            es, sb = stage_alloc()
            with es:
                (wq, rq), (wf, rf), (wi_, ri), (wg, rg) = load_w(sb, [a_win[idx, i, hg] for i in range(4)])
                lbf = sb('lbf', [128, 2, 8], F32); R_lbf = Res()
                k.dma('sp', lbf[:], a_lbf[:, :, :], writes=[R_lbf])
                lbr = sb('lbr', [128, 2, 512], F32); R_lbr = Res()
                for i in range(2):
                    k.dma('sp', lbr[:, i, :], a_lbr[i:i + 1, hg * 512:(hg + 1) * 512].partition_broadcast(128), writes=[R_lbr])
                k.op('act', lambda e: e.activation(lbf[:], lbf[:], AF.Exp), reads=[R_lbf], writes=[R_lbf])
                k.op('act', lambda e: e.activation(lbr[:], lbr[:], AF.Exp), reads=[R_lbr], writes=[R_lbr])
                lb_f = sb('lb_f', [128, 8], F32); oml_f = sb('oml_f', [128, 8], F32)
                lb_r = sb('lb_r', [128, 512], F32); oml_r = sb('oml_r', [128, 512], F32)
                for (src_, lb_, oml_, R_) in [(lbf, lb_f, oml_f, R_lbf), (lbr, lb_r, oml_r, R_lbr)]:
                    k.op('dve', lambda e: e.tensor_tensor(oml_[:], src_[:, 0, :], src_[:, 1, :], ALU.add), reads=[R_], writes=[R_])
                    k.op('dve', lambda e: e.reciprocal(oml_[:], oml_[:]), reads=[R_], writes=[R_])
                    if idx == 0:
                        k.op('dve', lambda e: e.tensor_tensor(lb_[:], src_[:, 0, :], src_[:, 0, :], ALU.subtract), reads=[R_], writes=[R_])
                    else:
                        k.op('dve', lambda e: e.tensor_copy(lb_[:], src_[:, 1, :]), reads=[R_], writes=[R_])
                    k.op('dve', lambda e: e.tensor_tensor(lb_[:], lb_[:], oml_[:], ALU.mult), reads=[R_], writes=[R_])
                    k.op('dve', lambda e: e.tensor_scalar(oml_[:], lb_[:], -1.0, 1.0, ALU.mult, ALU.add), reads=[R_], writes=[R_])
                ong = sb('ong', [128, 2], F32); R_on = Res()
                k.dma('sp', ong[:], a_on[:, :], writes=[R_on])
                St = sb('St', [128, 4, 128], F32); R_S = [Res() for _ in range(4)]
                Sb = sb('Sb', [128, 4, 128], BF16); R_Sb = [Res() for _ in range(4)]
                k.op('dve', lambda e: e.memset(St[:], 0.0), writes=R_S)
                k.op('dve', lambda e: e.memset(Sb[:], 0.0), writes=R_Sb)
                hts = Rot([(sb('ht%d' % i, [128, 8, 512], BF16), Res()) for i in range(2)])
                ots = Rot([(sb('ot%d' % i, [128, 4, 512], BF16), Res()) for i in range(2)])
                vtm = sb('vtm', [128, 4, 512], BF16); R_v = [Res() for _ in range(4)]
                khat = sb('khat', [128, 4, 2, 512], BF16); R_kh = [Res() for _ in range(4)]
                k.op('dve', lambda e: e.memset(khat[:], 0.0), writes=R_kh)
                lgf = sb('lgf', [128, 4, 512], F32); R_lg = [Res() for _ in range(4)]
                qf = sb('qf', [128, 4, 512], F32); R_qf = [Res() for _ in range(4)]
                sg = sb('sg', [128, 4, 512], F32); R_sg = [Res() for _ in range(4)]
                e1 = sb('e1', [128, 4, 512], F32); R_e1 = [Res() for _ in range(4)]
                qi = sb('qi', [128, 4, 512], BF16); R_qi = [Res() for _ in range(4)]
                qtl = sb('qtl', [128, 4, 512], BF16); R_qtl = [Res() for _ in range(4)]
                ktl = sb('ktl', [128, 4, 512], BF16); R_ktl = [Res() for _ in range(4)]
                nr = sb('nr', [128, 4, 16], F32); R_nr = [Res() for _ in range(4)]
                of = sb('of', [128, 4, 512], F32); R_of = [Res() for _ in range(4)]
                atm = Rot([(sb('atm%d' % i, [128, 128], BF16), Res()) for i in range(3)])
                sqb = Rot([(sb('sq%d' % i, [128, 512], BF16), Res()) for i in range(2)])
                for ti in range(NB512):
                    t0 = ti * 512
                    ht, rh = hts.next()
                    k.dma('sp', ht[:], hs[:, :, t0:t0 + 512].rearrange("k p t -> p k t"), reads=[R_hs[ti]], writes=[rh])
                    for blk in range(4):
                        bs = slice(blk * 128, (blk + 1) * 128)
                        pv, prv = psC.next()
                        k.mm(pv[:], [(ht[:, kc, bs], wi_[:, kc, :]) for kc in range(8)], reads=[rh, ri], writes=[prv])
                        k.op('act', lambda e: e.activation(vtm[:, blk, :], pv[:], AF.Identity), reads=[prv], writes=[R_v[blk]])
                        pf, prf = psC.next()
                        k.mm(pf[:], [(ht[:, kc, bs], wf[:, kc, :]) for kc in range(8)], reads=[rh, rf], writes=[prf])
                        t1, r1 = tmp.next()
                        sig_to(t1[:], pf[:], prf, r1)
                        k.op('dve', lambda e: e.tensor_tensor(t1[:], t1[:], oml_r[:], ALU.mult), reads=[r1, R_lbr], writes=[r1])
                        k.op('dve', lambda e: e.tensor_tensor(t1[:], t1[:], lb_r[:], ALU.add), reads=[r1, R_lbr], writes=[r1])
                        k.op('act', lambda e: e.activation(lgf[:, blk, :], t1[:], AF.Ln), reads=[r1], writes=[R_lg[blk]])
                        k.op('dve', lambda e: e.tensor_scalar(t1[:], t1[:], -1.0, 1.0, ALU.mult, ALU.add), reads=[r1], writes=[r1])
                        pd, prd = psC.next()
                        k.mm(pd[:], [(cf[:, 128:256], lgf[:, blk, :])], reads=[R_cf, R_lg[blk]], writes=[prd])
                        t2, r2 = tmp.next()
                        k.op('act', lambda e: e.activation(t2[:], pd[:], AF.Exp), reads=[prd], writes=[r2])
                        k.op('dve', lambda e: e.tensor_tensor(khat[0:64, blk, 0, :], t1[0:64, :], t2[0:64, :], ALU.mult), reads=[r1, r2], writes=[R_kh[blk]])
                        k.op('dve', lambda e: e.tensor_tensor(khat[64:128, blk, 1, :], t1[64:128, :], t2[64:128, :], ALU.mult), reads=[r1, r2], writes=[R_kh[blk]])
                    for h in range(4):
                        cs = slice(h * 128, (h + 1) * 128)
                        gh = hg * 4 + h
                        pq, prq = psA.next()
                        k.mm(pq[:], [(wq[:, kc, cs], ht[:, kc, :]) for kc in range(8)], reads=[rq, rh], writes=[prq])
                        sig_to(qf[:, h, :], pq[:], prq, R_qf[h])
                        k.op('dve', lambda e: e.tensor_tensor(qf[:, h, :], qf[:, h, :], pq[:], ALU.mult), reads=[R_qf[h], prq], writes=[R_qf[h]])
                        pg, prg = psA.next()
                        k.mm(pg[:], [(wg[:, kc, cs], ht[:, kc, :]) for kc in range(8)], reads=[rg, rh], writes=[prg])
                        sig_to(sg[:, h, :], pg[:], prg, R_sg[h])
                        k.op('dve', lambda e: e.tensor_tensor(sg[:, h, :], sg[:, h, :], pg[:], ALU.mult), reads=[R_sg[h], prg], writes=[R_sg[h]])
                        pf, prf = psA.next()
                        k.mm(pf[:], [(wf[:, kc, cs], ht[:, kc, :]) for kc in range(8)], reads=[rf, rh], writes=[prf])
                        kf, rkf = tmp.next()
                        sig_to(kf[:], pf[:], prf, rkf)
                        k.op('dve', lambda e: e.tensor_scalar(kf[:], kf[:], oml_f[:, gh:gh + 1], lb_f[:, gh:gh + 1], ALU.mult, ALU.add), reads=[rkf, R_lbf], writes=[rkf])
                        k.op('dve', lambda e: e.tensor_scalar(kf[:], kf[:], -1.0, 1.0, ALU.mult, ALU.add), reads=[rkf], writes=[rkf])
                        pb, prb = psA.next()
                        for blk in range(4):
                            k.mm(pb[:, blk * 128:(blk + 1) * 128], [(lgf[:, blk, cs], cf[:, 0:128])], reads=[R_lg[blk], R_cf], writes=[prb])
                        k.op('act', lambda e: e.activation(e1[:, h, :], pb[:], AF.Exp), reads=[prb], writes=[R_e1[h]])
                        b3 = pb[:].rearrange("p (c t) -> p c t", t=64)
                        k.op('dve', lambda e: e.tensor_copy(nr[:, h, 8:16], b3[:, :, 31]), reads=[prb], writes=[R_nr[h]])
                        k.op('dve', lambda e: e.tensor_scalar(nr[:, h, 0:8], nr[:, h, 8:16], -1.0, None, ALU.mult), reads=[R_nr[h]], writes=[R_nr[h]])
                        eq, req = tmp.next(); ek, rek = tmp.next()
                        for c in range(8):
                            c_ = slice(c * 64, (c + 1) * 64)
                            k.op('act', lambda e: e.activation(eq[:, c_], pb[:, c_], AF.Exp, bias=nr[:, h, c:c + 1], scale=1.0), reads=[prb, R_nr[h]], writes=[req])
                            k.op('act', lambda e: e.activation(ek[:, c_], pb[:, c_], AF.Exp, bias=nr[:, h, 8 + c:9 + c], scale=-1.0), reads=[prb, R_nr[h]], writes=[rek])
                        k.op('dve', lambda e: e.tensor_tensor(qtl[:, h, :], qf[:, h, :], eq[:], ALU.mult), reads=[R_qf[h], req], writes=[R_qtl[h]])
                        k.op('dve', lambda e: e.tensor_tensor(ktl[:, h, :], kf[:], ek[:], ALU.mult), reads=[rkf, rek], writes=[R_ktl[h]])
                        k.op('dve', lambda e: e.tensor_tensor(qi[:, h, :], qf[:, h, :], e1[:, h, :], ALU.mult), reads=[R_qf[h], R_e1[h]], writes=[R_qi[h]])
                    for blk in range(4):
                        bs = slice(blk * 128, (blk + 1) * 128)
                        for h in range(4):
                            cs = slice(h * 128, (h + 1) * 128)
                            pa, pra = psC.next()
                            k.mm(pa[:, 0:128], [(ktl[:, h, bs], qtl[:, h, bs])], reads=[R_ktl[h], R_qtl[h]], writes=[pra])
                            am, ram = atm.next()
                            k.op('dve', lambda e: e.tensor_tensor(am[:], pa[:, 0:128], cf[:, 0:128], ALU.mult), reads=[pra, R_cf], writes=[ram])
                            po, pro = psB.next()
                            for cc in range(2):
                                c = blk * 2 + cc
                                c_ = slice(c * 64, (c + 1) * 64)
                                rows = slice(cc * 64, (cc + 1) * 64)
                                k.mm(po[:, cc * 64:(cc + 1) * 64], [(Sb[:, h, :], qi[:, h, c_])], reads=[R_Sb[h], R_qi[h]], writes=[pro],
                                     start=(cc == 0), stop=False)
                                psn, prsn = psA.next()
                                k.mm(psn[:, 0:128], [(khat[:, blk, cc, cs], vtm[:, blk, cs])], reads=[R_kh[blk], R_v[blk]], writes=[prsn])
                                k.op('dve', lambda e: e.scalar_tensor_tensor(St[:, h, :], St[:, h, :], e1[:, h, c * 64 + 63:c * 64 + 64], psn[:, 0:128], ALU.mult, ALU.add),
                                     reads=[R_S[h], R_e1[h], prsn], writes=[R_S[h]])
                                k.op('act', lambda e: e.activation(Sb[:, h, :], St[:, h, :], AF.Identity), reads=[R_S[h]], writes=[R_Sb[h]])
                            k.mm(po[:, 0:128], [(vtm[:, blk, cs], am[:])], reads=[R_v[blk], ram], writes=[pro], start=False, stop=True)
                            k.op('act', lambda e: e.activation(of[:, h, bs], po[:, 0:128], AF.Identity), reads=[pro], writes=[R_of[h]])
                    ot, ro = ots.next()
                    for h in range(4):
                        sq, rsq = sqb.next()
                        k.op('act', lambda e: e.activation(sq[:], of[:, h, :], AF.Square), reads=[R_of[h]], writes=[rsq])
                        pss, prs = psC.next()
                        k.mm(pss[:], [(ones_bf, sq[:])], reads=[R_cb, rsq], writes=[prs])
                        rt, rr = rstd_from_ss(pss[:], prs, 128.0)
                        t1, r1 = tmp.next()
                        k.op('dve', lambda e: e.tensor_tensor(t1[:], of[:, h, :], rt[:], ALU.mult), reads=[R_of[h], rr], writes=[r1])
                        k.op('dve', lambda e: e.scalar_tensor_tensor(ot[:, h, :], t1[:], ong[:, idx:idx + 1], sg[:, h, :], ALU.mult, ALU.mult), reads=[r1, R_on, R_sg[h]], writes=[ro])
                    k.dma('sp', os_[hg * 4:(hg + 1) * 4, :, t0:t0 + 512].rearrange("k p t -> p k t"), ot[:], reads=[ro], writes=[R_os[hg][ti]])
                k.barrier()

    MIX = {0: mix_hgrn, 1: mix_attn, 2: mix_conv}
    for st in stages:
        if st[0] == 'pro':
            prologue()
        elif st[0] == 'rope':
            rope_stage()
        elif st[0] == 'tok':
            _, src, l_out, ffns, pren, dst = st
            tok_stage(xT_in if src == 'in' else xs, l_out, ffns, pren, xs)
        elif st[0] == 'mix':
            MIX[st[1] % 3](st[1])
    k.finish('sp')
    stats = (k.n_instr, k.n_wait)
    k.close()
    return nc, stats


FULL_STAGES = [('rope',), ('pro',), ('tok', 'in', None, [(0, 0)], 0, 'xs')]
for _l in range(DEPTH):
    FULL_STAGES.append(('mix', _l))
    if _l < DEPTH - 1:
        FULL_STAGES.append(('tok', 'xs', _l, [(_l, 2), (_l + 1, 0)], _l + 1, 'xs'))
    else:
        FULL_STAGES.append(('tok', 'xs', _l, [(_l, 2)], None, 'out'))


def make_consts():
    c = np.zeros((128, 1024), np.float32)
    s = np.arange(128)[:, None]; t = np.arange(128)[None, :]
    same = (s // 64) == (t // 64)
    c[:, 0:128] = (same & (s <= t))
    c[:, 128:256] = (same & (s > t))
    P = np.zeros((128, 128), np.float32)
    for m in range(128):
        d = m % 64
        if d < 8:
            P[m, m + 8] = 1.0
        elif d < 16:
            P[m, m - 8] = 1.0
    c[:, 256:384] = P.T
    c[:, 384:512] = (s <= t)
    c[:, 512:640] = 1.0
    c[:, 640:768] = ((s // 64) == (t // 64))
    inv_freq = (500000.0 ** (-np.arange(0, 16, 2, dtype=np.float32) / 16)).astype(np.float32)
    for p in range(128):
        d = p % 64
        if d < 16:
            c[p, 768] = inv_freq[d % 8]
            c[p, 769] = -1.0 if d < 8 else 1.0
    return c


def prep_shared(inp):
    f32 = np.float32
    sh = {}
    sh['ada_w'] = np.ascontiguousarray(inp['ada_w'], f32)
    sh['ada_b_l'] = np.ascontiguousarray(inp['ada_b'].reshape(DEPTH, 72, 128).transpose(2, 0, 1), f32)
    sh['norm_g_l'] = np.ascontiguousarray(inp['norm_g'].reshape(DEPTH, 3, 8, 128).transpose(3, 0, 1, 2), f32)
    wi = inp['ffn_wi'].reshape(DEPTH, 2, 8, 128, 2, NF, 128)
    wi = wi.transpose(0, 1, 5, 3, 4, 2, 6).reshape(DEPTH, 2, NF, 128, 2048)
    wo = inp['ffn_wo'].reshape(DEPTH, 2, NF, 128, D)
    sh['ffw'] = np.ascontiguousarray(np.concatenate([wi, wo], axis=-1), f32)
    wouts = [inp['a_w_out'][0], inp['b_w_out'][0], inp['c_w_out'][0], inp['a_w_out'][1]]
    sh['wout_l'] = np.ascontiguousarray(np.stack([w.reshape(8, 128, D).transpose(1, 0, 2) for w in wouts]), f32)

    def inl(w, nsplit):
        w = w.reshape(8, 128, nsplit, 2, 512)
        return np.ascontiguousarray(w.transpose(2, 3, 1, 0, 4), f32)
    sh['a_win_l'] = np.stack([inl(inp['a_w_in'][i], 4) for i in range(2)])
    sh['b_win_l'] = inl(inp['b_w_in'][0], 3)
    sh['c_win_l'] = inl(inp['c_w_in'][0], 3)
    sh['a_lb_fm'] = np.ascontiguousarray(inp['a_lb'].reshape(2, 8, 128).transpose(2, 0, 1), f32)
    sh['a_lb_row'] = np.ascontiguousarray(inp['a_lb'], f32)
    sh['a_onorm_l'] = np.ascontiguousarray(inp['a_onorm'].T, f32)
    g = inp['b_qk_g'][0]
    sh['b_qkg_l'] = np.ascontiguousarray(np.concatenate([g, g], axis=1).T, f32)
    sh['b_lam'] = np.ascontiguousarray(inp['b_lam'][0].reshape(1, 256), f32)
    sh['b_subln_l'] = np.ascontiguousarray(inp['b_subln'][0].reshape(128, 1), f32)
    sh['c_conv_l'] = np.ascontiguousarray(inp['c_conv'][0].reshape(3, 8, 128).transpose(2, 1, 0), f32)
    sh['consts'] = make_consts()
    return sh


def prep_core(inp, b, S):
    m = {}
    m['xT'] = np.ascontiguousarray(inp['x'][b, :S].T.reshape(8, 128, S), np.float32)
    m['c_l'] = np.ascontiguousarray(inp['c'][b].reshape(8, 128).T, np.float32)
    m['pos'] = np.ascontiguousarray(inp['positions'][b, :S].reshape(1, S), np.int32)
    return m


_CACHE = {}


def kernel(**inputs):
    inp = {k_: np.asarray(v) for k_, v in inputs.items()}
    B, S, _ = inp['x'].shape
    if S not in _CACHE:
        _CACHE[S] = build_program(S, FULL_STAGES)[0]
    nc = _CACHE[S]
    sh = prep_shared(inp)
    in_maps = []
    for b in range(B):
        m = dict(sh)
        m.update(prep_core(inp, b, S))
        in_maps.append(m)
    res = run_bass_kernel_spmd(nc, in_maps, core_ids=list(range(B)))
    out = np.stack([res.results[b]['xT_out'].reshape(D, S).T for b in range(B)])
    return np.ascontiguousarray(out, np.float32)
```

```python
import contextlib
import math
import numpy as np
import ml_dtypes
import concourse.bass as bass
import concourse.mybir as mybir
from concourse.bass_utils import run_bass_kernel_spmd

F32 = mybir.dt.float32
BF16 = mybir.dt.bfloat16
I32 = mybir.dt.int32
AF = mybir.ActivationFunctionType
ALU = mybir.AluOpType

D = 1024
FF = 2816
NF = 22
DEPTH = 4
EPS = 1e-6
TWO_PI = 2.0 * math.pi
CC_INC = 16


class Res:
    __slots__ = ('w', 'r', 'name')

    def __init__(self, name=''):
        self.w = None
        self.r = {}
        self.name = name


class KB:
    NDMA = {'sp': 12, 'pool': 12}

    def __init__(self, nc):
        self.nc = nc
        self.es = contextlib.ExitStack()
        self.engs = {'pe': nc.tensor, 'act': nc.scalar, 'dve': nc.vector, 'pool': nc.gpsimd, 'sp': nc.sync}
        self.sems = {}
        self.cnt = {}
        self.cur = {}
        self.gen = 0
        self.retired = set()
        self._fresh()
        self.dsem = {}
        self.drr = {}
        for q, n in self.NDMA.items():
            self.dsem[q] = []
            self.drr[q] = 0
            for i in range(n):
                nm = 'd_%s%d' % (q, i)
                self.sems[nm] = self.es.enter_context(nc.semaphore(nm))
                self.cnt[nm] = 0
                self.dsem[q].append(nm)
        self.waited = {e: {} for e in self.engs}
        self.n_instr = 0
        self.n_wait = 0
        self.uid = 0

    def _fresh(self):
        for e in ['pe', 'act', 'dve', 'pool']:
            if e in self.cur:
                self.retired.add(self.cur[e])
            nm = '%s@%d' % (e, self.gen)
            self.sems[nm] = self.es.enter_context(self.nc.semaphore('s_%s_%d' % (e, self.gen)))
            self.cnt[nm] = 0
            self.cur[e] = nm
        self.gen += 1

    def sb(self, name, shape, dt):
        return self.es.enter_context(self.nc.sbuf_tensor(name, list(shape), dt))

    def ps(self, name, shape, dt=F32):
        return self.es.enter_context(self.nc.psum_tensor(name, list(shape), dt))

    def _wait(self, eng, dep):
        s, v = dep
        if s in self.retired:
            return
        if eng == 'pe' and s == self.cur['pe']:
            return
        if self.waited[eng].get(s, 0) >= v:
            return
        self.engs[eng].wait_ge(self.sems[s], v)
        self.waited[eng][s] = v
        self.n_wait += 1

    def _deps(self, eng, reads, writes):
        deps = {}
        for r in reads:
            if r.w is not None:
                s, v = r.w
                deps[s] = max(deps.get(s, 0), v)
        for w in writes:
            if w.w is not None:
                s, v = w.w
                deps[s] = max(deps.get(s, 0), v)
            for s, v in w.r.items():
                deps[s] = max(deps.get(s, 0), v)
        for s, v in deps.items():
            self._wait(eng, (s, v))

    def _mark(self, tick, reads, writes):
        s, v = tick
        for r in reads:
            r.r[s] = max(r.r.get(s, 0), v)
        for w in writes:
            w.w = tick
            w.r = {}

    def op(self, eng, fn, reads=(), writes=()):
        self._deps(eng, reads, writes)
        ins = fn(self.engs[eng])
        nm = self.cur[eng]
        ins.then_inc(self.sems[nm], 1)
        self.cnt[nm] += 1
        self.n_instr += 1
        self._mark((nm, self.cnt[nm]), reads, writes)

    def mm(self, out, pairs, reads=(), writes=(), start=True, stop=True):
        self._deps('pe', reads, writes)
        n = len(pairs)
        ins = None
        for i, (l, r) in enumerate(pairs):
            ins = self.nc.tensor.matmul(out, l, r, start=(start and i == 0), stop=(stop and i == n - 1))
            self.n_instr += 1
        nm = self.cur['pe']
        ins.then_inc(self.sems[nm], 1)
        self.cnt[nm] += 1
        self._mark((nm, self.cnt[nm]), reads, writes)

    def dma(self, q, out, in_, reads=(), writes=(), **kw):
        sl = self.dsem[q]
        nm = sl[self.drr[q] % len(sl)]
        self.drr[q] += 1
        if self.cnt[nm] > 0:
            self._wait(q, (nm, self.cnt[nm]))
        self._deps(q, reads, writes)
        self.engs[q].dma_start(out=out, in_=in_, **kw).then_inc(self.sems[nm], 16)
        self.cnt[nm] += 16
        self.n_instr += 1
        self._mark((nm, self.cnt[nm]), reads, writes)

    def coll(self, kind, groups, in_, out, reads=(), writes=()):
        q = 'pool'
        sl = self.dsem[q]
        nm = sl[self.drr[q] % len(sl)]
        self.drr[q] += 1
        if self.cnt[nm] > 0:
            self._wait(q, (nm, self.cnt[nm]))
        self._deps(q, reads, writes)
        self.nc.gpsimd.collective_compute(kind, ALU.bypass, replica_groups=groups, ins=[in_], outs=[out]).then_inc(self.sems[nm], CC_INC)
        self.cnt[nm] += CC_INC
        self.n_instr += 1
        self._mark((nm, self.cnt[nm]), reads, writes)

    def barrier(self, fresh=True):
        for eng in ['pe', 'act', 'dve', 'pool', 'sp']:
            for nm in self.sems:
                if self.cnt[nm] > 0 and nm not in self.retired:
                    if eng == 'pe' and nm == self.cur['pe']:
                        continue
                    self._wait(eng, (nm, self.cnt[nm]))
        if fresh and self.gen < 16:
            self._fresh()

    def finish(self, eng='sp'):
        for nm in self.sems:
            if self.cnt[nm] > 0 and nm not in self.retired:
                self._wait(eng, (nm, self.cnt[nm]))

    def close(self):
        self.es.close()


class Rot:
    def __init__(self, items):
        self.items = items
        self.i = 0

    def next(self):
        it = self.items[self.i % len(self.items)]
        self.i += 1
        return it


class WStream:
    def __init__(self, k, bufs, kw=None):
        self.k = k
        self.bufs = bufs
        self.q = []
        self.issued = 0
        self.used = 0
        self.kw = kw or {}
        self.srcs = []

    def add(self, src):
        self.srcs.append(src)

    def pump(self):
        while self.issued < len(self.srcs) and self.issued - self.used < len(self.bufs):
            t, r = self.bufs[self.issued % len(self.bufs)]
            src = self.srcs[self.issued]
            self.k.dma('pool', t[:], src, writes=[r], **self.kw)
            self.issued += 1

    def get(self):
        self.pump()
        assert self.used < self.issued
        it = self.bufs[self.used % len(self.bufs)]
        self.used += 1
        return it


MAGIC = 12582912.0
C1 = 6.28125
C2 = TWO_PI - 6.28125
PI_LO = 3.1415925


def build_program(S, stages):
    nc = bass.Bass("TRN2", target_bir_lowering=False)
    k = KB(nc)
    TT = min(1024, S)
    NSUB = TT // 512
    NTILE = S // TT
    NB512 = S // 512

    def din(name, shape, dt=F32):
        return nc.dram_tensor(name, list(shape), dt, kind="ExternalInput").ap()

    xT_in = din("xT", [8, 128, S])
    c_in = din("c_l", [128, 8])
    adaw = din("ada_w", [DEPTH, D, 9 * D])
    adab = din("ada_b_l", [128, DEPTH, 72])
    normg = din("norm_g_l", [128, DEPTH, 3, 8])
    ffw = din("ffw", [DEPTH, 2, NF, 128, 3072])
    woutm = din("wout_l", [DEPTH, 128, 8, D])
    a_win = din("a_win_l", [2, 4, 2, 128, 8, 512])
    a_lbf = din("a_lb_fm", [128, 2, 8])
    a_lbr = din("a_lb_row", [2, D])
    a_on = din("a_onorm_l", [128, 2])
    b_win = din("b_win_l", [3, 2, 128, 8, 512])
    b_qkg = din("b_qkg_l", [128, 2])
    b_lam = din("b_lam", [1, 256])
    b_sub = din("b_subln_l", [128, 1])
    c_win = din("c_win_l", [3, 2, 128, 8, 512])
    c_cv = din("c_conv_l", [128, 8, 3])
    pos_in = din("pos", [1, S], I32)
    cst = din("consts", [128, 1024])
    xT_out = nc.dram_tensor("xT_out", [8, 128, S], F32, kind="ExternalOutput").ap()
    xs = xT_out
    hs = nc.dram_tensor("hs", [8, 128, S], BF16).ap()
    os_ = nc.dram_tensor("os", [8, 128, S], BF16).ap()
    R_xs = [Res() for _ in range(NTILE)]
    R_hs = [Res() for _ in range(NB512)]
    R_os = [[Res() for _ in range(NB512)] for _ in range(2)]
    rp = nc.dram_tensor("rp", [128, NB512, 2, 512], F32).ap()
    R_rp = [Res() for _ in range(NB512)]

    cf = k.sb('cf', [128, 1024], F32); R_cf = Res()
    k.dma('sp', cf[:], cst[:, :], writes=[R_cf])
    cb = k.sb('cb', [128, 1024], BF16); R_cb = Res()
    k.op('dve', lambda e: e.tensor_copy(cb[:], cf[:]), reads=[R_cf], writes=[R_cb])
    ones_bf = cb[:, 512:640]
    bones_bf = cb[:, 640:768]
    modp = k.sb('modp', [128, DEPTH, 3, 3, 8], F32); R_mod = Res()
    epsb = k.sb('epsb', [128, 1], F32)
    k.op('dve', lambda e: e.memset(epsb[:], EPS), writes=[R_cf])

    PS = [k.ps('ps%d' % i, [128, 512]) for i in range(8)]
    RPS = [Res() for _ in range(8)]
    psA = Rot([(PS[i], RPS[i]) for i in range(4)])
    psB = Rot([(PS[i], RPS[i]) for i in range(4, 6)])
    psC = Rot([(PS[i], RPS[i]) for i in range(6, 8)])
    TMP = [(k.sb('tmp%d' % i, [128, 512], F32), Res()) for i in range(6)]
    tmp = Rot(TMP)
    rsp = Rot([(k.sb('rsp%d' % i, [128, 512], F32), Res()) for i in range(2)])

    def rstd_from_ss(ss_ps, R_ss, n, width=512):
        t, r = rsp.next()
        k.op('act', lambda e: e.activation(t[:, :width], ss_ps, AF.Ln, bias=epsb[:, 0:1], scale=1.0 / n), reads=[R_ss, R_cf], writes=[r])
        k.op('act', lambda e: e.activation(t[:, :width], t[:, :width], AF.Exp, scale=-0.5), reads=[r], writes=[r])
        return t, r

    def sig_to(dst, src_ap, R_src, R_dst, extra_reads=()):
        k.op('act', lambda e: e.activation(dst, src_ap, AF.Exp, scale=-1.0), reads=[R_src] + list(extra_reads), writes=[R_dst])
        k.op('act', lambda e: e.activation(dst, dst, AF.Identity, bias=cf[:, 512:513], scale=1.0), reads=[R_dst, R_cf], writes=[R_dst])
        k.op('dve', lambda e: e.reciprocal(dst, dst), reads=[R_dst], writes=[R_dst])

    def stage_alloc():
        es = contextlib.ExitStack()

        def sb(name, shape, dt):
            k.uid += 1
            return es.enter_context(nc.sbuf_tensor('%s_%d' % (name, k.uid), list(shape), dt))
        return es, sb

    def prologue():
        es, sb = stage_alloc()
        with es:
            ct = sb('ct', [128, 8], F32); R_ct = Res()
            k.dma('sp', ct[:], c_in[:, :], writes=[R_ct])
            cs_ = sb('cs_', [128, 8], F32)
            sig_to(cs_[:], ct[:], R_ct, R_ct)
            k.op('dve', lambda e: e.tensor_tensor(ct[:], ct[:], cs_[:], ALU.mult), reads=[R_ct], writes=[R_ct])
            ab = sb('ab', [128, DEPTH, 72], F32); R_ab = Res()
            k.dma('sp', ab[:], adab[:, :, :], writes=[R_ab])
            ng = sb('ng', [128, DEPTH, 3, 8], F32); R_ng = Res()
            k.dma('sp', ng[:], normg[:, :, :, :], writes=[R_ng])
            CW = 1152
            awb = Rot([(sb('awb%d' % i, [128, 8, CW], F32), Res()) for i in range(2)])
            ncc = CW // 128
            for l in range(DEPTH):
                for g in range(9 * D // CW):
                    t, r = awb.next()
                    src = adaw[l, :, g * CW:(g + 1) * CW].rearrange("(k p) c -> p k c", p=128)
                    k.dma('sp', t[:], src, writes=[r])
                    pt, pr = psC.next()
                    for cc in range(ncc):
                        k.mm(pt[:, cc:cc + 1], [(t[:, kc, cc * 128:(cc + 1) * 128], ct[:, kc:kc + 1]) for kc in range(8)],
                             reads=[r, R_ct], writes=[pr])
                    k.op('dve', lambda e: e.tensor_tensor(ab[:, l, g * ncc:(g + 1) * ncc], pt[:, 0:ncc], ab[:, l, g * ncc:(g + 1) * ncc], ALU.add),
                         reads=[pr, R_ab], writes=[R_ab])
            for l in range(DEPTH):
                for j in range(3):
                    base = j * 24
                    k.op('dve', lambda e: e.tensor_copy(modp[:, l, j, 0, :], ab[:, l, base:base + 8]), reads=[R_ab], writes=[R_mod])
                    k.op('dve', lambda e: e.scalar_tensor_tensor(modp[:, l, j, 1, :], ab[:, l, base + 8:base + 16], 1.0, ng[:, l, j, :], ALU.add, ALU.mult),
                         reads=[R_ab, R_ng], writes=[R_mod])
                    cj = 1.0 if j == 1 else 0.5
                    k.op('dve', lambda e: e.tensor_scalar(modp[:, l, j, 2, :], ab[:, l, base + 16:base + 24], 1.0, cj, ALU.add, ALU.mult),
                         reads=[R_ab], writes=[R_mod])
            k.barrier()

    def tok_stage(src, l_out, ffns, prenorm_l, dst):
        es, sb = stage_alloc()
        with es:
            xt = sb('xt', [128, 8, TT], F32); R_xt = [[Res() for _ in range(NSUB)] for _ in range(8)]
            hb = sb('hb', [128, 8, TT], BF16); R_hb = [Res() for _ in range(NSUB)]
            act = sb('actT', [128, NF, TT], BF16); R_act = [[Res() for _ in range(NSUB)] for _ in range(NF)]
            wo_sb = sb('wo_sb', [128, NF, D], BF16); R_wo = [Res() for _ in range(NF)]
            wi_bufs = [(sb('wi%d' % i, [128, 2048], BF16), Res()) for i in range(6)]
            wm = sb('wm', [128, 8, D], BF16); R_wm = Res()
            allx = [R_xt[dc][s] for dc in range(8) for s in range(NSUB)]

            def norm_to_hb(l, j, sub):
                sl = slice(sub * 512, (sub + 1) * 512)
                for dc in range(8):
                    k.op('act', lambda e: e.activation(act[:, dc, sl], xt[:, dc, sl], AF.Square), reads=[R_xt[dc][sub]], writes=[R_act[dc][sub]])
                pt, pr = psC.next()
                k.mm(pt[:], [(ones_bf, act[:, dc, sl]) for dc in range(8)], reads=[R_cb] + [R_act[dc][sub] for dc in range(8)], writes=[pr])
                rt, rr = rstd_from_ss(pt[:], pr, float(D))
                for dc in range(8):
                    t, r = tmp.next()
                    k.op('dve', lambda e: e.scalar_tensor_tensor(t[:], xt[:, dc, sl], modp[:, l, j, 1, dc:dc + 1], rt[:], ALU.mult, ALU.mult),
                         reads=[R_xt[dc][sub], R_mod, rr], writes=[r])
                    k.op('act', lambda e: e.activation(hb[:, dc, sl], t[:], AF.Identity, bias=modp[:, l, j, 0, dc:dc + 1], scale=1.0),
                         reads=[r, R_mod], writes=[R_hb[sub]])

            def resid_update(l, j, dc, sub, pt, pr):
                sl = slice(sub * 512, (sub + 1) * 512)
                k.op('dve', lambda e: e.scalar_tensor_tensor(xt[:, dc, sl], pt[:], modp[:, l, j, 2, dc:dc + 1], xt[:, dc, sl], ALU.mult, ALU.add),
                     reads=[pr, R_mod, R_xt[dc][sub]], writes=[R_xt[dc][sub]])

            ws = WStream(k, wi_bufs, kw=dict(max_dma_last_dim=8192))
            for ti in range(NTILE):
                for (l, j) in ffns:
                    for f in range(NF):
                        ws.add(ffw[l, j // 2, f, :, 0:2048])
            for ti in range(NTILE):
                t0 = ti * TT
                k.dma('sp', xt[:], src[:, :, t0:t0 + TT].rearrange("k p t -> p k t"), reads=[R_xs[ti]], writes=allx)
                if l_out is not None:
                    if ti == 0:
                        k.dma('pool', wm[:], woutm[l_out], writes=[R_wm], max_dma_last_dim=8192)
                    k.dma('sp', hb[:], os_[:, :, t0:t0 + TT].rearrange("k p t -> p k t"),
                          reads=[R_os[g][t0 // 512 + s] for s in range(NSUB) for g in range(2)], writes=R_hb)
                    for dc in range(8):
                        for sub in range(NSUB):
                            sl = slice(sub * 512, (sub + 1) * 512)
                            pt, pr = psB.next()
                            k.mm(pt[:], [(wm[:, kc, dc * 128:(dc + 1) * 128], hb[:, kc, sl]) for kc in range(8)], reads=[R_wm, R_hb[sub]], writes=[pr])
                            resid_update(l_out, 1, dc, sub, pt, pr)
                for (l, j) in ffns:
                    for sub in range(NSUB):
                        norm_to_hb(l, j, sub)
                    for f in range(NF):
                        wt, wr = ws.get()
                        k.dma('pool', wo_sb[:, f, :], ffw[l, j // 2, f, :, 2048:3072], writes=[R_wo[f]], max_dma_last_dim=8192)
                        for sub in range(NSUB):
                            sl = slice(sub * 512, (sub + 1) * 512)
                            pg, prg = psA.next()
                            pu, pru = psA.next()
                            k.mm(pg[:], [(wt[:, kc * 128:(kc + 1) * 128], hb[:, kc, sl]) for kc in range(8)], reads=[wr, R_hb[sub]], writes=[prg])
                            k.mm(pu[:], [(wt[:, 1024 + kc * 128:1024 + (kc + 1) * 128], hb[:, kc, sl]) for kc in range(8)], reads=[wr, R_hb[sub]], writes=[pru])
                            t, r = tmp.next()
                            sig_to(t[:], pg[:], prg, r)
                            k.op('dve', lambda e: e.tensor_tensor(t[:], t[:], pg[:], ALU.mult), reads=[r, prg], writes=[r])
                            k.op('dve', lambda e: e.tensor_tensor(act[:, f, sl], t[:], pu[:], ALU.mult), reads=[r, pru], writes=[R_act[f][sub]])
                        ws.pump()
                    for dc in range(8):
                        for sub in range(NSUB):
                            sl = slice(sub * 512, (sub + 1) * 512)
                            pt, pr = psB.next()
                            k.mm(pt[:], [(wo_sb[:, f, dc * 128:(dc + 1) * 128], act[:, f, sl]) for f in range(NF)],
                                 reads=R_wo + [R_act[f][sub] for f in range(NF)], writes=[pr])
                            resid_update(l, j, dc, sub, pt, pr)
                if prenorm_l is not None:
                    for sub in range(NSUB):
                        norm_to_hb(prenorm_l, 1, sub)
                    k.dma('sp', hs[:, :, t0:t0 + TT].rearrange("k p t -> p k t"), hb[:], reads=R_hb, writes=[R_hs[t0 // 512 + s] for s in range(NSUB)])
                k.dma('sp', dst[:, :, t0:t0 + TT].rearrange("k p t -> p k t"), xt[:], reads=allx, writes=[R_xs[ti]])
            k.barrier()

    def load_w(sb, srcs):
        out = []
        for i, s_ in enumerate(srcs):
            t = sb('w%d' % i, [128, 8, 512], BF16); r = Res()
            k.dma('pool', t[:], s_, writes=[r], max_dma_last_dim=8192)
            out.append((t, r))
        return out

    def mix_conv(l):
        for hg in range(2):
            es, sb = stage_alloc()
            with es:
                (wbg, rbg), (wcg, rcg), (wu, ru) = load_w(sb, [c_win[i, hg] for i in range(3)])
                cv = sb('cv', [128, 8, 3], F32); R_cv = Res()
                k.dma('sp', cv[:], c_cv[:, :, :], writes=[R_cv])
                up = sb('up', [128, 4, 514], F32); R_up = [Res() for _ in range(4)]
                k.op('dve', lambda e: e.memset(up[:], 0.0), writes=R_up)
                hts = Rot([(sb('ht%d' % i, [128, 8, 512], BF16), Res()) for i in range(2)])
                ots = Rot([(sb('ot%d' % i, [128, 4, 512], BF16), Res()) for i in range(2)])
                for ti in range(NB512):
                    t0 = ti * 512
                    ht, rh = hts.next()
                    k.dma('sp', ht[:], hs[:, :, t0:t0 + 512].rearrange("k p t -> p k t"), reads=[R_hs[ti]], writes=[rh])
                    ot, ro = ots.next()
                    for fc in range(4):
                        gfc = hg * 4 + fc
                        cs = slice(fc * 128, (fc + 1) * 128)
                        pb, prb = psA.next(); pc, prc = psA.next(); pu, pru = psA.next()
                        k.mm(pb[:], [(wbg[:, kc, cs], ht[:, kc, :]) for kc in range(8)], reads=[rbg, rh], writes=[prb])
                        k.mm(pc[:], [(wcg[:, kc, cs], ht[:, kc, :]) for kc in range(8)], reads=[rcg, rh], writes=[prc])
                        k.mm(pu[:], [(wu[:, kc, cs], ht[:, kc, :]) for kc in range(8)], reads=[ru, rh], writes=[pru])
                        t1, r1 = tmp.next()
                        k.op('act', lambda e: e.activation(t1[:], pc[:], AF.Identity), reads=[prc], writes=[r1])
                        k.op('dve', lambda e: e.tensor_tensor(up[:, fc, 2:514], t1[:], pu[:], ALU.mult), reads=[r1, pru], writes=[R_up[fc]])
                        t2, r2 = tmp.next()
                        k.op('dve', lambda e: e.tensor_scalar(t2[:], up[:, fc, 2:514], cv[:, gfc, 2:3], None, ALU.mult), reads=[R_up[fc], R_cv], writes=[r2])
                        k.op('dve', lambda e: e.scalar_tensor_tensor(t2[:], up[:, fc, 1:513], cv[:, gfc, 1:2], t2[:], ALU.mult, ALU.add), reads=[R_up[fc], R_cv, r2], writes=[r2])
                        k.op('dve', lambda e: e.scalar_tensor_tensor(t2[:], up[:, fc, 0:512], cv[:, gfc, 0:1], t2[:], ALU.mult, ALU.add), reads=[R_up[fc], R_cv, r2], writes=[r2])
                        k.op('dve', lambda e: e.tensor_tensor(ot[:, fc, :], t2[:], pb[:], ALU.mult), reads=[r2, prb], writes=[ro])
                        k.op('act', lambda e: e.activation(up[:, fc, 0:2], up[:, fc, 512:514], AF.Identity), reads=[R_up[fc]], writes=[R_up[fc]])
                    k.dma('sp', os_[hg * 4:(hg + 1) * 4, :, t0:t0 + 512].rearrange("k p t -> p k t"), ot[:], reads=[ro], writes=[R_os[hg][ti]])
                k.barrier()

    def rope_stage():
        es, sb = stage_alloc()
        with es:
            tl = Rot([[(sb('posi%d' % i, [128, 512], I32), Res()), (sb('cs%d' % i, [128, 2, 512], F32), Res())] for i in range(2)])
            for ti in range(NB512):
                (pi_, rpi), (cs2, rcs) = tl.next()
                rope_tables([(pi_, rpi), (cs2[:, 0, :], rcs), (cs2[:, 1, :], rcs)], ti * 512)
                k.dma('sp', rp[:, ti, :, :], cs2[:], reads=[rcs], writes=[R_rp[ti]])
            k.barrier()

    def rope_tables(sb_tiles, t0):
        (pi_t, R_pi), (ct_t, R_c), (st_t, R_s) = sb_tiles
        k.dma('sp', pi_t[:], pos_in[0:1, t0:t0 + 512].partition_broadcast(128), writes=[R_pi])
        ang, ra = tmp.next()
        k.op('dve', lambda e: e.tensor_copy(ang[:], pi_t[:]), reads=[R_pi], writes=[ra])
        k.op('dve', lambda e: e.tensor_scalar(ang[:], ang[:], cf[:, 768:769], None, ALU.mult), reads=[ra, R_cf], writes=[ra])
        for which, (dst, rd) in enumerate([(st_t, R_s), (ct_t, R_c)]):
            a2, r2 = tmp.next()
            n_, rn = tmp.next()
            off = 0.0 if which == 0 else 0.5 * math.pi
            k.op('dve', lambda e: e.tensor_scalar(a2[:], ang[:], off, None, ALU.add), reads=[ra], writes=[r2])
            k.op('dve', lambda e: e.tensor_scalar(n_[:], a2[:], 1.0 / TWO_PI, MAGIC, ALU.mult, ALU.add), reads=[r2], writes=[rn])
            k.op('dve', lambda e: e.tensor_scalar(n_[:], n_[:], -MAGIC, None, ALU.add), reads=[rn], writes=[rn])
            k.op('dve', lambda e: e.scalar_tensor_tensor(a2[:], n_[:], -C1, a2[:], ALU.mult, ALU.add), reads=[rn, r2], writes=[r2])
            k.op('dve', lambda e: e.scalar_tensor_tensor(a2[:], n_[:], -C2, a2[:], ALU.mult, ALU.add), reads=[rn, r2], writes=[r2])
            k.op('dve', lambda e: e.tensor_scalar(a2[:], a2[:], -PI_LO, PI_LO, ALU.max, ALU.min), reads=[r2], writes=[r2])
            k.op('act', lambda e: e.activation(dst, a2[:], AF.Sin), reads=[r2], writes=[rd])
        k.op('dve', lambda e: e.tensor_scalar(st_t, st_t, cf[:, 769:770], None, ALU.mult), reads=[R_s, R_cf], writes=[R_s])

    def mix_attn(l):
        lambda_init = 0.8 - 0.6 * math.exp(-0.3 * l)
        scale = 64 ** -0.5
        NBLK = S // 128
        for hg in range(2):
            es, sb = stage_alloc()
            with es:
                (wq, rq), (wk, rk), (wv, rv) = load_w(sb, [b_win[i, hg] for i in range(3)])
                sm = sb('sm', [128, 8], F32); R_sm = Res()
                k.dma('sp', sm[:, 0:2], b_qkg[:, :], writes=[R_sm])
                k.dma('sp', sm[:, 2:3], b_sub[:, :], writes=[R_sm])
                k.op('dve', lambda e: e.tensor_scalar(sm[:, 2:3], sm[:, 2:3], 1.0 - lambda_init, None, ALU.mult), reads=[R_sm], writes=[R_sm])
                lm = sb('lm', [1, 256], F32); R_lm = Res()
                k.dma('sp', lm[:], b_lam[:, :], writes=[R_lm])
                l2 = sb('l2', [1, 8], F32)
                k.op('dve', lambda e: e.tensor_tensor(lm[:, 0:64], lm[:, 0:64], lm[:, 64:128], ALU.mult), reads=[R_lm], writes=[R_lm])
                k.op('dve', lambda e: e.tensor_tensor(lm[:, 128:192], lm[:, 128:192], lm[:, 192:256], ALU.mult), reads=[R_lm], writes=[R_lm])
                k.op('dve', lambda e: e.reduce_sum(l2[:, 0:1], lm[:, 0:64], mybir.AxisListType.X), reads=[R_lm], writes=[R_lm])
                k.op('dve', lambda e: e.reduce_sum(l2[:, 1:2], lm[:, 128:192], mybir.AxisListType.X), reads=[R_lm], writes=[R_lm])
                k.op('act', lambda e: e.activation(l2[:, 0:2], l2[:, 0:2], AF.Exp), reads=[R_lm], writes=[R_lm])
                k.op('dve', lambda e: e.tensor_tensor(l2[:, 2:3], l2[:, 1:2], l2[:, 0:1], ALU.subtract), reads=[R_lm], writes=[R_lm])
                k.op('dve', lambda e: e.tensor_scalar(l2[:, 2:3], l2[:, 2:3], -lambda_init, None, ALU.add), reads=[R_lm], writes=[R_lm])
                pt, pr = psC.next()
                k.mm(pt[:, 0:1], [(cf[0:1, 512:640], l2[0:1, 2:3])], reads=[R_cf, R_lm], writes=[pr])
                k.op('dve', lambda e: e.tensor_copy(sm[:, 3:4], pt[:, 0:1]), reads=[pr], writes=[R_sm])

                kT = sb('kT', [128, 4, S], BF16); R_kT = [Res() for _ in range(NB512)]
                vt = sb('vt', [128, NBLK, 512], BF16); R_vt = [Res() for _ in range(NB512)]
                qt = sb('qt', [128, 4, 2, 512], BF16); R_qt = [Res() for _ in range(4)]
                k.op('dve', lambda e: e.memset(qt[:], 0.0), writes=R_qt)
                hts = Rot([(sb('ht%d' % i, [128, 8, 512], BF16), Res()) for i in range(1)])
                ots = Rot([(sb('ot%d' % i, [128, 512], BF16), Res()) for i in range(2)])
                ebuf = Rot([(sb('e%d' % i, [128, 512], BF16), Res()) for i in range(3)])
                lacc = sb('lacc', [128, 2, 512], F32); R_la = [Res(), Res()]
                spool = Rot([(PS[i], RPS[i]) for i in range(4, 8)])
                sqb = Rot([(sb('sq%d' % i, [128, 512], BF16), Res()) for i in range(1)])
                cs2 = sb('cs2', [128, 2, 512], F32); R_c = Res(); R_s = R_c
                cT = cs2[:, 0, :]; sT = cs2[:, 1, :]
                for ti in range(NB512):
                    t0 = ti * 512
                    ht, rh = hts.next()
                    k.dma('sp', ht[:], hs[:, :, t0:t0 + 512].rearrange("k p t -> p k t"), reads=[R_hs[ti]], writes=[rh])
                    k.dma('sp', cs2[:], rp[:, ti, :, :], reads=[R_rp[ti]], writes=[R_c])
                    for blk in range(4):
                        pv, prv = psC.next()
                        k.mm(pv[:], [(ht[:, kc, blk * 128:(blk + 1) * 128], wv[:, kc, :]) for kc in range(8)], reads=[rh, rv], writes=[prv])
                        k.op('act', lambda e: e.activation(vt[:, ti * 4 + blk, :], pv[:], AF.Identity), reads=[prv], writes=[R_vt[ti]])
                    for h in range(4):
                        cs = slice(h * 128, (h + 1) * 128)
                        for which, (w_, rw_) in enumerate([(wq, rq), (wk, rk)]):
                            pp, prp = psC.next()
                            k.mm(pp[:], [(w_[:, kc, cs], ht[:, kc, :]) for kc in range(8)], reads=[rw_, rh], writes=[prp])
                            sq, rsq = sqb.next()
                            k.op('act', lambda e: e.activation(sq[:], pp[:], AF.Square), reads=[prp], writes=[rsq])
                            pss, prs = psC.next()
                            k.mm(pss[:], [(bones_bf, sq[:])], reads=[R_cb, rsq], writes=[prs])
                            rt, rr = rstd_from_ss(pss[:], prs, 64.0)
                            t1, r1 = tmp.next()
                            k.op('dve', lambda e: e.scalar_tensor_tensor(t1[:], pp[:], sm[:, which:which + 1], rt[:], ALU.mult, ALU.mult), reads=[prp, R_sm, rr], writes=[r1])
                            pq, prq = psC.next()
                            k.mm(pq[:], [(cf[:, 256:384], t1[:])], reads=[R_cf, r1], writes=[prq])
                            t2, r2 = tmp.next()
                            k.op('dve', lambda e: e.tensor_tensor(t2[:], pq[:], sT, ALU.mult), reads=[prq, R_s], writes=[r2])
                            k.op('dve', lambda e: e.tensor_tensor(t1[:], t1[:], cT, ALU.mult), reads=[r1, R_c], writes=[r1])
                            if which == 0:
                                k.op('dve', lambda e: e.tensor_tensor(qt[0:64, h, 0, :], t1[0:64, :], t2[0:64, :], ALU.add), reads=[r1, r2], writes=[R_qt[h]])
                                k.op('dve', lambda e: e.tensor_tensor(qt[64:128, h, 1, :], t1[64:128, :], t2[64:128, :], ALU.add), reads=[r1, r2], writes=[R_qt[h]])
                            else:
                                k.op('dve', lambda e: e.tensor_tensor(kT[:, h, t0:t0 + 512], t1[:], t2[:], ALU.add), reads=[r1, r2], writes=[R_kT[ti]])
                    for h in range(4):
                        ot, ro = ots.next()
                        acc = [psA.next() for _ in range(4)]
                        nkb = ti * 4 + 4
                        units = [(kb, c) for kb in range(nkb) for c in range(2)]
                        pend = {}

                        def emit_s(u):
                            kb, c = u
                            j = kb - ti * 4
                            n0 = max(0, j) * 128
                            ps_, prs_ = spool.next()
                            k.mm(ps_[:, n0:512], [(kT[:, h, kb * 128:(kb + 1) * 128], qt[:, h, c, n0:512])],
                                 reads=[R_kT[kb // 4], R_qt[h]], writes=[prs_])
                            eb, re_ = ebuf.next()
                            k.op('act', lambda e: e.activation(eb[:, n0:512], ps_[:, n0:512], AF.Exp, scale=scale), reads=[prs_], writes=[re_])
                            if j >= 0:
                                k.op('pool', lambda e: e.tensor_tensor(eb[:, n0:n0 + 128], eb[:, n0:n0 + 128], cb[:, 384:512], ALU.mult), reads=[re_, R_cb], writes=[re_])
                            if kb == 0:
                                k.op('dve', lambda e: e.tensor_copy(lacc[:, c, :], eb[:, :]), reads=[re_], writes=[R_la[c]])
                            else:
                                k.op('dve', lambda e: e.tensor_tensor(lacc[:, c, n0:512], lacc[:, c, n0:512], eb[:, n0:512], ALU.add), reads=[re_, R_la[c]], writes=[R_la[c]])
                            pend[u] = (eb, re_, n0)

                        def emit_pv(u):
                            kb, c = u
                            eb, re_, n0 = pend.pop(u)
                            (po, pro), (pl, prl) = acc[2 * c], acc[2 * c + 1]
                            k.mm(po[:, n0:512], [(vt[:, kb, h * 128:(h + 1) * 128], eb[:, n0:512])], reads=[R_vt[kb // 4], re_], writes=[pro],
                                 start=(kb == 0), stop=(kb == nkb - 1))

                        LOOK = 2
                        for i in range(len(units) + LOOK):
                            if i < len(units):
                                emit_s(units[i])
                            if i >= LOOK:
                                emit_pv(units[i - LOOK])
                        (po1, pro1), (pl1, prl1), (po2, pro2), (pl2, prl2) = acc
                        k.mm(pl1[:], [(cf[:, 512:640], lacc[:, 0, :])], reads=[R_cf, R_la[0]], writes=[prl1])
                        k.mm(pl2[:], [(cf[:, 512:640], lacc[:, 1, :])], reads=[R_cf, R_la[1]], writes=[prl2])
                        ra_, rra = tmp.next(); rb_, rrb = tmp.next()
                        k.op('dve', lambda e: e.reciprocal(ra_[:], pl1[:]), reads=[prl1], writes=[rra])
                        k.op('dve', lambda e: e.reciprocal(rb_[:], pl2[:]), reads=[prl2], writes=[rrb])
                        k.op('dve', lambda e: e.tensor_tensor(ra_[:], po1[:], ra_[:], ALU.mult), reads=[pro1, rra], writes=[rra])
                        k.op('dve', lambda e: e.scalar_tensor_tensor(rb_[:], rb_[:], sm[:, 3:4], po2[:], ALU.mult, ALU.mult), reads=[pro2, rrb, R_sm], writes=[rrb])
                        k.op('dve', lambda e: e.tensor_tensor(ra_[:], ra_[:], rb_[:], ALU.add), reads=[rra, rrb], writes=[rra])
                        sq, rsq = sqb.next()
                        k.op('act', lambda e: e.activation(sq[:], ra_[:], AF.Square), reads=[rra], writes=[rsq])
                        pss, prs = psC.next()
                        k.mm(pss[:], [(ones_bf, sq[:])], reads=[R_cb, rsq], writes=[prs])
                        rt, rr = rstd_from_ss(pss[:], prs, 128.0)
                        k.op('dve', lambda e: e.scalar_tensor_tensor(ot[:], ra_[:], sm[:, 2:3], rt[:], ALU.mult, ALU.mult), reads=[rra, R_sm, rr], writes=[ro])
                        k.dma('sp', os_[hg * 4 + h, :, t0:t0 + 512], ot[:], reads=[ro], writes=[R_os[hg][ti]])
                k.barrier()

    def mix_hgrn(l):
        idx = l // 3
        for hg in range(2):
            es, sb = stage_alloc()
            with es:
                (wq, rq), (wf, rf), (wi_, ri), (wg, rg) = load_w(sb, [a_win[idx, i, hg] for i in range(4)])
                lbf = sb('lbf', [128, 2, 8], F32); R_lbf = Res()
                k.dma('sp', lbf[:], a_lbf[:, :, :], writes=[R_lbf])
                lbr = sb('lbr', [128, 2, 512], F32); R_lbr = Res()
                for i in range(2):
                    k.dma('sp', lbr[:, i, :], a_lbr[i:i + 1, hg * 512:(hg + 1) * 512].partition_broadcast(128), writes=[R_lbr])
                k.op('act', lambda e: e.activation(lbf[:], lbf[:], AF.Exp), reads=[R_lbf], writes=[R_lbf])
                k.op('act', lambda e: e.activation(lbr[:], lbr[:], AF.Exp), reads=[R_lbr], writes=[R_lbr])
                lb_f = sb('lb_f', [128, 8], F32); oml_f = sb('oml_f', [128, 8], F32)
                lb_r = sb('lb_r', [128, 512], F32); oml_r = sb('oml_r', [128, 512], F32)
                for (src_, lb_, oml_, R_) in [(lbf, lb_f, oml_f, R_lbf), (lbr, lb_r, oml_r, R_lbr)]:
                    k.op('dve', lambda e: e.tensor_tensor(oml_[:], src_[:, 0, :], src_[:, 1, :], ALU.add), reads=[R_], writes=[R_])
                    k.op('dve', lambda e: e.reciprocal(oml_[:], oml_[:]), reads=[R_], writes=[R_])
                    if idx == 0:
                        k.op('dve', lambda e: e.tensor_tensor(lb_[:], src_[:, 0, :], src_[:, 0, :], ALU.subtract), reads=[R_], writes=[R_])
                    else:
                        k.op('dve', lambda e: e.tensor_copy(lb_[:], src_[:, 1, :]), reads=[R_], writes=[R_])
                    k.op('dve', lambda e: e.tensor_tensor(lb_[:], lb_[:], oml_[:], ALU.mult), reads=[R_], writes=[R_])
                    k.op('dve', lambda e: e.tensor_scalar(oml_[:], lb_[:], -1.0, 1.0, ALU.mult, ALU.add), reads=[R_], writes=[R_])
                ong = sb('ong', [128, 2], F32); R_on = Res()
                k.dma('sp', ong[:], a_on[:, :], writes=[R_on])
                St = sb('St', [128, 4, 128], F32); R_S = [Res() for _ in range(4)]
                Sb = sb('Sb', [128, 4, 128], BF16); R_Sb = [Res() for _ in range(4)]
                k.op('dve', lambda e: e.memset(St[:], 0.0), writes=R_S)
                k.op('dve', lambda e: e.memset(Sb[:], 0.0), writes=R_Sb)
                hts = Rot([(sb('ht%d' % i, [128, 8, 512], BF16), Res()) for i in range(2)])
                ots = Rot([(sb('ot%d' % i, [128, 4, 512], BF16), Res()) for i in range(2)])
                vtm_2 = [sb('vtm%d' % i_, [128, 4, 512], BF16) for i_ in range(2)]; R_v_2 = [[Res() for _ in range(4)] for i_ in range(2)]
                khat_2 = [sb('khat%d' % i_, [128, 4, 2, 512], BF16) for i_ in range(2)]; R_kh_2 = [[Res() for _ in range(4)] for i_ in range(2)]
                for i_ in range(2):
                    k.op('dve', lambda e: e.memset(khat_2[i_][:], 0.0), writes=R_kh_2[i_])
                lgf = sb('lgf', [128, 4, 512], F32); R_lg = [Res() for _ in range(4)]
                qf = sb('qf', [128, 4, 512], F32); R_qf = [Res() for _ in range(4)]
                sg_2 = [sb('sg%d' % i_, [128, 4, 512], F32) for i_ in range(2)]; R_sg_2 = [[Res() for _ in range(4)] for i_ in range(2)]
                e1_2 = [sb('e1%d' % i_, [128, 4, 512], F32) for i_ in range(2)]; R_e1_2 = [[Res() for _ in range(4)] for i_ in range(2)]
                qi_2 = [sb('qi%d' % i_, [128, 4, 512], BF16) for i_ in range(2)]; R_qi_2 = [[Res() for _ in range(4)] for i_ in range(2)]
                qtl_2 = [sb('qtl%d' % i_, [128, 4, 512], BF16) for i_ in range(2)]; R_qtl_2 = [[Res() for _ in range(4)] for i_ in range(2)]
                ktl_2 = [sb('ktl%d' % i_, [128, 4, 512], BF16) for i_ in range(2)]; R_ktl_2 = [[Res() for _ in range(4)] for i_ in range(2)]
                nr = sb('nr', [128, 4, 16], F32); R_nr = [Res() for _ in range(4)]
                of = sb('of', [128, 4, 512], F32); R_of = [Res() for _ in range(4)]
                atm = Rot([(sb('atm%d' % i, [128, 128], BF16), Res()) for i in range(3)])
                sqb = Rot([(sb('sq%d' % i, [128, 512], BF16), Res()) for i in range(2)])
                def front(ti):
                    vtm = vtm_2[ti % 2]; R_v = R_v_2[ti % 2]
                    khat = khat_2[ti % 2]; R_kh = R_kh_2[ti % 2]
                    sg = sg_2[ti % 2]; R_sg = R_sg_2[ti % 2]
                    e1 = e1_2[ti % 2]; R_e1 = R_e1_2[ti % 2]
                    qi = qi_2[ti % 2]; R_qi = R_qi_2[ti % 2]
                    qtl = qtl_2[ti % 2]; R_qtl = R_qtl_2[ti % 2]
                    ktl = ktl_2[ti % 2]; R_ktl = R_ktl_2[ti % 2]
                    t0 = ti * 512
                    ht, rh = hts.next()
                    k.dma('sp', ht[:], hs[:, :, t0:t0 + 512].rearrange("k p t -> p k t"), reads=[R_hs[ti]], writes=[rh])
                    for blk in range(4):
                        bs = slice(blk * 128, (blk + 1) * 128)
                        pv, prv = psC.next()
                        k.mm(pv[:], [(ht[:, kc, bs], wi_[:, kc, :]) for kc in range(8)], reads=[rh, ri], writes=[prv])
                        k.op('act', lambda e: e.activation(vtm[:, blk, :], pv[:], AF.Identity), reads=[prv], writes=[R_v[blk]])
                        pf, prf = psC.next()
                        k.mm(pf[:], [(ht[:, kc, bs], wf[:, kc, :]) for kc in range(8)], reads=[rh, rf], writes=[prf])
                        t1, r1 = tmp.next()
                        sig_to(t1[:], pf[:], prf, r1)
                        k.op('dve', lambda e: e.tensor_tensor(t1[:], t1[:], oml_r[:], ALU.mult), reads=[r1, R_lbr], writes=[r1])
                        k.op('dve', lambda e: e.tensor_tensor(t1[:], t1[:], lb_r[:], ALU.add), reads=[r1, R_lbr], writes=[r1])
                        k.op('act', lambda e: e.activation(lgf[:, blk, :], t1[:], AF.Ln), reads=[r1], writes=[R_lg[blk]])
                        k.op('dve', lambda e: e.tensor_scalar(t1[:], t1[:], -1.0, 1.0, ALU.mult, ALU.add), reads=[r1], writes=[r1])
                        pd, prd = psC.next()
                        k.mm(pd[:], [(cf[:, 128:256], lgf[:, blk, :])], reads=[R_cf, R_lg[blk]], writes=[prd])
                        t2, r2 = tmp.next()
                        k.op('act', lambda e: e.activation(t2[:], pd[:], AF.Exp), reads=[prd], writes=[r2])
                        k.op('dve', lambda e: e.tensor_tensor(khat[0:64, blk, 0, :], t1[0:64, :], t2[0:64, :], ALU.mult), reads=[r1, r2], writes=[R_kh[blk]])
                        k.op('dve', lambda e: e.tensor_tensor(khat[64:128, blk, 1, :], t1[64:128, :], t2[64:128, :], ALU.mult), reads=[r1, r2], writes=[R_kh[blk]])
                    for h in range(4):
                        cs = slice(h * 128, (h + 1) * 128)
                        gh = hg * 4 + h
                        pq, prq = psA.next()
                        k.mm(pq[:], [(wq[:, kc, cs], ht[:, kc, :]) for kc in range(8)], reads=[rq, rh], writes=[prq])
                        sig_to(qf[:, h, :], pq[:], prq, R_qf[h])
                        k.op('dve', lambda e: e.tensor_tensor(qf[:, h, :], qf[:, h, :], pq[:], ALU.mult), reads=[R_qf[h], prq], writes=[R_qf[h]])
                        pg, prg = psA.next()
                        k.mm(pg[:], [(wg[:, kc, cs], ht[:, kc, :]) for kc in range(8)], reads=[rg, rh], writes=[prg])
                        sig_to(sg[:, h, :], pg[:], prg, R_sg[h])
                        k.op('dve', lambda e: e.tensor_tensor(sg[:, h, :], sg[:, h, :], pg[:], ALU.mult), reads=[R_sg[h], prg], writes=[R_sg[h]])
                        pf, prf = psA.next()
                        k.mm(pf[:], [(wf[:, kc, cs], ht[:, kc, :]) for kc in range(8)], reads=[rf, rh], writes=[prf])
                        kf, rkf = tmp.next()
                        sig_to(kf[:], pf[:], prf, rkf)
                        k.op('dve', lambda e: e.tensor_scalar(kf[:], kf[:], oml_f[:, gh:gh + 1], lb_f[:, gh:gh + 1], ALU.mult, ALU.add), reads=[rkf, R_lbf], writes=[rkf])
                        k.op('dve', lambda e: e.tensor_scalar(kf[:], kf[:], -1.0, 1.0, ALU.mult, ALU.add), reads=[rkf], writes=[rkf])
                        pb, prb = psA.next()
                        for blk in range(4):
                            k.mm(pb[:, blk * 128:(blk + 1) * 128], [(lgf[:, blk, cs], cf[:, 0:128])], reads=[R_lg[blk], R_cf], writes=[prb])
                        k.op('act', lambda e: e.activation(e1[:, h, :], pb[:], AF.Exp), reads=[prb], writes=[R_e1[h]])
                        b3 = pb[:].rearrange("p (c t) -> p c t", t=64)
                        k.op('dve', lambda e: e.tensor_copy(nr[:, h, 8:16], b3[:, :, 31]), reads=[prb], writes=[R_nr[h]])
                        k.op('dve', lambda e: e.tensor_scalar(nr[:, h, 0:8], nr[:, h, 8:16], -1.0, None, ALU.mult), reads=[R_nr[h]], writes=[R_nr[h]])
                        eq, req = tmp.next(); ek, rek = tmp.next()
                        for c in range(8):
                            c_ = slice(c * 64, (c + 1) * 64)
                            k.op('act', lambda e: e.activation(eq[:, c_], pb[:, c_], AF.Exp, bias=nr[:, h, c:c + 1], scale=1.0), reads=[prb, R_nr[h]], writes=[req])
                            k.op('act', lambda e: e.activation(ek[:, c_], pb[:, c_], AF.Exp, bias=nr[:, h, 8 + c:9 + c], scale=-1.0), reads=[prb, R_nr[h]], writes=[rek])
                        k.op('dve', lambda e: e.tensor_tensor(qtl[:, h, :], qf[:, h, :], eq[:], ALU.mult), reads=[R_qf[h], req], writes=[R_qtl[h]])
                        k.op('dve', lambda e: e.tensor_tensor(ktl[:, h, :], kf[:], ek[:], ALU.mult), reads=[rkf, rek], writes=[R_ktl[h]])
                        k.op('dve', lambda e: e.tensor_tensor(qi[:, h, :], qf[:, h, :], e1[:, h, :], ALU.mult), reads=[R_qf[h], R_e1[h]], writes=[R_qi[h]])
                def back(ti):
                    vtm = vtm_2[ti % 2]; R_v = R_v_2[ti % 2]
                    khat = khat_2[ti % 2]; R_kh = R_kh_2[ti % 2]
                    sg = sg_2[ti % 2]; R_sg = R_sg_2[ti % 2]
                    e1 = e1_2[ti % 2]; R_e1 = R_e1_2[ti % 2]
                    qi = qi_2[ti % 2]; R_qi = R_qi_2[ti % 2]
                    qtl = qtl_2[ti % 2]; R_qtl = R_qtl_2[ti % 2]
                    ktl = ktl_2[ti % 2]; R_ktl = R_ktl_2[ti % 2]
                    t0 = ti * 512
                    for blk in range(4):
                        bs = slice(blk * 128, (blk + 1) * 128)
                        for h in range(4):
                            cs = slice(h * 128, (h + 1) * 128)
                            pa, pra = psC.next()
                            k.mm(pa[:, 0:128], [(ktl[:, h, bs], qtl[:, h, bs])], reads=[R_ktl[h], R_qtl[h]], writes=[pra])
                            am, ram = atm.next()
                            k.op('dve', lambda e: e.tensor_tensor(am[:], pa[:, 0:128], cf[:, 0:128], ALU.mult), reads=[pra, R_cf], writes=[ram])
                            po, pro = psB.next()
                            for cc in range(2):
                                c = blk * 2 + cc
                                c_ = slice(c * 64, (c + 1) * 64)
                                rows = slice(cc * 64, (cc + 1) * 64)
                                k.mm(po[:, cc * 64:(cc + 1) * 64], [(Sb[:, h, :], qi[:, h, c_])], reads=[R_Sb[h], R_qi[h]], writes=[pro],
                                     start=(cc == 0), stop=False)
                                psn, prsn = psA.next()
                                k.mm(psn[:, 0:128], [(khat[:, blk, cc, cs], vtm[:, blk, cs])], reads=[R_kh[blk], R_v[blk]], writes=[prsn])
                                k.op('dve', lambda e: e.scalar_tensor_tensor(St[:, h, :], St[:, h, :], e1[:, h, c * 64 + 63:c * 64 + 64], psn[:, 0:128], ALU.mult, ALU.add),
                                     reads=[R_S[h], R_e1[h], prsn], writes=[R_S[h]])
                                k.op('act', lambda e: e.activation(Sb[:, h, :], St[:, h, :], AF.Identity), reads=[R_S[h]], writes=[R_Sb[h]])
                            k.mm(po[:, 0:128], [(vtm[:, blk, cs], am[:])], reads=[R_v[blk], ram], writes=[pro], start=False, stop=True)
                            k.op('act', lambda e: e.activation(of[:, h, bs], po[:, 0:128], AF.Identity), reads=[pro], writes=[R_of[h]])
                    ot, ro = ots.next()
                    for h in range(4):
                        sq, rsq = sqb.next()
                        k.op('act', lambda e: e.activation(sq[:], of[:, h, :], AF.Square), reads=[R_of[h]], writes=[rsq])
                        pss, prs = psC.next()
                        k.mm(pss[:], [(ones_bf, sq[:])], reads=[R_cb, rsq], writes=[prs])
                        rt, rr = rstd_from_ss(pss[:], prs, 128.0)
                        t1, r1 = tmp.next()
                        k.op('dve', lambda e: e.tensor_tensor(t1[:], of[:, h, :], rt[:], ALU.mult), reads=[R_of[h], rr], writes=[r1])
                        k.op('dve', lambda e: e.scalar_tensor_tensor(ot[:, h, :], t1[:], ong[:, idx:idx + 1], sg[:, h, :], ALU.mult, ALU.mult), reads=[r1, R_on, R_sg[h]], writes=[ro])
                    k.dma('sp', os_[hg * 4:(hg + 1) * 4, :, t0:t0 + 512].rearrange("k p t -> p k t"), ot[:], reads=[ro], writes=[R_os[hg][ti]])
                front(0)
                for ti in range(NB512):
                    if ti + 1 < NB512:
                        front(ti + 1)
                    back(ti)
                k.barrier()

    MIX = {0: mix_hgrn, 1: mix_attn, 2: mix_conv}
    for st in stages:
        if st[0] == 'pro':
            prologue()
        elif st[0] == 'rope':
            rope_stage()
        elif st[0] == 'tok':
            _, src, l_out, ffns, pren, dst = st
            tok_stage(xT_in if src == 'in' else xs, l_out, ffns, pren, xs)
        elif st[0] == 'mix':
            MIX[st[1] % 3](st[1])
    k.finish('sp')
    stats = (k.n_instr, k.n_wait)
    k.close()
    return nc, stats


FULL_STAGES = [('rope',), ('pro',), ('tok', 'in', None, [(0, 0)], 0, 'xs')]
for _l in range(DEPTH):
    FULL_STAGES.append(('mix', _l))
    if _l < DEPTH - 1:
        FULL_STAGES.append(('tok', 'xs', _l, [(_l, 2), (_l + 1, 0)], _l + 1, 'xs'))
    else:
        FULL_STAGES.append(('tok', 'xs', _l, [(_l, 2)], None, 'out'))


def make_consts():
    c = np.zeros((128, 1024), np.float32)
    s = np.arange(128)[:, None]; t = np.arange(128)[None, :]
    same = (s // 64) == (t // 64)
    c[:, 0:128] = (same & (s <= t))
    c[:, 128:256] = (same & (s > t))
    P = np.zeros((128, 128), np.float32)
    for m in range(128):
        d = m % 64
        if d < 8:
            P[m, m + 8] = 1.0
        elif d < 16:
            P[m, m - 8] = 1.0
    c[:, 256:384] = P.T
    c[:, 384:512] = (s <= t)
    c[:, 512:640] = 1.0
    c[:, 640:768] = ((s // 64) == (t // 64))
    inv_freq = (500000.0 ** (-np.arange(0, 16, 2, dtype=np.float32) / 16)).astype(np.float32)
    for p in range(128):
        d = p % 64
        if d < 16:
            c[p, 768] = inv_freq[d % 8]
            c[p, 769] = -1.0 if d < 8 else 1.0
    return c


def prep_shared(inp):
    f32 = np.float32
    sh = {}
    sh['ada_w'] = np.ascontiguousarray(inp['ada_w'], f32)
    sh['ada_b_l'] = np.ascontiguousarray(inp['ada_b'].reshape(DEPTH, 72, 128).transpose(2, 0, 1), f32)
    sh['norm_g_l'] = np.ascontiguousarray(inp['norm_g'].reshape(DEPTH, 3, 8, 128).transpose(3, 0, 1, 2), f32)
    wi = inp['ffn_wi'].reshape(DEPTH, 2, 8, 128, 2, NF, 128)
    wi = wi.transpose(0, 1, 5, 3, 4, 2, 6).reshape(DEPTH, 2, NF, 128, 2048)
    wo = inp['ffn_wo'].reshape(DEPTH, 2, NF, 128, D)
    sh['ffw'] = np.ascontiguousarray(np.concatenate([wi, wo], axis=-1), f32)
    wouts = [inp['a_w_out'][0], inp['b_w_out'][0], inp['c_w_out'][0], inp['a_w_out'][1]]
    sh['wout_l'] = np.ascontiguousarray(np.stack([w.reshape(8, 128, D).transpose(1, 0, 2) for w in wouts]), f32)

    def inl(w, nsplit):
        w = w.reshape(8, 128, nsplit, 2, 512)
        return np.ascontiguousarray(w.transpose(2, 3, 1, 0, 4), f32)
    sh['a_win_l'] = np.stack([inl(inp['a_w_in'][i], 4) for i in range(2)])
    sh['b_win_l'] = inl(inp['b_w_in'][0], 3)
    sh['c_win_l'] = inl(inp['c_w_in'][0], 3)
    sh['a_lb_fm'] = np.ascontiguousarray(inp['a_lb'].reshape(2, 8, 128).transpose(2, 0, 1), f32)
    sh['a_lb_row'] = np.ascontiguousarray(inp['a_lb'], f32)
    sh['a_onorm_l'] = np.ascontiguousarray(inp['a_onorm'].T, f32)
    g = inp['b_qk_g'][0]
    sh['b_qkg_l'] = np.ascontiguousarray(np.concatenate([g, g], axis=1).T, f32)
    sh['b_lam'] = np.ascontiguousarray(inp['b_lam'][0].reshape(1, 256), f32)
    sh['b_subln_l'] = np.ascontiguousarray(inp['b_subln'][0].reshape(128, 1), f32)
    sh['c_conv_l'] = np.ascontiguousarray(inp['c_conv'][0].reshape(3, 8, 128).transpose(2, 1, 0), f32)
    sh['consts'] = make_consts()
    return sh


def prep_core(inp, b, S):
    m = {}
    m['xT'] = np.ascontiguousarray(inp['x'][b, :S].T.reshape(8, 128, S), np.float32)
    m['c_l'] = np.ascontiguousarray(inp['c'][b].reshape(8, 128).T, np.float32)
    m['pos'] = np.ascontiguousarray(inp['positions'][b, :S].reshape(1, S), np.int32)
    return m


_CACHE = {}


def kernel(**inputs):
    inp = {k_: np.asarray(v) for k_, v in inputs.items()}
    B, S, _ = inp['x'].shape
    if S not in _CACHE:
        _CACHE[S] = build_program(S, FULL_STAGES)[0]
    nc = _CACHE[S]
    sh = prep_shared(inp)
    in_maps = []
    for b in range(B):
        m = dict(sh)
        m.update(prep_core(inp, b, S))
        in_maps.append(m)
    res = run_bass_kernel_spmd(nc, in_maps, core_ids=list(range(B)))
    out = np.stack([res.results[b]['xT_out'].reshape(D, S).T for b in range(B)])
    return np.ascontiguousarray(out, np.float32)
```
